# Optimizing a Trainium2 kernel written in Bass

```python
import math
import jax, jax.numpy as jnp
from jax import lax
import numpy as np

D_MODEL = 1024
BATCH = 8
SEQ = 2048
DEPTH = 1
DEC_BATCH = 128
DEC_SEQ = 1
PAST_LEN = 16384
PAGE_SIZE = 128

RMS_EPS = 1e-6
SC_DIM = D_MODEL
SC_WIDTH = 3
SSM_D_INNER = 2 * D_MODEL
SSM_HEAD_DIM = 64
SSM_HEADS = SSM_D_INNER // SSM_HEAD_DIM
SSM_STATE = 128
SSM_GROUPS = 4
SSM_CONV = 4
SSM_CHUNK = 128
SSM_CONV_DIM = SSM_D_INNER + 2 * SSM_GROUPS * SSM_STATE
MEM_LEN = 256
ATTN_HEADS = 4
ATTN_HEAD_DIM = D_MODEL // ATTN_HEADS
ATTN_DIM = ATTN_HEADS * ATTN_HEAD_DIM
N_BRANCHES = 3
FFN_HIDDEN = ((8 * D_MODEL // 3 + 255) // 256) * 256
IN_SIZES = (SC_DIM, SC_DIM, SC_DIM, SSM_D_INNER, SSM_CONV_DIM, SSM_HEADS, ATTN_DIM, N_BRANCHES * D_MODEL)
IN_COLS = sum(IN_SIZES)

kernel_name = 'hybrid_gated_conv_ssd_memattn_step'


def rmsnorm(x, w, eps=RMS_EPS):
    xf = x.astype(jnp.float32)
    y = xf * lax.rsqrt(jnp.mean(xf * xf, axis=-1, keepdims=True) + eps)
    return (y * w.astype(jnp.float32)).astype(x.dtype)


def group_rmsnorm(x, w, groups, eps=RMS_EPS):
    xf = x.astype(jnp.float32)
    shp = xf.shape
    xg = xf.reshape(shp[:-1] + (groups, shp[-1] // groups))
    xg = xg * lax.rsqrt(jnp.mean(xg * xg, axis=-1, keepdims=True) + eps)
    return (xg.reshape(shp) * w.astype(jnp.float32)).astype(x.dtype)


def causal_depthwise_conv(u, hist, w):
    width = w.shape[0]
    seq = u.shape[1]
    full = jnp.concatenate([hist.astype(u.dtype), u], axis=1)
    out = full[:, 0:seq] * w[0]
    for k in range(1, width):
        out = out + full[:, k:k + seq] * w[k]
    return out, full[:, seq:]


def ssd_chunked(xs, dt, a, bm, cm, s0):
    bsz, seq, nh, hd = xs.shape
    ng, ns = bm.shape[2], bm.shape[3]
    hpg = nh // ng
    q = SSM_CHUNK
    nc = seq // q
    f32 = jnp.float32
    xdt = (xs.astype(f32) * dt[..., None]).reshape(bsz, nc, q, ng, hpg, hd)
    br = bm.astype(f32).reshape(bsz, nc, q, ng, ns)
    cr = cm.astype(f32).reshape(bsz, nc, q, ng, ns)
    acum = jnp.cumsum((dt * a).reshape(bsz, nc, q, ng, hpg), axis=2)
    causal = jnp.tril(jnp.ones((q, q), dtype=bool))[:, :, None, None]
    seg = acum[:, :, :, None] - acum[:, :, None, :]
    lmat = jnp.exp(jnp.where(causal, seg, -jnp.inf))
    cb = jnp.einsum('bcign,bcjgn->bcijg', cr, br)
    y_diag = jnp.einsum('bcijg,bcijgh,bcjghp->bcighp', cb, lmat, xdt)
    decay_states = jnp.exp(acum[:, :, -1:] - acum)
    chunk_states = jnp.einsum('bcjgn,bcjgh,bcjghp->bcghpn', br, decay_states, xdt)
    chunk_decay = jnp.exp(acum[:, :, -1])

    def carry_step(s, inp):
        st, dec = inp
        return s * dec[..., None, None] + st, s

    s_init = s0.astype(f32).reshape(bsz, ng, hpg, hd, ns)
    s_final, s_prev = lax.scan(carry_step, s_init,
                               (jnp.moveaxis(chunk_states, 1, 0), jnp.moveaxis(chunk_decay, 1, 0)))
    s_prev = jnp.moveaxis(s_prev, 0, 1)
    y_off = jnp.einsum('bcign,bcghpn,bcigh->bcighp', cr, s_prev, jnp.exp(acum))
    y = (y_diag + y_off).reshape(bsz, seq, nh, hd)
    return y.astype(xs.dtype), s_final.reshape(bsz, nh, hd, ns).astype(s0.dtype)


def ssd_recurrent(xs, dt, a, bm, cm, s0):
    hpg = xs.shape[2] // bm.shape[2]
    f32 = jnp.float32

    def step(s, inp):
        xt, dtt, bt, ct = inp
        bh = jnp.repeat(bt, hpg, axis=1)
        ch = jnp.repeat(ct, hpg, axis=1)
        s = s * jnp.exp(dtt * a)[..., None, None] + (dtt[..., None] * xt)[..., None] * bh[:, :, None, :]
        return s, jnp.einsum('bhpn,bhn->bhp', s, ch)

    seqs = (jnp.moveaxis(xs.astype(f32), 1, 0), jnp.moveaxis(dt.astype(f32), 1, 0),
            jnp.moveaxis(bm.astype(f32), 1, 0), jnp.moveaxis(cm.astype(f32), 1, 0))
    s_final, ys = lax.scan(step, s0.astype(f32), seqs)
    return jnp.moveaxis(ys, 0, 1).astype(xs.dtype), s_final.astype(s0.dtype)


def mixer_block(xn, mem_k, mem_v, sc_hist, ssm_hist, ssm_s0, ssd_fn,
                w_in, sc_conv_w, w_sc_out, ssm_conv_w, ssm_conv_b, ssm_dt_bias, ssm_a_log, ssm_d,
                ssm_norm_w, w_ssm_out, w_attn_o, w_merge_o):
    bsz, seq, _ = xn.shape
    splits = [int(v) for v in np.cumsum(IN_SIZES)[:-1]]
    sc_b, sc_c, sc_x, z, xbc, dt_raw, q, gates = jnp.split(xn @ w_in, splits, axis=-1)
    conv_u, sc_new = causal_depthwise_conv(sc_c * sc_x, sc_hist, sc_conv_w)
    y_a = (sc_b * conv_u) @ w_sc_out
    xbc_c, ssm_conv_new = causal_depthwise_conv(xbc, ssm_hist, ssm_conv_w)
    xbc_c = jax.nn.silu(xbc_c + ssm_conv_b)
    xs, bm, cm = jnp.split(xbc_c, [SSM_D_INNER, SSM_D_INNER + SSM_GROUPS * SSM_STATE], axis=-1)
    xs = xs.reshape(bsz, seq, SSM_HEADS, SSM_HEAD_DIM)
    bm = bm.reshape(bsz, seq, SSM_GROUPS, SSM_STATE)
    cm = cm.reshape(bsz, seq, SSM_GROUPS, SSM_STATE)
    dt = jax.nn.softplus(dt_raw.astype(jnp.float32) + ssm_dt_bias.astype(jnp.float32))
    a = -jnp.exp(ssm_a_log.astype(jnp.float32))
    y_ssd, ssm_new = ssd_fn(xs, dt, a, bm, cm, ssm_s0)
    y_ssd = (y_ssd + ssm_d[:, None] * xs).reshape(bsz, seq, SSM_D_INNER) * jax.nn.silu(z)
    y_b = group_rmsnorm(y_ssd, ssm_norm_w, SSM_GROUPS) @ w_ssm_out
    qh = q.reshape(bsz, seq, ATTN_HEADS, ATTN_HEAD_DIM)
    scores = jnp.einsum('blhd,bmhd->bhlm', qh, mem_k).astype(jnp.float32) * (ATTN_HEAD_DIM ** -0.5)
    probs = jax.nn.softmax(scores, axis=-1).astype(mem_v.dtype)
    y_c = jnp.einsum('bhlm,bmhd->blhd', probs, mem_v).reshape(bsz, seq, ATTN_DIM) @ w_attn_o
    g_a, g_b, g_c = jnp.split(jax.nn.sigmoid(gates), N_BRANCHES, axis=-1)
    merged = g_a * y_a + g_b * y_b + g_c * y_c
    return merged @ w_merge_o, (sc_new, ssm_conv_new, ssm_new)


def swiglu(xn, w_gate, w_up, w_down):
    return (jax.nn.silu(xn @ w_gate) * (xn @ w_up)) @ w_down


def setup_inputs(seed: int = 0) -> dict:
    key = jax.random.key(seed)
    ks = jax.random.split(key, 32)
    f32 = jnp.float32

    def nrm(k, shape, scale):
        return jax.random.normal(k, shape, f32) * scale

    dt0 = jnp.exp(jax.random.uniform(ks[13], (DEPTH, SSM_HEADS), f32) * (math.log(0.1) - math.log(0.001)) + math.log(0.001))
    return {
        'x_prompt': nrm(ks[0], (BATCH, SEQ, D_MODEL), 1.0),
        'x_sample': nrm(ks[1], (DEC_BATCH, DEC_SEQ, D_MODEL), 1.0),
        'mem_prompt': nrm(ks[2], (BATCH, MEM_LEN, D_MODEL), 1.0),
        'cache_mem_k': nrm(ks[3], (DEPTH, DEC_BATCH, MEM_LEN, ATTN_HEADS, ATTN_HEAD_DIM), 1.0),
        'cache_mem_v': nrm(ks[4], (DEPTH, DEC_BATCH, MEM_LEN, ATTN_HEADS, ATTN_HEAD_DIM), 1.0),
        'state_conv': nrm(ks[5], (DEPTH, DEC_BATCH, SC_WIDTH - 1, SC_DIM), 1.0),
        'state_ssm_conv': nrm(ks[6], (DEPTH, DEC_BATCH, SSM_CONV - 1, SSM_CONV_DIM), 1.0),
        'state_ssm': nrm(ks[7], (DEPTH, DEC_BATCH, SSM_HEADS, SSM_HEAD_DIM, SSM_STATE), 0.2),
        'norm_mix_w': 1.0 + nrm(ks[8], (DEPTH, D_MODEL), 0.02),
        'w_in': nrm(ks[9], (DEPTH, D_MODEL, IN_COLS), D_MODEL ** -0.5),
        'sc_conv_w': nrm(ks[10], (DEPTH, SC_WIDTH, SC_DIM), SC_WIDTH ** -0.5),
        'w_sc_out': nrm(ks[11], (DEPTH, SC_DIM, D_MODEL), SC_DIM ** -0.5),
        'ssm_conv_w': nrm(ks[12], (DEPTH, SSM_CONV, SSM_CONV_DIM), SSM_CONV ** -0.5),
        'ssm_conv_b': nrm(ks[14], (DEPTH, SSM_CONV_DIM), 0.02),
        'ssm_dt_bias': dt0 + jnp.log(-jnp.expm1(-dt0)),
        'ssm_a_log': jnp.log(jax.random.uniform(ks[15], (DEPTH, SSM_HEADS), f32, 1.0, 16.0)),
        'ssm_d': 1.0 + nrm(ks[16], (DEPTH, SSM_HEADS), 0.1),
        'ssm_norm_w': 1.0 + nrm(ks[17], (DEPTH, SSM_D_INNER), 0.02),
        'w_ssm_out': nrm(ks[18], (DEPTH, SSM_D_INNER, D_MODEL), SSM_D_INNER ** -0.5),
        'norm_mem_w': 1.0 + nrm(ks[19], (DEPTH, D_MODEL), 0.02),
        'w_mem_k': nrm(ks[20], (DEPTH, D_MODEL, ATTN_DIM), D_MODEL ** -0.5),
        'w_mem_v': nrm(ks[21], (DEPTH, D_MODEL, ATTN_DIM), D_MODEL ** -0.5),
        'w_attn_o': nrm(ks[22], (DEPTH, ATTN_DIM, D_MODEL), ATTN_DIM ** -0.5),
        'w_merge_o': nrm(ks[23], (DEPTH, D_MODEL, D_MODEL), D_MODEL ** -0.5),
        'norm_ffn_w': 1.0 + nrm(ks[24], (DEPTH, D_MODEL), 0.02),
        'w_ffn_gate': nrm(ks[25], (DEPTH, D_MODEL, FFN_HIDDEN), D_MODEL ** -0.5),
        'w_ffn_up': nrm(ks[26], (DEPTH, D_MODEL, FFN_HIDDEN), D_MODEL ** -0.5),
        'w_ffn_down': nrm(ks[27], (DEPTH, FFN_HIDDEN, D_MODEL), FFN_HIDDEN ** -0.5),
        'norm_final_w': 1.0 + nrm(ks[28], (D_MODEL,), 0.02),
    }


def reference(x_prompt, x_sample, mem_prompt, cache_mem_k, cache_mem_v, state_conv, state_ssm_conv, state_ssm,
              norm_mix_w, w_in, sc_conv_w, w_sc_out, ssm_conv_w, ssm_conv_b, ssm_dt_bias, ssm_a_log, ssm_d,
              ssm_norm_w, w_ssm_out, norm_mem_w, w_mem_k, w_mem_v, w_attn_o, w_merge_o,
              norm_ffn_w, w_ffn_gate, w_ffn_up, w_ffn_down, norm_final_w):
    xp = x_prompt
    xs = x_sample
    bp = xp.shape[0]
    p_mk, p_mv, p_conv, p_ssmc, p_ssm = [], [], [], [], []
    s_conv, s_ssmc, s_ssm = [], [], []
    for l in range(DEPTH):
        layer_w = (w_in[l], sc_conv_w[l], w_sc_out[l], ssm_conv_w[l], ssm_conv_b[l], ssm_dt_bias[l], ssm_a_log[l],
                   ssm_d[l], ssm_norm_w[l], w_ssm_out[l], w_attn_o[l], w_merge_o[l])
        mem_n = rmsnorm(mem_prompt, norm_mem_w[l])
        mk = (mem_n @ w_mem_k[l]).reshape(bp, MEM_LEN, ATTN_HEADS, ATTN_HEAD_DIM)
        mv = (mem_n @ w_mem_v[l]).reshape(bp, MEM_LEN, ATTN_HEADS, ATTN_HEAD_DIM)
        h, (sc_p, ssmc_p, ssm_p) = mixer_block(
            rmsnorm(xp, norm_mix_w[l]), mk, mv,
            jnp.zeros((bp, SC_WIDTH - 1, SC_DIM), xp.dtype),
            jnp.zeros((bp, SSM_CONV - 1, SSM_CONV_DIM), xp.dtype),
            jnp.zeros((bp, SSM_HEADS, SSM_HEAD_DIM, SSM_STATE), xp.dtype),
            ssd_chunked, *layer_w)
        xp = xp + h
        xp = xp + swiglu(rmsnorm(xp, norm_ffn_w[l]), w_ffn_gate[l], w_ffn_up[l], w_ffn_down[l])
        p_mk.append(mk)
        p_mv.append(mv)
        p_conv.append(sc_p)
        p_ssmc.append(ssmc_p)
        p_ssm.append(ssm_p)
        h, (sc_s, ssmc_s, ssm_s) = mixer_block(
            rmsnorm(xs, norm_mix_w[l]), cache_mem_k[l], cache_mem_v[l],
            state_conv[l], state_ssm_conv[l], state_ssm[l],
            ssd_recurrent, *layer_w)
        xs = xs + h
        xs = xs + swiglu(rmsnorm(xs, norm_ffn_w[l]), w_ffn_gate[l], w_ffn_up[l], w_ffn_down[l])
        s_conv.append(sc_s)
        s_ssmc.append(ssmc_s)
        s_ssm.append(ssm_s)
    y_prompt = rmsnorm(xp, norm_final_w)
    y_sample = rmsnorm(xs, norm_final_w)
    return (y_prompt, y_sample, jnp.stack(p_mk), jnp.stack(p_mv), jnp.stack(p_conv), jnp.stack(p_ssmc),
            jnp.stack(p_ssm), jnp.stack(s_conv), jnp.stack(s_ssmc), jnp.stack(s_ssm))
```

```python
import contextlib
import numpy as np
import concourse.bass as bass
import concourse.mybir as mybir
from concourse.bass_utils import run_bass_kernel_spmd

F32 = mybir.dt.float32
BF16 = mybir.dt.bfloat16
AF = mybir.ActivationFunctionType
ALU = mybir.AluOpType
AX = mybir.AxisListType.X

D = 1024
SEQ = 2048
T = 512
NT = SEQ // T
NSB = 16
FF = 2816
EPS = 1e-6
C_SCB, C_SCC, C_SCX, C_Z, C_XBC, C_DT, C_Q, C_G = 0, 1024, 2048, 3072, 5120, 8192, 8224, 9248
INC = 12320
ENGS = ["pe", "act", "dve", "pool", "sp"]
WINDOW = 6


class Op:
    __slots__ = ("idx", "eng", "fn", "deps", "signal", "sem", "val", "pos", "slot", "is_dma")


class Sched:
    def __init__(self, nc):
        self.nc = nc
        self.ops = []
        self.by_eng = {e: [] for e in ENGS}
        self.lastw = {}
        self.readers = {}
        self.slot_last = {}
        self.slot_count = {}
        self.auto_reads = []

    def _chan(self, o):
        return ("dma", o.slot) if o.is_dma else o.eng

    def add(self, eng, fn, reads=(), writes=(), slot=None):
        op = Op()
        op.idx = len(self.ops)
        op.eng = eng
        op.fn = fn
        op.slot = slot
        op.is_dma = slot is not None
        op.signal = False
        op.sem = None
        op.val = 0
        op.pos = len(self.by_eng[eng])
        deps = {}
        if self.auto_reads:
            reads = list(reads) + self.auto_reads
        extra = [("psx", k[1]) for k in reads if isinstance(k, tuple) and k[0] == "ps"]
        if extra:
            writes = list(writes) + extra

        def dep(o):
            if o is None:
                return
            ch = self._chan(o)
            if ch not in deps or deps[ch].idx < o.idx:
                deps[ch] = o

        for k in reads:
            dep(self.lastw.get(k))
        for k in writes:
            dep(self.lastw.get(k))
            for o in self.readers.get(k, {}).values():
                dep(o)
        if op.is_dma:
            dep(self.slot_last.get(slot))
            self.slot_last[slot] = op
            self.slot_count[slot] = self.slot_count.get(slot, 0) + 1
            op.val = 16 * self.slot_count[slot]
        final = []
        for ch, o in deps.items():
            if (not o.is_dma) and o.eng == eng and not op.is_dma:
                if eng == "pe":
                    continue
                if op.pos - o.pos > WINDOW:
                    continue
            o.signal = True
            final.append(o)
        op.deps = final
        ch = self._chan(op)
        for k in reads:
            self.readers.setdefault(k, {})[ch] = op
        for k in writes:
            self.lastw[k] = op
            self.readers[k] = {}
        self.ops.append(op)
        self.by_eng[eng].append(op)
        return op

    def emit(self, stack):
        nc = self.nc
        EPOCH = 2000
        ssem = {}
        for s in self.slot_count:
            ssem[s] = stack.enter_context(nc.semaphore("d_%d" % len(ssem)))
        for e in ENGS:
            c = 0
            cur = None
            for op in self.by_eng[e]:
                if op.is_dma:
                    op.sem = ssem[op.slot]
                elif op.signal:
                    if c % EPOCH == 0:
                        cur = stack.enter_context(nc.semaphore("c_%s_%d" % (e, c // EPOCH)))
                    op.sem = cur
                    op.val = c % EPOCH + 1
                    c += 1
        block = stack.enter_context(nc.Block())

        def run(e, eng):
            water = {}
            for op in self.by_eng[e]:
                need = {}
                for o in op.deps:
                    k = o.sem.num
                    if water.get(k, 0) >= o.val:
                        continue
                    if k not in need or need[k][1] < o.val:
                        need[k] = (o.sem, o.val)
                for k, (s, v) in need.items():
                    eng.wait_ge(s, v)
                    water[k] = v
                ins = op.fn(eng)
                if op.is_dma:
                    ins.then_inc(op.sem, 16)
                elif op.signal:
                    ins.then_inc(op.sem, 1)
            if e == "sp":
                for s, cnt in self.slot_count.items():
                    eng.wait_ge(ssem[s], 16 * cnt)

        @block.tensor
        def _(eng):
            run("pe", eng)

        @block.scalar
        def _(eng):
            run("act", eng)

        @block.vector
        def _(eng):
            run("dve", eng)

        @block.gpsimd
        def _(eng):
            run("pool", eng)

        @block.sync
        def _(eng):
            run("sp", eng)


def fap(ap, dims, off=0):
    return bass.AP(tensor=ap.tensor, offset=ap.offset + off, ap=[list(ap.ap[0])] + [list(d) for d in dims])


def build(with_sample=True, dbg=None, ntiles=NT, branches='ACB', stop=0):
    nc = bass.Bass("TRN2", target_bir_lowering=False)
    dram = {}

    def din(name, shape):
        dram[name] = nc.dram_tensor(name, list(shape), F32, kind="ExternalInput").ap()
        return dram[name]

    def dout(name, shape):
        dram[name] = nc.dram_tensor(name, list(shape), F32, kind="ExternalOutput").ap()
        return dram[name]

    xin = din("x_prompt", [SEQ, D])
    memin = din("mem_prompt", [256, D])
    w_in = din("w_in", [D, INC])
    w_sc_out = din("w_sc_out", [D, D])
    w_ssm_out = din("w_ssm_out", [2048, D])
    w_mem_k = din("w_mem_k", [D, D])
    w_mem_v = din("w_mem_v", [D, D])
    w_attn_o = din("w_attn_o", [D, D])
    w_merge_o = din("w_merge_o", [D, D])
    w_gate = din("w_ffn_gate", [D, FF])
    w_up = din("w_ffn_up", [D, FF])
    w_down = din("w_ffn_down", [FF, D])
    nrm_bc = din("nrm_bc", [128, 4, D])
    scw_c = din("scw_c", [128, 8, 3])
    cw_c = din("cw_c", [128, 24, 4])
    cb_c = din("cb_c", [128, 24])
    nw_c = din("nw_c", [128, 16])
    hp_bc = din("hp_bc", [128, 3, 32])
    cst = din("cst", [128, 5, 128])

    xs_in = din("x_sample", [NSB, D])
    ck_in = din("cache_k", [NSB, 256, D])
    cv_in = din("cache_v", [NSB, 256, D])
    st_conv = din("st_conv", [NSB, 2, D])
    st_sconv = din("st_sconv", [NSB, 3, 3072])
    st_ssm = din("st_ssm", [NSB, 2048, 128])
    Dc_in = din("Dc_in", [128, 16])
    y_sample = dout("y_sample", [NSB, D])
    sconv = dout("sconv", [NSB, 2, D])
    sssmc = dout("sssmc", [NSB, 3, 3072])
    sssm = dout("sssm", [NSB, 2048, 128])
    y_prompt = dout("y_prompt", [SEQ, D])
    pmk = dout("pmk", [256, D])
    pmv = dout("pmv", [256, D])
    pconv = dout("pconv", [128, 8, 2])
    pssmc = dout("pssmc", [128, 24, 3])
    pssm = dout("pssm", [2048, 128])
    if dbg:
        dbg_out = {k: dout("dbg_" + k, shp) for k, shp in dbg.items()}

    st = contextlib.ExitStack()
    E = st.enter_context

    def sb(name, shape, dt=F32):
        return E(nc.sbuf_tensor(name, list(shape), dt))

    S = Sched(nc)
    stop_mode = stop
    X = sb("X", [128, 4, D])
    MG = sb("MG", [128, 8, T])
    G1 = sb("G1", [128, 8, T], BF16)
    G2 = sb("G2", [128, 8, T], BF16)
    G3 = sb("G3", [128, 8, T], BF16)
    HB = sb("HB", [128, 16384], BF16)
    NPAGE = 16
    PAGE = 1024
    WRb = sb("WRb", [128, NPAGE * PAGE], BF16)
    WR = [WRb]
    page_owner = {}
    nrm = sb("nrm", [128, D])
    scw = sb("scw", [128, 8, 3])
    cw = sb("cw", [128, 24, 4])
    cb = sb("cb", [128, 24])
    nw = sb("nw", [128, 16])
    hp = sb("hp", [128, 3, 32])
    cstf = sb("cstf", [128, 5, 128])
    cstb = sb("cstb", [128, 5, 128], BF16)
    abc = sb("abc", [128, 32])
    xnb = [sb("xnb0", [128, D], BF16)] * 2
    junk = sb("junk", [128, D], BF16)
    stat = sb("stat", [128, 64])
    histA = sb("histA", [128, 8, 2], BF16)
    histB = sb("histB", [128, 24, 3], BF16)
    pcv = sb("pcv", [128, 8, 2])
    pscv = sb("pscv", [128, 24, 3])
    ub = [sb("ub%d" % i, [128, 2 + T], BF16) for i in range(2)]
    xb = [sb("xb%d" % i, [128, 3 + T], BF16) for i in range(2)]
    dg = [sb("dg%d" % i, [128, 4, 128], BF16) for i in range(2)]
    sA = [sb("sA%d" % i, [128, T]) for i in range(2)]
    sB = [sb("sB%d" % i, [128, T]) for i in range(2)]
    sC = [sb("sC%d" % i, [128, T], BF16) for i in range(2)]
    KT = sb("KT", [128, 8, 256], BF16)
    Vb = sb("Vb", [128, 2, D], BF16)
    ostg = [sb("ostg%d" % i, [128, D]) for i in range(2)]
    BT = sb("BT", [128, 4, T], BF16)
    CT = sb("CT", [128, 4, T], BF16)
    Btok = sb("Btok", [128, 4, 4, 128], BF16)
    Sf = sb("Sf", [128, 2048])
    Sbf = sb("Sbf", [128, 2048], BF16)
    dtt = sb("dtt", [128, 4, 32])
    lndt = sb("lndt", [128, 4, 32])
    dtab = sb("dtab", [128, 4, 32], BF16)
    acs = sb("acs", [128, 64])
    eac = sb("eac", [128, 64])
    dsd = sb("dsd", [128, 32])
    nbias = sb("nbias", [128, 32])
    nbh4 = sb("nbh4", [128, 4, 32], BF16)
    nbl4 = sb("nbl4", [128, 4, 32], BF16)
    xdd = sb("xdd", [128, 2048], BF16)
    xddg = [xdd[:, 0:512], xdd[:, 512:1024]]
    CBs = sb("CBs", [128, 4, 128], BF16)
    Lt = [sb("Lt%d" % i, [128, 4, 128], BF16) for i in range(2)]
    MT = [sb("MT%d" % i, [128, 4, 128], BF16) for i in range(2)]
    Lt.append(xdd[:, 1024:1536].rearrange("p (a b) -> p a b", a=4))
    MT.append(xdd[:, 1536:2048].rearrange("p (a b) -> p a b", a=4))
    yb = [sb("yb%d" % i, [128, 512]) for i in range(2)]
    ynb = [sb("ynb%d" % i, [128, 512], BF16) for i in range(2)]
    sm = sb("sm", [128, 16])
    PS = [E(nc.psum_tensor("ps%d" % i, [128, 512], F32)) for i in range(8)]

    ident_b = cstb[:, 0, :]
    tri_b = cstb[:, 1, :]
    mneg_b = cstb[:, 2, :]
    ones_b = cstb[:, 3, :]
    ident_f = cstf[:, 0, :]

    PT = HB[:, 0:4096].rearrange("p (h m t) -> p h m t", h=4, m=2)
    zs_v = HB[:, 0:8192].rearrange("p (j c) -> p j c", j=4)
    xs_v = HB[:, 8192:16384].rearrange("p (j c) -> p j c", j=4)
    mT_v = HB[:, 0:4096].rearrange("p (k t) -> p k t", k=8)
    hT_v = HB[:, 4096:4096 + 22 * T].rearrange("p (k t) -> p k t", k=22)

    cnt = {"ps": 0, "w": 0, "misc": 0, "rr": 0}

    def nb():
        i = cnt["ps"] % 8
        cnt["ps"] += 1
        return i

    def mslot():
        cnt["misc"] += 1
        return ("m", cnt["misc"] % 12)

    def ev_eng():
        cnt["rr"] += 1
        return "act" if cnt["rr"] % 2 else "dve"

    def dma_in(dst, src, wkey):
        S.add("sp", lambda g: g.dma_start(out=dst, in_=src), writes=[wkey], slot=mslot())

    def dma_out(dst, src, rkeys, slot=None):
        S.add("sp", lambda g: g.dma_start(out=dst, in_=src), reads=rkeys, slot=slot or mslot())

    def wload(Wap, r0, kch, c0, ncols):
        n = kch * ncols
        npg = (n + PAGE - 1) // PAGE
        start = cnt["w"]
        if start + npg > NPAGE:
            start = 0
        cnt["w"] = (start + npg) % NPAGE
        cnt["wser"] = cnt.get("wser", 0) + 1
        ser = cnt["wser"]
        prev = set()
        for p in range(start, start + npg):
            if p in page_owner:
                prev.add(page_owner[p])
            page_owner[p] = ser
        dst = WRb[:, start * PAGE:start * PAGE + n].rearrange("p (k c) -> p k c", k=kch)
        src = Wap[r0:r0 + kch * 128, c0:c0 + ncols].rearrange("(k p) c -> p k c", p=128)
        S.add("pool", lambda g: g.dma_start(out=dst, in_=src), writes=[("wl", ser)] + [("wl", q) for q in prev], slot=("w", start))
        return dst, ("wl", ser)

    def mm(out, lhsT, rhs, start, stop, reads, pskey):
        if stop_mode == 5:
            return
        S.add("pe", lambda g: g.matmul(out, lhsT, rhs, start=start, stop=stop), reads=reads, writes=[pskey])

    def tr(out, in_, idn, reads, pskey):
        S.add("pe", lambda g: g.transpose(out, in_, idn), reads=reads, writes=[pskey])

    def act(out, in_, func, reads, writes, bias=None, scale=1.0, accum=None):
        kw = {}
        if bias is not None:
            kw["bias"] = bias
        if accum is not None:
            kw["accum_out"] = accum
        S.add("act", lambda g: g.activation(out=out, in_=in_, func=func, scale=scale, **kw), reads=reads, writes=writes)

    def tt(eng, out, in0, in1, op, reads, writes):
        S.add(eng, lambda g: g.tensor_tensor(out=out, in0=in0, in1=in1, op=op), reads=reads, writes=writes)

    def ts(eng, out, in0, s1, s2, op0, op1, reads, writes):
        if op1 is None:
            S.add(eng, lambda g: g.tensor_scalar(out=out, in0=in0, scalar1=s1, scalar2=None, op0=op0), reads=reads, writes=writes)
        else:
            S.add(eng, lambda g: g.tensor_scalar(out=out, in0=in0, scalar1=s1, scalar2=s2, op0=op0, op1=op1), reads=reads, writes=writes)

    def cp(eng, out, in_, reads, writes):
        if eng == "act":
            act(out, in_, AF.Copy, reads, writes)
        else:
            S.add(eng, lambda g: g.tensor_copy(out, in_), reads=reads, writes=writes)

    def stt(out, in0, scalar, in1, op0, op1, reads, writes):
        S.add("dve", lambda g: g.scalar_tensor_tensor(out=out, in0=in0, scalar=scalar, in1=in1, op0=op0, op1=op1), reads=reads, writes=writes)

    def memset(eng, ap, val, wkeys):
        S.add(eng, lambda g: g.memset(ap, val), writes=wkeys)

    def rstd_from_ssq(col_ssq, col_out, n, P, key):
        a = stat[0:P, col_ssq:col_ssq + 1]
        o = stat[0:P, col_out:col_out + 1]
        ts("dve", o, a, 1.0 / n, EPS, ALU.mult, ALU.add, [key], [key])
        act(o, o, AF.Ln, [key], [key])
        act(o, o, AF.Exp, [key], [key], scale=-0.5)

    dma_in(scw[:], scw_c[:, :, :], "scw")
    dma_in(cw[:], cw_c[:, :, :], "cw")
    dma_in(cb[:], cb_c[:, :], "cb")
    dma_in(nw[:], nw_c[:, :], "nw")
    dma_in(hp[:], hp_bc[:, :, :], "hp")
    dma_in(cstf[:], cst[:, :, :], "cstf")
    cp("dve", cstb[:], cstf[:], ["cstf"], ["cstb"])
    act(abc[:], hp[:, 1, :], AF.Exp, ["hp"], ["abc"])
    ts("dve", abc[:], abc[:], -1.0, None, ALU.mult, None, ["abc"], ["abc"])
    memset("dve", histA[:], 0.0, ["histA"])
    memset("dve", histB[:], 0.0, ["histB"])
    memset("dve", Sf[:], 0.0, ["Sf"])
    memset("dve", Sbf[:], 0.0, ["Sbf"])

    def finish():
        S.emit(st)
        st.close()
        return nc
    if stop == 1:
        return finish()
    def norm_transpose(src3, P, nsub, wrow, dstT, tag, Tn):
        dma_in(nrm[:], nrm_bc[:, wrow, :], "nrm")
        xn2 = [(xnb[0], ("xnb", 0)), (ostg[1][:].bitcast(BF16)[:, 0:D], ("ostg", 1))]
        for j in range(nsub):
            act(junk[0:P, :], src3[0:P, j, :], AF.Square, [(tag, j)], ["junk", ("nst", j)], accum=stat[0:P, 48 + j:49 + j])
        for j in range(nsub):
            ts("dve", stat[0:P, 52 + j:53 + j], stat[0:P, 48 + j:49 + j], 1.0 / D, EPS, ALU.mult, ALU.add, [("nst", j)], [("nst", j)])
        for j in range(nsub):
            act(stat[0:P, 52 + j:53 + j], stat[0:P, 52 + j:53 + j], AF.Ln, [("nst", j)], [("nst", j)])
        for j in range(nsub):
            act(stat[0:P, 52 + j:53 + j], stat[0:P, 52 + j:53 + j], AF.Exp, [("nst", j)], [("nst", j)], scale=-0.5)
        for j in range(nsub):
            xn, xkey = xn2[j % 2]
            stt(xn[0:P, :], src3[0:P, j, :], stat[0:P, 52 + j:53 + j], nrm[0:P, :], ALU.mult, ALU.mult,
                [(tag, j), ("nst", j), "nrm"], [xkey])
            b = nb()
            pT = PS[b][:].bitcast(BF16)
            for k in range(8):
                tr(pT[:, k * P:(k + 1) * P] if P == 128 else pT[:, k * 128:k * 128 + P], xn[0:P, k * 128:(k + 1) * 128], ident_b[0:P, 0:P],
                   [xkey, "cstb"], ("ps", b))
            if P == 128:
                cp(ev_eng(), dstT[:, :, j * 128:(j + 1) * 128], pT[:, 0:1024].rearrange("p (k t) -> p k t", k=8),
                   [("ps", b)], [("G1", "all")])
            else:
                cp(ev_eng(), dstT[:, :, 0:P], fap(pT[:, 0:1], [[128, 8], [1, P]]), [("ps", b)], [("G1", "all")])

    def proj_fm(Wap, r0, kch, c0, nblk, srcT, Tn, src_keys, consume, blk_per_slab=4):
        blk = 0
        while blk < nblk:
            nbk = min(blk_per_slab, nblk - blk)
            slab, wk = wload(Wap, r0, kch, c0 + blk * 128, nbk * 128)
            for bb in range(nbk):
                b = nb()
                for k in range(kch):
                    mm(PS[b][:, 0:Tn], slab[:, k, bb * 128:(bb + 1) * 128], srcT[:, k, 0:Tn], k == 0, k == kch - 1,
                       [wk] + src_keys, ("ps", b))
                consume(blk + bb, b)
            blk += nbk

    MEMX = X
    for j in range(2):
        dma_in(X[:, j, :], memin[j * 128:(j + 1) * 128, :], ("X", j))
    norm_transpose(X, 128, 2, 3, G1, "X", 256)
    if stop == 2:
        return finish()
    if stop == 6:
        cp("act", ostg[0][:, 0:512], PS[5][:, :], [("ps", 5)], [("ostg", 0)])
        cp("dve", Vb[:, 0, 0:512], PS[5][:, :], [("ps", 5)], ["Vb"])
        return finish()
    G1k = [("G1", "all")]
    import os
    PARTS = os.environ.get("KV_PARTS", "WMCOK")
    KVW = os.environ.get("KV_WHICH", "kv")
    KVN = int(os.environ.get("KV_NCBK", "2"))
    KVM = int(os.environ.get("KV_NMC", "2"))
    for which, Wm, outd in (("k", w_mem_k, pmk), ("v", w_mem_v, pmv)):
        if which not in KVW:
            continue
        for cbk in range(KVN):
            if "W" in PARTS:
                slab, wk = wload(Wm, 0, 8, cbk * 512, 512)
            else:
                slab, wk = WRb[:, 0:4096].rearrange("p (k c) -> p k c", k=8), ("wl", 0)
            for mc in range(KVM):
                b = nb()
                if "M" in PARTS:
                    for k in range(8):
                        mm(PS[b][:, :], G1[:, k, mc * 128:(mc + 1) * 128], slab[:, k, :], k == 0, k == 7, [wk] + G1k, ("ps", b))
                so = ostg[cnt["rr"] % 2]
                sk = ("ostg", cnt["rr"] % 2)
                cnt["rr"] += 1
                if "C" in PARTS:
                    if os.environ.get("KV_VB", "dve") != "dve_only":
                        cp("act", so[:, 0:512], PS[b][:, :], [("ps", b)], [sk])
                    if which == "v":
                        vbm = os.environ.get("KV_VB", "dve")
                        if vbm in ("dve", "dve_only"):
                            cp("dve", Vb[:, mc, cbk * 512:(cbk + 1) * 512], PS[b][:, :], [("ps", b)], ["Vb"])
                        elif vbm == "act":
                            cp("act", Vb[:, mc, cbk * 512:(cbk + 1) * 512], PS[b][:, :], [("ps", b)], ["Vb"])
                        elif vbm == "dve_sb":
                            cp("dve", Vb[:, mc, cbk * 512:(cbk + 1) * 512], so[:, 0:512], [sk], ["Vb"])
                        elif vbm == "dve_half":
                            cp("dve", Vb[:, mc, cbk * 512:cbk * 512 + 256], PS[b][:, 0:256], [("ps", b)], ["Vb"])
                if "O" in PARTS:
                    dma_out(outd[mc * 128:(mc + 1) * 128, cbk * 512:(cbk + 1) * 512], so[:, 0:512], [sk], slot=sk)
            if which == "k" and "K" in PARTS:
                for bb in range(4):
                    b = nb()
                    if "M" in PARTS:
                        for k in range(8):
                            mm(PS[b][:, 0:256], slab[:, k, bb * 128:(bb + 1) * 128], G1[:, k, 0:256], k == 0, k == 7, [wk] + G1k, ("ps", b))
                    cp(ev_eng(), KT[:, cbk * 4 + bb, :], PS[b][:, 0:256], [("ps", b)], ["KT"])
    if stop == 7:
        return finish()

    def tile(ti, last):
        Tn = T
        P = 128
        NS = 4
        t0 = ti * T
        xk = [("G1", "all")]
        for j in range(NS):
            dma_in(X[:, j, :], xin[t0 + j * 128:t0 + (j + 1) * 128, :], ("X", j))
        norm_transpose(X, 128, NS, 0, G1, "X", Tn)

        def gated_merge(bname, mode):
            pass

        def out_proj_and_merge(Wo, kch, srcT, src_keys, gate_c0, mode, bps):
            ybank = {}

            def cons_y(blk, b):
                ybank[blk] = b
                slab, wk = wload(w_in, 0, 8, gate_c0 + blk * 128, 128)
                bg = nb()
                for k in range(8):
                    mm(PS[bg][:, 0:Tn], slab[:, k, 0:128], G1[:, k, 0:Tn], k == 0, k == 7, [wk] + xk, ("ps", bg))
                i = blk % 2
                act(sA[i][:, 0:Tn], PS[bg][:, 0:Tn], AF.Sigmoid, [("ps", bg)], [("sA", i)])
                if mode == "first":
                    tt("dve", MG[:, blk, 0:Tn], sA[i][:, 0:Tn], PS[b][:, 0:Tn], ALU.mult, [("sA", i), ("ps", b)], [("MG", blk)])
                elif mode == "mid":
                    tt("dve", sB[i][:, 0:Tn], sA[i][:, 0:Tn], PS[b][:, 0:Tn], ALU.mult, [("sA", i), ("ps", b)], [("sB", i)])
                    tt("dve", MG[:, blk, 0:Tn], MG[:, blk, 0:Tn], sB[i][:, 0:Tn], ALU.add, [("sB", i), ("MG", blk)], [("MG", blk)])
                else:
                    tt("dve", sB[i][:, 0:Tn], sA[i][:, 0:Tn], PS[b][:, 0:Tn], ALU.mult, [("sA", i), ("ps", b)], [("sB", i)])
                    tt("dve", mT_v[:, blk, 0:Tn], MG[:, blk, 0:Tn], sB[i][:, 0:Tn], ALU.add,
                       [("sB", i), ("MG", blk), "HBphase2"], [("mT", blk)])

            proj_fm(Wo, 0, kch, 0, 8, srcT, Tn, src_keys, cons_y, blk_per_slab=bps)

        def branchA():
            pend = [None]
            for c in range(8):
                i = c % 2
                slc, wkc = wload(w_in, 0, 8, C_SCC + c * 128, 128)
                slx, wkx = wload(w_in, 0, 8, C_SCX + c * 128, 128)
                slb, wkb = wload(w_in, 0, 8, C_SCB + c * 128, 128)
                b1 = nb()
                for k in range(8):
                    mm(PS[b1][:, 0:Tn], slc[:, k, :], G1[:, k, 0:Tn], k == 0, k == 7, [wkc] + xk, ("ps", b1))
                act(sA[i][:, 0:Tn], PS[b1][:, 0:Tn], AF.Copy, [("ps", b1)], [("sA", i)])
                b2 = nb()
                for k in range(8):
                    mm(PS[b2][:, 0:Tn], slx[:, k, :], G1[:, k, 0:Tn], k == 0, k == 7, [wkx] + xk, ("ps", b2))
                b4 = nb()
                for k in range(8):
                    mm(PS[b4][:, 0:Tn], slb[:, k, :], G1[:, k, 0:Tn], k == 0, k == 7, [wkb] + xk, ("ps", b4))
                u = ub[i]
                cp("dve", u[:, 0:2], histA[:, c, :], ["histA"], [("ub", i)])
                tt("dve", u[:, 2:2 + Tn], sA[i][:, 0:Tn], PS[b2][:, 0:Tn], ALU.mult, [("sA", i), ("ps", b2)], [("ub", i)])
                cp("dve", histA[:, c, :], u[:, Tn:Tn + 2], [("ub", i)], ["histA"])
                if last:
                    tt("dve", pcv[:, c, :], sA[i][:, Tn - 2:Tn], PS[b2][:, Tn - 2:Tn], ALU.mult, [("sA", i), ("ps", b2)], ["pcv"])
                for k3 in range(3):
                    ts("dve", dg[i][:, k3, :], ident_b, scw[:, c, k3:k3 + 1], None, ALU.mult, None, ["cstb", "scw"], [("dg", i)])
                act(sB[i][:, 0:Tn], PS[b4][:, 0:Tn], AF.Copy, [("ps", b4)], [("sB", i)])

                def part2(c=c, i=i, u=u):
                    b3 = nb()
                    for k3 in range(3):
                        mm(PS[b3][:, 0:Tn], dg[i][:, k3, :], u[:, k3:k3 + Tn], k3 == 0, k3 == 2, [("dg", i), ("ub", i)], ("ps", b3))
                    tt("dve", G2[:, c, 0:Tn], sB[i][:, 0:Tn], PS[b3][:, 0:Tn], ALU.mult, [("sB", i), ("ps", b3)], [("G2", c)])
                if pend[0]:
                    pend[0]()
                pend[0] = part2
            pend[0]()
            out_proj_and_merge(w_sc_out, 8, G2, [("G2", c) for c in range(8)], C_G, "first", 4)

        def branchC():
            def cons_q(blk, b):
                act(G2[:, blk, 0:Tn], PS[b][:, 0:Tn], AF.Copy, [("ps", b)], [("G2", blk)], scale=1.0 / 16.0)
            proj_fm(w_in, 0, 8, C_Q, 8, G1, Tn, xk, cons_q)
            pbufs = [(sC[0][:, 0:512].rearrange("p (h m) -> p h m", h=2), ("sC", 0)), (sC[1][:, 0:512].rearrange("p (h m) -> p h m", h=2), ("sC", 1))]
            pendc = [None]
            itc = 0
            for j in range(NS):
                for hpair in range(2):
                    pb, pbk = pbufs[itc % 2]
                    so = (itc % 2) * 8
                    itc += 1
                    b = nb()
                    for hh in range(2):
                        h = hpair * 2 + hh
                        for dc in range(2):
                            mm(PS[b][:, hh * 256:(hh + 1) * 256], G2[:, 2 * h + dc, j * 128:(j + 1) * 128], KT[:, 2 * h + dc, :],
                               dc == 0, dc == 1, [("G2", 2 * h + dc), "KT"], ("ps", b))
                    smk = ("sm", so)
                    S.add("dve", lambda g, b=b, so=so: g.tensor_reduce(out=sm[:, so:so + 2], in_=PS[b][:, :].rearrange("p (h m) -> p h m", h=2),
                                                                      op=ALU.max, axis=AX), reads=[("ps", b)], writes=[smk])
                    ts("dve", sm[:, so + 2:so + 4], sm[:, so:so + 2], -1.0, None, ALU.mult, None, [smk], [smk])
                    for hh in range(2):
                        act(pb[:, hh, :], PS[b][:, hh * 256:(hh + 1) * 256], AF.Exp, [("ps", b), smk], [pbk, smk],
                            bias=sm[:, so + 2 + hh:so + 3 + hh], accum=sm[:, so + 4 + hh:so + 5 + hh])
                    S.add("dve", lambda g, so=so: g.reciprocal(sm[:, so + 6:so + 8], sm[:, so + 4:so + 6]), reads=[smk], writes=[smk])
                    for hh in range(2):
                        ts("dve", pb[:, hh, :], pb[:, hh, :], sm[:, so + 6 + hh:so + 7 + hh], None, ALU.mult, None, [pbk, smk], [pbk])

                    def trpart(j=j, hpair=hpair, pb=pb, pbk=pbk):
                        bt = nb()
                        pT = PS[bt][:].bitcast(BF16)
                        for hh in range(2):
                            for mc in range(2):
                                tr(pT[:, (hh * 2 + mc) * 128:(hh * 2 + mc + 1) * 128], pb[:, hh, mc * 128:(mc + 1) * 128], ident_b,
                                   [pbk, "cstb"], ("ps", bt))
                        cp(ev_eng(), PT[:, hpair * 2:hpair * 2 + 2, :, j * 128:(j + 1) * 128],
                           pT[:, 0:512].rearrange("p (h m t) -> p h m t", h=2, m=2), [("ps", bt), "HBphase1"], [("PT", j, hpair)])
                    if pendc[0]:
                        pendc[0]()
                    pendc[0] = trpart
            pendc[0]()
            ptk = [("PT", j, hpv) for j in range(NS) for hpv in range(2)]
            for h in range(4):
                for dblk in range(2):
                    b = nb()
                    for mc in range(2):
                        mm(PS[b][:, 0:Tn], Vb[:, mc, h * 256 + dblk * 128:h * 256 + (dblk + 1) * 128], PT[:, h, mc, 0:Tn],
                           mc == 0, mc == 1, ["Vb"] + ptk, ("ps", b))
                    cp(ev_eng(), G3[:, 2 * h + dblk, 0:Tn], PS[b][:, 0:Tn], [("ps", b)], [("G3", 2 * h + dblk)])
            S.add("dve", lambda g: g.memset(stat[:, 22:23], 0.0), writes=ptk + ["HBphase1"])
            out_proj_and_merge(w_attn_o, 8, G3, [("G3", c) for c in range(8)], C_G + 2048, "mid", 4)

        def branchB():
            for cbk in range(4):
                slab, wk = wload(w_in, 0, 8, C_Z + cbk * 512, 512)
                for j in range(NS):
                    b = nb()
                    for k in range(8):
                        mm(PS[b][:, :], G1[:, k, j * 128:(j + 1) * 128], slab[:, k, :], k == 0, k == 7, [wk] + xk, ("ps", b))
                    act(zs_v[:, j, cbk * 512:(cbk + 1) * 512], PS[b][:, :], AF.Silu, [("ps", b), "HBphase1"], [("zs", j, cbk)])
            slab, wk = wload(w_in, 0, 8, C_DT, 32)
            for j in range(NS):
                b = nb()
                for k in range(8):
                    mm(PS[b][:, 0:32], G1[:, k, j * 128:(j + 1) * 128], slab[:, k, :], k == 0, k == 7, [wk] + xk, ("ps", b))
                tt("dve", dtt[:, j, :], PS[b][:, 0:32], hp[:, 0, :], ALU.add, [("ps", b), "hp"], [("dt", j)])
                act(dtt[:, j, :], dtt[:, j, :], AF.Exp, [("dt", j)], [("dt", j)])
                act(dtt[:, j, :], dtt[:, j, :], AF.Ln, [("dt", j)], [("dt", j)], bias=1.0)
                act(lndt[:, j, :], dtt[:, j, :], AF.Ln, [("dt", j)], [("lndt", j)])
                tt("dve", dtab[:, j, :], dtt[:, j, :], abc[:], ALU.mult, [("dt", j), "abc"], [("dtab", j)])
            pq = []

            def cons_xbc(cc, b):
                i = cc % 2
                xbuf = xb[i]
                cp("dve", xbuf[:, 0:3], histB[:, cc, :], ["histB"], [("xb", i)])
                act(xbuf[:, 3:3 + Tn], PS[b][:, 0:Tn], AF.Copy, [("ps", b)], [("xb", i)])
                cp("dve", histB[:, cc, :], xbuf[:, Tn:Tn + 3], [("xb", i)], ["histB"])
                if last:
                    cp("dve", pscv[:, cc, :], PS[b][:, Tn - 3:Tn], [("ps", b)], ["pscv"])
                for k4 in range(4):
                    ts("dve", dg[i][:, k4, :], ident_b, cw[:, cc, k4:k4 + 1], None, ALU.mult, None, ["cstb", "cw"], [("dg", i)])
                st2 = {}

                def part2(cc=cc, i=i, xbuf=xbuf, st2=st2):
                    b2 = nb()
                    for k4 in range(4):
                        mm(PS[b2][:, 0:Tn], dg[i][:, k4, :], xbuf[:, k4:k4 + Tn], k4 == 0, k4 == 3, [("dg", i), ("xb", i)], ("ps", b2))
                    if cc < 20:
                        dst = sC[i][:, 0:Tn] if cc < 16 else BT[:, cc - 16, 0:Tn]
                        dk = ("sC", i) if cc < 16 else ("BT", cc - 16)
                        act(dst, PS[b2][:, 0:Tn], AF.Silu, [("ps", b2), "cb"], [dk], bias=cb[:, cc:cc + 1])
                        st2["dst"], st2["dk"] = dst, dk
                    else:
                        act(CT[:, cc - 20, 0:Tn], PS[b2][:, 0:Tn], AF.Silu, [("ps", b2), "cb"], [("CT", cc - 20)], bias=cb[:, cc:cc + 1])

                def part3(cc=cc, st2=st2):
                    if cc >= 20:
                        return
                    dst, dk = st2["dst"], st2["dk"]
                    bt = nb()
                    pT = PS[bt][:].bitcast(BF16)
                    for j in range(NS):
                        tr(pT[:, j * 128:(j + 1) * 128], dst[:, j * 128:(j + 1) * 128], ident_b, [dk, "cstb"], ("ps", bt))
                    if cc < 16:
                        cp(ev_eng(), xs_v[:, :, cc * 128:(cc + 1) * 128], pT[:, 0:512].rearrange("p (j c) -> p j c", j=4),
                           [("ps", bt), "HBphase1"], [("xs", cc)])
                    else:
                        cp(ev_eng(), Btok[:, :, cc - 16, :], pT[:, 0:512].rearrange("p (j c) -> p j c", j=4), [("ps", bt)], ["Btok"])
                pq.append([part2, part3])
                if len(pq) >= 2:
                    pq[-2][0]()
                if len(pq) >= 3:
                    pq[-3][1]()
            proj_fm(w_in, 0, 8, C_XBC, 24, G1, Tn, xk, cons_xbc)
            pq[-1][0]()
            pq[-2][1]()
            pq[-1][1]()
            xsk = [("xs", c) for c in range(16)]
            ypre = {}
            for blk2 in range(2):
                ypre[("y", blk2)] = wload(w_ssm_out, 0, 16, blk2 * 256, 256)
                for bb in range(2):
                    ypre[("g", blk2 * 2 + bb)] = wload(w_in, 0, 8, C_G + 1024 + (blk2 * 2 + bb) * 128, 128)
            st4 = junk[:].bitcast(F32)
            eac4 = st4[:, 0:256].rearrange("p (j c) -> p j c", j=4)
            dsd4 = st4[:, 256:384].rearrange("p (j c) -> p j c", j=4)
            nbias4 = st4[:, 384:512].rearrange("p (j c) -> p j c", j=4)
            for j in range(NS):
                ba = 7
                mm(PS[ba][:, 0:32], tri_b, dtab[:, j, :], True, True, ["cstb", ("dtab", j)], ("ps", ba))
                mm(PS[ba][:, 32:64], ones_b, dtab[:, j, :], True, True, ["cstb", ("dtab", j)], ("ps", ba))
                cp("dve", acs[:], PS[ba][:, 0:64], [("ps", ba)], ["acs"])
                act(eac4[:, j, :], acs[:], AF.Exp, ["acs", "junk"], [("eac4", j)])
                tt("dve", dsd[:], acs[:, 32:64], acs[:, 0:32], ALU.subtract, ["acs"], ["dsd"])
                act(dsd[:], dsd[:], AF.Exp, ["dsd"], ["dsd"])
                tt("dve", dsd4[:, j, :], dsd[:], dtt[:, j, :], ALU.mult, ["dsd", ("dt", j), "junk"], [("dsd4", j)])
                tt("dve", nbias4[:, j, :], lndt[:, j, :], acs[:, 0:32], ALU.subtract, [("lndt", j), "acs", "junk"], [("nbias4", j)])
                cp("dve", nbh4[:, j, :], nbias4[:, j, :], [("nbias4", j)], [("nbh", j)])
                tt("dve", nbl4[:, j, :], nbias4[:, j, :], nbh4[:, j, :], ALU.subtract, [("nbias4", j), ("nbh", j)], [("nbh", j)])
            BY = [0, 1]
            BO = [2, 3]
            BL = [4, 5]
            for j in range(NS):
                zk = [("zs", j, c) for c in range(4)]
                bc_ = 7
                for g4 in range(4):
                    mm(PS[bc_][:, g4 * 128:(g4 + 1) * 128], BT[:, g4, j * 128:(j + 1) * 128], CT[:, g4, j * 128:(j + 1) * 128], True, True,
                       [("BT", g4), ("CT", g4)], ("ps", bc_))
                cp("act", CBs[:].rearrange("p g i -> p (g i)"), PS[bc_][:, :], [("ps", bc_)], ["CBs"])

                def stageA(t, j=j):
                    g4, half = t // 2, t % 2
                    by, bo = BY[g4 % 2], BO[g4 % 2]
                    if half == 0:
                        mm(PS[bo][:, :], CT[:, g4, j * 128:(j + 1) * 128], Sbf[:, g4 * 512:(g4 + 1) * 512], True, True,
                           [("CT", g4), ("Sbf", g4)], ("ps", bo))
                    li = t % 3
                    bl = BL[t % 2]
                    h0 = g4 * 8 + half * 4
                    mm(PS[bl][:, :], ident_b, fap(mneg_b, [[0, 4], [1, 128]]), True, False, ["cstb"], ("ps", bl))
                    mm(PS[bl][:, :], ident_b, fap(nbh4[:, j, h0:h0 + 1], [[1, 4], [0, 128]]), False, False, ["cstb", ("nbh", j)], ("ps", bl))
                    mm(PS[bl][:, :], ident_b, fap(nbl4[:, j, h0:h0 + 1], [[1, 4], [0, 128]]), False, False, ["cstb", ("nbh", j)], ("ps", bl))
                    for hh in range(4):
                        h = h0 + hh
                        mm(PS[bl][:, hh * 128:(hh + 1) * 128], fap(dtab[:, j, h:h + 1], [[0, 128]]), tri_b, False, hh == 3,
                           [("dtab", j), "cstb"], ("ps", bl))
                    act(Lt[li][:].rearrange("p a b -> p (a b)"), PS[bl][:, :], AF.Exp, [("ps", bl)], [("Lt", li)])
                    tt("dve", MT[li][:], Lt[li][:], fap(CBs[:, g4, 0:1], [[0, 4], [1, 128]]), ALU.mult, [("Lt", li), "CBs"], [("MT", li)])

                def S1(t, j=j):
                    g4, half = t // 2, t % 2
                    by = BY[g4 % 2]
                    li = t % 3
                    for hh in range(4):
                        h = g4 * 8 + half * 4 + hh
                        hl = half * 4 + hh
                        mm(PS[by][:, hl * 64:(hl + 1) * 64], MT[li][:, hh, :], xs_v[:, j, h * 64:(h + 1) * 64], True, True,
                           [("MT", li)] + xsk, ("ps", by))

                def S2(g4, j=j, zk=zk):
                    by, bo = BY[g4 % 2], BO[g4 % 2]
                    i = g4 % 2
                    tt("dve", sA[i][:, :].rearrange("p (h q) -> p h q", h=8), PS[bo][:, :].rearrange("p (h q) -> p h q", h=8),
                       fap(eac4[:, j, g4 * 8:g4 * 8 + 1], [[1, 8], [0, 64]]), ALU.mult, [("ps", bo), ("eac4", j)], [("sA", i)])
                    tt("dve", sA[i][:, :], sA[i][:, :], PS[by][:, :], ALU.add, [("sA", i), ("ps", by)], [("sA", i)])
                    tt("pool", sB[i][:, :].rearrange("p (h q) -> p h q", h=8), xs_v[:, j, g4 * 512:(g4 + 1) * 512].rearrange("p (h q) -> p h q", h=8),
                       fap(hp[:, 2, g4 * 8:g4 * 8 + 1], [[1, 8], [0, 64]]), ALU.mult, xsk + ["hp"], [("sB", i)])
                    tt("dve", sA[i][:, :], sA[i][:, :], sB[i][:, :], ALU.add, [("sA", i), ("sB", i)], [("sA", i)])
                    tt("dve", yb[i][:, :], sA[i][:, :], zs_v[:, j, g4 * 512:(g4 + 1) * 512], ALU.mult, [("sA", i)] + zk, [("yb", i)])
                    xg = xddg[i]
                    tt("pool", xg[:].rearrange("p (h q) -> p h q", h=8), xs_v[:, j, g4 * 512:(g4 + 1) * 512].rearrange("p (h q) -> p h q", h=8),
                       fap(dsd4[:, j, g4 * 8:g4 * 8 + 1], [[1, 8], [0, 64]]), ALU.mult, xsk + [("dsd4", j)], [("xddg", i)])
                    tt("pool", Sf[:, g4 * 512:(g4 + 1) * 512].rearrange("p (h q) -> p h q", h=8),
                       Sf[:, g4 * 512:(g4 + 1) * 512].rearrange("p (h q) -> p h q", h=8),
                       fap(eac4[:, j, 32 + g4 * 8:32 + g4 * 8 + 1], [[1, 8], [0, 64]]), ALU.mult, [("Sf", g4), ("eac4", j)], [("Sf", g4)])

                def S3(g4):
                    i = g4 % 2
                    act(sB[i][:, :], yb[i][:, :], AF.Square, [("yb", i)], [("sB", i), ("gss", i)], accum=stat[:, 8 + i:9 + i])
                    rstd_from_ssq(8 + i, 12 + i, 512, 128, ("gss", i))

                def S4(g4, j=j):
                    i = g4 % 2
                    act(ynb[i][:, :], yb[i][:, :], AF.Copy, [("yb", i), ("gss", i)], [("ynb", i)], scale=stat[:, 12 + i:13 + i])
                    mm(PS[7][:, :], Btok[:, j, g4, :], xddg[i][:], True, True, ["Btok", ("xddg", i)], ("ps", 7))
                    pT = PS[6][:].bitcast(BF16)
                    for q in range(4):
                        tr(pT[:, q * 128:(q + 1) * 128], ynb[i][:, q * 128:(q + 1) * 128], ident_b, [("ynb", i), "cstb"], ("ps", 6))

                def S5(g4, j=j):
                    pT = PS[6][:].bitcast(BF16)
                    tt("dve", Sf[:, g4 * 512:(g4 + 1) * 512], Sf[:, g4 * 512:(g4 + 1) * 512], PS[7][:, :], ALU.add,
                       [("Sf", g4), ("ps", 7)], [("Sf", g4)])
                    cp("act", Sbf[:, g4 * 512:(g4 + 1) * 512], Sf[:, g4 * 512:(g4 + 1) * 512], [("Sf", g4)], [("Sbf", g4)])
                    for q in range(4):
                        ch = g4 * 4 + q
                        dstb = G2 if ch < 8 else G3
                        dk = ("G2" if ch < 8 else "G3", ch % 8)
                        if q % 2 == 0:
                            ts("dve", dstb[:, ch % 8, j * 128:(j + 1) * 128], pT[:, q * 128:(q + 1) * 128], nw[:, ch:ch + 1], None,
                               ALU.mult, None, [("ps", 6), "nw"], [dk])
                        else:
                            act(dstb[:, ch % 8, j * 128:(j + 1) * 128], pT[:, q * 128:(q + 1) * 128], AF.Copy, [("ps", 6), "nw"], [dk],
                                scale=nw[:, ch:ch + 1])

                for u in range(14):
                    for g4 in range(4):
                        if u == 2 * g4 + 7:
                            S5(g4)
                    for g4 in range(4):
                        if u == 2 * g4 + 6:
                            S4(g4)
                    for g4 in range(4):
                        if u == 2 * g4 + 5:
                            S3(g4)
                    for g4 in range(4):
                        if u == 2 * g4 + 4:
                            S2(g4)
                    if 2 <= u < 10:
                        S1(u - 2)
                    if u < 8:
                        stageA(u)
            if last:
                for q in range(16):
                    b = nb()
                    tr(PS[b][:, 0:128], Sf[:, q * 128:(q + 1) * 128], ident_f, [("Sf", q // 4), "cstf"], ("ps", b))
                    so = ostg[q % 2]
                    cp(ev_eng(), so[:, 0:128], PS[b][:, 0:128], [("ps", b)], [("ostg", q % 2)])
                    dma_out(pssm[q * 128:(q + 1) * 128, :], so[:, 0:128], [("ostg", q % 2)], slot=("ostg", q % 2))
            ybank = {}

            def ynT(k):
                return (G2 if k < 8 else G3)[:, k % 8, 0:Tn]
            S.add("dve", lambda g: g.memset(stat[:, 20:21], 0.0), writes=["HBphase2", "HBphase1"] + xsk + [("zs", j, c) for j in range(4) for c in range(4)])
            for blk2 in range(4):
                slab, wk = ypre.pop(("y", blk2)) if ("y", blk2) in ypre else wload(w_ssm_out, 0, 16, blk2 * 256, 256)
                for bb in range(2):
                    blk = blk2 * 2 + bb
                    b = nb()
                    for k in range(16):
                        mm(PS[b][:, 0:Tn], slab[:, k, bb * 128:(bb + 1) * 128], ynT(k), k == 0, k == 15,
                           [wk, ("G2", k % 8) if k < 8 else ("G3", k % 8)], ("ps", b))
                    slg, wkg = ypre.pop(("g", blk)) if ("g", blk) in ypre else wload(w_in, 0, 8, C_G + 1024 + blk * 128, 128)
                    bg = nb()
                    for k in range(8):
                        mm(PS[bg][:, 0:Tn], slg[:, k, 0:128], G1[:, k, 0:Tn], k == 0, k == 7, [wkg] + xk, ("ps", bg))
                    i = blk % 2
                    act(sA[i][:, 0:Tn], PS[bg][:, 0:Tn], AF.Sigmoid, [("ps", bg)], [("sA", i)])
                    tt("dve", sB[i][:, 0:Tn], sA[i][:, 0:Tn], PS[b][:, 0:Tn], ALU.mult, [("sA", i), ("ps", b)], [("sB", i)])
                    tt("dve", mT_v[:, blk, 0:Tn], MG[:, blk, 0:Tn], sB[i][:, 0:Tn], ALU.add,
                       [("sB", i), ("MG", blk), "HBphase2"], [("mT", blk)])

        if 'A' in branches:
            branchA()
        if 'C' in branches:
            branchC()
        if 'B' in branches:
            branchB()
        if branches != 'ACB':
            return
        mtk = [("mT", c) for c in range(8)]
        dma_in(nrm[:], nrm_bc[:, 1, :], "nrm")
        mslabs = [wload(w_merge_o, 0, 8, cbk * 512, 512) for cbk in range(2)]
        xn2 = [(xnb[0], ("xnb", 0)), (ostg[1][:].bitcast(BF16)[:, 0:D], ("ostg", 1))]
        pendn = [None]
        for j in range(NS):
            for cbk in range(2):
                slab, wk = mslabs[cbk]
                b = nb()
                for k in range(8):
                    mm(PS[b][:, :], mT_v[:, k, j * 128:(j + 1) * 128], slab[:, k, :], k == 0, k == 7, [wk] + mtk, ("ps", b))
                tt("dve", X[:, j, cbk * 512:(cbk + 1) * 512], X[:, j, cbk * 512:(cbk + 1) * 512], PS[b][:, :], ALU.add,
                   [("X", j), ("ps", b)], [("X", j)])
            act(junk[:, :], X[:, j, :], AF.Square, [("X", j)], ["junk", ("nst", j)], accum=stat[:, 48 + j:49 + j])
            ts("dve", stat[:, 52 + j:53 + j], stat[:, 48 + j:49 + j], 1.0 / D, EPS, ALU.mult, ALU.add, [("nst", j)], [("nst", j)])
            act(stat[:, 52 + j:53 + j], stat[:, 52 + j:53 + j], AF.Ln, [("nst", j)], [("nst", j)])
            act(stat[:, 52 + j:53 + j], stat[:, 52 + j:53 + j], AF.Exp, [("nst", j)], [("nst", j)], scale=-0.5)
            xn, xkey = xn2[j % 2]
            stt(xn[:, :], X[:, j, :], stat[:, 52 + j:53 + j], nrm[:, :], ALU.mult, ALU.mult, [("X", j), ("nst", j), "nrm"], [xkey])

            def trp(j=j, xn=xn, xkey=xkey):
                b = nb()
                pT = PS[b][:].bitcast(BF16)
                for k in range(8):
                    tr(pT[:, k * 128:(k + 1) * 128], xn[:, k * 128:(k + 1) * 128], ident_b, [xkey, "cstb"], ("ps", b))
                cp(ev_eng(), G1[:, :, j * 128:(j + 1) * 128], pT[:, 0:1024].rearrange("p (k t) -> p k t", k=8), [("ps", b)], [("G1", "all")])
            if pendn[0]:
                pendn[0]()
            pendn[0] = trp
        pendn[0]()
        for fb in range(22):
            slg, wkg = wload(w_gate, 0, 8, fb * 128, 128)
            slu, wku = wload(w_up, 0, 8, fb * 128, 128)
            b1 = nb()
            for k in range(8):
                mm(PS[b1][:, 0:Tn], slg[:, k, :], G1[:, k, 0:Tn], k == 0, k == 7, [wkg] + xk, ("ps", b1))
            i = fb % 2
            act(sA[i][:, 0:Tn], PS[b1][:, 0:Tn], AF.Silu, [("ps", b1)], [("sA", i)])
            b2 = nb()
            for k in range(8):
                mm(PS[b2][:, 0:Tn], slu[:, k, :], G1[:, k, 0:Tn], k == 0, k == 7, [wku] + xk, ("ps", b2))
            tt("dve", hT_v[:, fb, 0:Tn], sA[i][:, 0:Tn], PS[b2][:, 0:Tn], ALU.mult, [("sA", i), ("ps", b2), "HBphase2"], [("hT", fb)])
        htk = [("hT", c) for c in range(22)]
        for cbk in range(2):
            banks = [nb() for _ in range(NS)]
            for half in range(2):
                slab, wk = wload(w_down, half * 11 * 128, 11, cbk * 512, 512)
                for j in range(NS):
                    for k in range(11):
                        kk = half * 11 + k
                        mm(PS[banks[j]][:, :], hT_v[:, kk, j * 128:(j + 1) * 128], slab[:, k, :], kk == 0, kk == 21,
                           [wk] + htk, ("ps", banks[j]))
            for j in range(NS):
                tt("dve", X[:, j, cbk * 512:(cbk + 1) * 512], X[:, j, cbk * 512:(cbk + 1) * 512], PS[banks[j]][:, :], ALU.add,
                   [("X", j), ("ps", banks[j])], [("X", j)])
        S.add("dve", lambda g: g.memset(stat[:, 21:22], 0.0), writes=["HBphase1", "HBphase2"] + htk + mtk)
        dma_in(nrm[:], nrm_bc[:, 2, :], "nrm")
        for j in range(NS):
            act(junk[:, :], X[:, j, :], AF.Square, [("X", j)], ["junk", "stat0"], accum=stat[:, 0:1])
            rstd_from_ssq(0, 1, D, 128, "stat0")
            so = ostg[j % 2]
            stt(so[:, :], X[:, j, :], stat[:, 1:2], nrm[:, :], ALU.mult, ALU.mult, [("X", j), "stat0", "nrm"], [("ostg", j % 2)])
            dma_out(y_prompt[t0 + j * 128:t0 + (j + 1) * 128, :], so[:, :], [("ostg", j % 2)], slot=("ostg", j % 2))

    if stop in (3, 4, 5):
        return finish()
    for ti in range(ntiles):
        tile(ti, ti == NT - 1)
    dma_out(pconv[:, :, :], pcv[:], ["pcv"])
    dma_out(pssmc[:, :, :], pscv[:], ["pscv"])
    def sample():
        P = NSB
        Tn = NSB
        allkeys = list(set(S.lastw.keys()) | set(S.readers.keys()))
        S.add("dve", lambda g: g.memset(stat[:, 30:31], 0.0), writes=allkeys + ["SPH"])
        S.auto_reads = ["SPH"]
        xk = [("G1", "all")]
        Dc = dsd[:, 0:16]
        stat_g = eac[:, 0:64]
        dma_in(Dc, Dc_in[:, :], "Dc")
        Xf = X[:].rearrange("p j c -> p (j c)")
        Xs3 = X
        xtok2 = Xf[0:P, 1024:3072]
        bctok = Xf[0:P, 3072:4096]
        MGf = MG[:].rearrange("p k t -> p (k t)")
        t2 = MGf[:, 0:2048]
        prod = MGf[:, 2048:4096]
        HBf = HB[:].bitcast(F32)
        Kb = HBf[:, 0:2048]
        Vs = HBf[:, 2048:4096]
        St = HBf[:, 4096:6144]
        t1 = HBf[:, 6144:8192]
        G3f = G3[:].rearrange("p k t -> p (k t)").bitcast(F32)
        G2b = G2[:].rearrange("p k t -> p (k t)")
        G2f = G2b.bitcast(F32)

        def v3(ap, n):
            return ap.rearrange("p (a b) -> p a b", b=n)
        hsA = v3(G3f[:, 0:256], 16)
        hsB = v3(G3f[:, 256:1408], 16)
        uT = v3(G3f[:, 1408:1536], 16)
        xbcT = v3(G3f[:, 1536:1920], 16)
        MGs = v3(G3f[:, 1920:2048], 16)
        zT = v3(G2f[:, 0:256], 16)
        dtT = v3(G2f[:, 256:512], 16)
        decT = v3(G2f[:, 512:768], 16)
        Ys = v3(G2f[:, 768:1024], 16)
        yg = v3(G2f[:, 1024:1280], 16)
        sq = v3(G2f[:, 1280:1536], 16)
        vTs = v3(G2b[:, 3072:3200], 16)
        qTs = v3(G2b[:, 3200:3328], 16)
        oTs = v3(G2b[:, 3328:3456], 16)
        ynTs = v3(G2b[:, 3456:3712], 16)
        mTs = v3(G2b[:, 3712:3840], 16)
        hTs = v3(sC[0][:, 0:352], 16)
        xraw = v3(sA[1][:, 0:384], 16)
        Vaug = Sf[:].bitcast(BF16)[:, 0:2056].rearrange("p (m h d) -> p m h d", m=2, h=4)
        q_tok = xdd[0:P, 0:1024]
        o_tokb = xdd[0:P, 1024:2048]
        o_raw = Xf[0:P, 1024:2048]
        tA = sA[0][:, 0:Tn]
        tB = sB[0][:, 0:Tn]
        tC = sB[1][:, 0:Tn]
        dts = dtt[0:P, 0, :]
        dtas = dtt[0:P, 1, :]
        decs = dtt[0:P, 2, :]
        ssc = acs[:, 0:8]
        sp8 = sm[:, 0:8]
        p8 = CBs[:, 0, 0:8]
        den1 = stat[0:1, 40:44]
        dent = stat[0:P, 44:48]
        idf16 = ident_f[0:P, 0:P]
        idb16 = ident_b[0:P, 0:P]

        dma_in(X[0:P, 0, :], xs_in[:, :], ("X", 0))
        norm_transpose(X, P, 1, 0, G1, "X", Tn)

        def load_T(src_ap, dst3, row0, tag):
            i = cnt["rr"] % 2
            cnt["rr"] += 1
            so = ostg[i]
            S.add("sp", lambda g: g.dma_start(out=so[0:P, :], in_=src_ap), writes=[("ostg", i)], slot=("ostg", i))
            b = nb()
            for c in range(8):
                tr(PS[b][:, c * 16:(c + 1) * 16], so[0:P, c * 128:(c + 1) * 128], idf16, [("ostg", i), "cstf"], ("ps", b))
            cp(ev_eng(), dst3[:, row0:row0 + 8, :], PS[b][:, 0:128].rearrange("p (c t) -> p c t", c=8), [("ps", b)], [tag])
        for r in range(2):
            load_T(st_conv[:, r, :], hsA, r * 8, "hsA")
        for r in range(3):
            for q3 in range(3):
                load_T(st_sconv[:, r, q3 * 1024:(q3 + 1) * 1024], hsB, r * 24 + q3 * 8, "hsB")
        S.add("sp", lambda g: g.dma_start(out=sconv[:, 0, :], in_=st_conv[:, 1, :]), slot=mslot())
        S.add("sp", lambda g: g.dma_start(out=sssmc[:, 0:2, :], in_=st_sconv[:, 1:3, :]), slot=mslot())

        def store_T(src3, row0, dst_ap, skeys):
            i = cnt["rr"] % 2
            cnt["rr"] += 1
            so = ostg[i]
            for half in range(2):
                b = nb()
                for c in range(4):
                    tr(PS[b][0:P, c * 128:(c + 1) * 128], src3[:, row0 + half * 4 + c, :], ident_f, skeys + ["cstf"], ("ps", b))
                cp(ev_eng(), so[0:P, half * 512:(half + 1) * 512], PS[b][0:P, :], [("ps", b)], [("ostg", i)])
            S.add("sp", lambda g: g.dma_start(out=dst_ap, in_=so[0:P, :]), reads=[("ostg", i)], slot=("ostg", i))

        def s_out_proj(Wo, kch, srcT, src_keys, gate_c0, mode):
            bps = 4 if kch == 8 else 2
            for blk0 in range(0, 8, bps):
                slab, wk = wload(Wo, 0, kch, blk0 * 128, bps * 128)
                if blk0 % 4 == 0:
                    slg, wkg = wload(w_in, 0, 8, gate_c0 + blk0 * 128, 512)
                for bb in range(bps):
                    blk = blk0 + bb
                    b = nb()
                    for k in range(kch):
                        mm(PS[b][:, 0:Tn], slab[:, k, bb * 128:(bb + 1) * 128], srcT[:, k, :], k == 0, k == kch - 1, [wk] + src_keys, ("ps", b))
                    bg = nb()
                    go = (blk % 4) * 128
                    for k in range(8):
                        mm(PS[bg][:, 0:Tn], slg[:, k, go:go + 128], G1[:, k, 0:Tn], k == 0, k == 7, [wkg] + xk, ("ps", bg))
                    act(tA, PS[bg][:, 0:Tn], AF.Sigmoid, [("ps", bg)], ["tA"])
                    if mode == "first":
                        tt("dve", MGs[:, blk, :], tA, PS[b][:, 0:Tn], ALU.mult, ["tA", ("ps", b)], ["MGs"])
                    else:
                        tt("dve", tC, tA, PS[b][:, 0:Tn], ALU.mult, ["tA", ("ps", b)], ["tC"])
                        if mode == "mid":
                            tt("dve", MGs[:, blk, :], MGs[:, blk, :], tC, ALU.add, ["tC", "MGs"], ["MGs"])
                        else:
                            tt("dve", mTs[:, blk, :], MGs[:, blk, :], tC, ALU.add, ["tC", "MGs"], ["mTs"])

        for c in range(8):
            if c % 4 == 0:
                slc4, wkc = wload(w_in, 0, 8, C_SCC + c * 128, 512)
                slx4, wkx = wload(w_in, 0, 8, C_SCX + c * 128, 512)
                slb4, wkb = wload(w_in, 0, 8, C_SCB + c * 128, 512)
            co = (c % 4) * 128
            b1 = nb()
            for k in range(8):
                mm(PS[b1][:, 0:Tn], slc4[:, k, co:co + 128], G1[:, k, 0:Tn], k == 0, k == 7, [wkc] + xk, ("ps", b1))
            act(tA, PS[b1][:, 0:Tn], AF.Copy, [("ps", b1)], ["tA"])
            b2 = nb()
            for k in range(8):
                mm(PS[b2][:, 0:Tn], slx4[:, k, co:co + 128], G1[:, k, 0:Tn], k == 0, k == 7, [wkx] + xk, ("ps", b2))
            tt("dve", uT[:, c, :], tA, PS[b2][:, 0:Tn], ALU.mult, ["tA", ("ps", b2)], ["uT"])
            ts("dve", tB, hsA[:, c, :], scw[:, c, 0:1], None, ALU.mult, None, ["hsA", "scw"], ["tB"])
            stt(tB, hsA[:, 8 + c, :], scw[:, c, 1:2], tB, ALU.mult, ALU.add, ["hsA", "scw", "tB"], ["tB"])
            stt(tB, uT[:, c, :], scw[:, c, 2:3], tB, ALU.mult, ALU.add, ["uT", "scw", "tB"], ["tB"])
            b4 = nb()
            for k in range(8):
                mm(PS[b4][:, 0:Tn], slb4[:, k, co:co + 128], G1[:, k, 0:Tn], k == 0, k == 7, [wkb] + xk, ("ps", b4))
            tt("dve", vTs[:, c, :], tB, PS[b4][:, 0:Tn], ALU.mult, ["tB", ("ps", b4)], ["vTs"])
        store_T(uT, 0, sconv[:, 1, :], ["uT"])
        s_out_proj(w_sc_out, 8, vTs, ["vTs"], C_G, "first")

        def cons_q(blk, b):
            act(qTs[:, blk, :], PS[b][:, 0:Tn], AF.Copy, [("ps", b)], ["qTs"], scale=1.0 / 16.0)
        proj_fm(w_in, 0, 8, C_Q, 8, G1, Tn, xk, cons_q)
        bq = nb()
        pTq = PS[bq][:].bitcast(BF16)
        for blk in range(8):
            tr(pTq[0:P, blk * 128:(blk + 1) * 128], qTs[:, blk, :], ident_b, ["qTs", "cstb"], ("ps", bq))
        cp("act", q_tok, pTq[0:P, 0:1024], [("ps", bq)], ["q_tok"])
        memset("dve", Vaug[:, :, :, 256:257], 1.0, ["Vaug1"])
        HQ = [HBf[:, 0:2048], HBf[:, 2048:4096], HBf[:, 4096:6144], HBf[:, 6144:8192]]
        HK = ["hbA", "hbB", "hbC", "hbD"]
        den1s = [stat[0:1, 40:44], stat[0:1, 60:64]]

        def kv_load(bi):
            i = bi % 2
            S.add("sp", lambda g: g.dma_start(out=HQ[2 * i].rearrange("p (m c) -> p m c", m=2),
                                              in_=ck_in[bi].rearrange("(m p) c -> p m c", p=128)), writes=[HK[2 * i]], slot=("kb", i))
            S.add("sp", lambda g: g.dma_start(out=HQ[2 * i + 1].rearrange("p (m c) -> p m c", m=2),
                                              in_=cv_in[bi].rearrange("(m p) c -> p m c", p=128)), writes=[HK[2 * i + 1]], slot=("vs", i))
        kv_load(0)
        for bi in range(P):
            if bi + 1 < P:
                kv_load(bi + 1)
            i = bi % 2
            Kb_, Vs_, kk, vk = HQ[2 * i], HQ[2 * i + 1], HK[2 * i], HK[2 * i + 1]
            oh = fap(ident_b[0:P, bi:bi + 1], [[0, 128]])
            bq0 = nb()
            mm(PS[bq0][:, :], oh, q_tok[:, 0:512], True, True, ["cstb", "q_tok"], ("ps", bq0))
            bq1 = nb()
            mm(PS[bq1][:, :], oh, q_tok[:, 512:1024], True, True, ["cstb", "q_tok"], ("ps", bq1))
            Kb3 = Kb_.rearrange("p (m c) -> p m c", m=2)
            pr3 = prod.rearrange("p (m c) -> p m c", m=2)
            tt("dve", pr3[:, :, 0:512], Kb3[:, :, 0:512], fap(PS[bq0][:, 0:1], [[0, 2], [1, 512]]), ALU.mult, [kk, ("ps", bq0)], ["prod"])
            tt("dve", pr3[:, :, 512:1024], Kb3[:, :, 512:1024], fap(PS[bq1][:, 0:1], [[0, 2], [1, 512]]), ALU.mult, [kk, ("ps", bq1)], ["prod"])
            S.add("dve", lambda g: g.tensor_reduce(out=ssc, in_=prod.rearrange("p (a d) -> p a d", d=256), op=ALU.add, axis=AX),
                  reads=["prod"], writes=["ssc"])
            ts("dve", ssc, ssc, 80.0, None, ALU.min, None, ["ssc"], ["ssc"])
            act(p8, ssc, AF.Exp, ["ssc"], ["p8"])
            act(Vaug[:, :, :, 0:256], Vs_.rearrange("p (m h d) -> p m h d", m=2, h=4), AF.Copy, [vk], ["Vaug"])
            bos = [nb(), nb(), nb(), nb()]
            for h in range(4):
                bo = bos[h]
                for mc in range(2):
                    mm(PS[bo][0:1, 0:257], p8[:, mc * 4 + h:mc * 4 + h + 1], Vaug[:, mc, h, :], mc == 0, mc == 1,
                       ["p8", "Vaug", "Vaug1"], ("ps", bo))
            orow = ostg[i][0:1, :]
            den1 = den1s[i]
            for h in range(4):
                bo = bos[h]
                cp("act", orow[:, h * 256:(h + 1) * 256], PS[bo][0:1, 0:256], [("ps", bo)], [("ostg", i)])
                cp("act", den1[:, h:h + 1], PS[bo][0:1, 256:257], [("ps", bo)], [("den1", i)])
            S.add("sp", lambda g, bi=bi, orow=orow: g.dma_start(out=o_raw[bi:bi + 1, :], in_=orow), reads=[("ostg", i)], writes=["o_raw"], slot=("ostg", i))
            S.add("sp", lambda g, bi=bi, den1=den1: g.dma_start(out=dent[bi:bi + 1, :], in_=den1), reads=[("den1", i)], writes=["dent"], slot=("den", i))
        S.add("dve", lambda g: g.reciprocal(dent, dent), reads=["dent"], writes=["dent"])
        tt("dve", o_tokb.rearrange("p (h d) -> p h d", h=4), o_raw.rearrange("p (h d) -> p h d", h=4),
           fap(dent[:, 0:1], [[1, 4], [0, 256]]), ALU.mult, ["o_raw", "dent"], ["o_tokb"])
        bo_ = nb()
        pTo = PS[bo_][:].bitcast(BF16)
        for blk in range(8):
            tr(pTo[:, blk * 16:(blk + 1) * 16], o_tokb[:, blk * 128:(blk + 1) * 128], idb16, ["o_tokb", "cstb"], ("ps", bo_))
        cp("act", oTs, pTo[:, 0:128].rearrange("p (k t) -> p k t", k=8), [("ps", bo_)], ["oTs"])
        s_out_proj(w_attn_o, 8, oTs, ["oTs"], C_G + 2048, "mid")

        def cons_z(blk, b):
            act(zT[:, blk, :], PS[b][:, 0:Tn], AF.Silu, [("ps", b)], ["zT"])
        proj_fm(w_in, 0, 8, C_Z, 16, G1, Tn, xk, cons_z)
        slab, wk = wload(w_in, 0, 8, C_DT, 32)
        b = nb()
        for k in range(8):
            mm(PS[b][0:P, 0:32], G1[:, k, 0:Tn], slab[:, k, :], k == 0, k == 7, [wk] + xk, ("ps", b))
        tt("dve", dts, PS[b][0:P, 0:32], hp[0:P, 0, :], ALU.add, [("ps", b), "hp"], ["dts"])
        act(dts, dts, AF.Exp, ["dts"], ["dts"])
        act(dts, dts, AF.Ln, ["dts"], ["dts"], bias=1.0)
        tt("dve", dtas, dts, abc[0:P, :], ALU.mult, ["dts", "abc"], ["dtas"])
        act(decs, dtas, AF.Exp, ["dtas"], ["decs"])

        def cons_xbc(cc, b):
            act(xraw[:, cc, :], PS[b][:, 0:Tn], AF.Copy, [("ps", b)], ["xraw"])
            ts("dve", tB, hsB[:, cc, :], cw[:, cc, 0:1], None, ALU.mult, None, ["hsB", "cw"], ["tB"])
            stt(tB, hsB[:, 24 + cc, :], cw[:, cc, 1:2], tB, ALU.mult, ALU.add, ["hsB", "cw", "tB"], ["tB"])
            stt(tB, hsB[:, 48 + cc, :], cw[:, cc, 2:3], tB, ALU.mult, ALU.add, ["hsB", "cw", "tB"], ["tB"])
            stt(tB, xraw[:, cc, :], cw[:, cc, 3:4], tB, ALU.mult, ALU.add, ["xraw", "cw", "tB"], ["tB"])
            act(xbcT[:, cc, :], tB, AF.Silu, ["tB", "cb"], ["xbcT"], bias=cb[:, cc:cc + 1])
        proj_fm(w_in, 0, 8, C_XBC, 24, G1, Tn, xk, cons_xbc)
        for q3 in range(3):
            store_T(xraw, q3 * 8, sssmc[:, 2, q3 * 1024:(q3 + 1) * 1024], ["xraw"])
        for half in range(2):
            b = nb()
            for c in range(4):
                tr(PS[b][0:P, c * 128:(c + 1) * 128], xbcT[:, 16 + half * 4 + c, :], ident_f, ["xbcT", "cstf"], ("ps", b))
            cp(ev_eng(), bctok[:, half * 512:(half + 1) * 512], PS[b][0:P, :], [("ps", b)], ["bctok"])

        def bcast_T(src16, dstT, key):
            cp("dve", xtok2.rearrange("p (h q) -> p h q", h=32), fap(src16[:, 0:1], [[1, 32], [0, 64]]), [key], ["xtok2"])
            b = nb()
            for hpi in range(16):
                tr(PS[b][:, hpi * 16:(hpi + 1) * 16], xtok2[:, hpi * 128:(hpi + 1) * 128], idf16, ["xtok2", "cstf"], ("ps", b))
            cp(ev_eng(), dstT, PS[b][:, 0:256].rearrange("p (a t) -> p a t", a=16), [("ps", b)], [key + "T"])
        bcast_T(dts, dtT, "dts")
        bcast_T(decs, decT, "decs")
        tt("dve", dtT, dtT, xbcT[:, 0:16, :], ALU.mult, ["dtsT", "xbcT"], ["dtsT"])
        def st_load(bi):
            i = bi % 2
            S.add("sp", lambda g: g.dma_start(out=HQ[2 * i].rearrange("p (a n) -> p a n", a=16),
                                              in_=st_ssm[bi].rearrange("(a q) n -> q a n", q=128)), writes=[HK[2 * i]], slot=("kb", i))
        st_load(0)
        for bi in range(P):
            if bi + 1 < P:
                st_load(bi + 1)
            i = bi % 2
            St_, t1_, sk_, tk_ = HQ[2 * i], HQ[2 * i + 1], HK[2 * i], HK[2 * i + 1]
            oh = fap(ident_f[0:P, bi:bi + 1], [[0, 128]])
            bB = nb()
            mm(PS[bB][:, :], oh, bctok[:, 0:512], True, True, ["cstf", "bctok"], ("ps", bB))
            bC = nb()
            mm(PS[bC][:, :], oh, bctok[:, 512:1024], True, True, ["cstf", "bctok"], ("ps", bC))
            tt("dve", t1_.rearrange("p (a n) -> p a n", a=16), St_.rearrange("p (a n) -> p a n", a=16),
               fap(decT[:, 0, bi:bi + 1], [[16, 16], [0, 128]]), ALU.mult, [sk_, "decsT"], [tk_])
            tt("dve", fap(t2[:, 0:1], [[512, 4], [128, 4], [1, 128]]), fap(PS[bB][:, 0:1], [[128, 4], [0, 4], [1, 128]]),
               fap(dtT[:, 0, bi:bi + 1], [[64, 4], [16, 4], [0, 128]]), ALU.mult, [("ps", bB), "dtsT"], ["t2"])
            tt("pool", t1_, t1_, t2, ALU.add, [tk_, "t2"], [tk_])
            S.add("pool", lambda g, bi=bi, t1_=t1_: g.dma_start(out=sssm[bi].rearrange("(a q) n -> q a n", q=128),
                                                             in_=t1_.rearrange("p (a n) -> p a n", a=16)), reads=[tk_], slot=("sto", i))
            tt("dve", fap(t2[:, 0:1], [[512, 4], [128, 4], [1, 128]]), fap(t1_[:, 0:1], [[512, 4], [128, 4], [1, 128]]),
               fap(PS[bC][:, 0:1], [[128, 4], [0, 4], [1, 128]]), ALU.mult, [tk_, ("ps", bC)], ["t2"])
            S.add("dve", lambda g, bi=bi: g.tensor_reduce(out=fap(Ys[:, 0, bi:bi + 1], [[16, 16]]), in_=t2.rearrange("p (a n) -> p a n", a=16),
                                                          op=ALU.add, axis=AX), reads=["t2"], writes=["Ys"])
        tt("dve", yg, xbcT[:, 0:16, :], fap(Dc[:, 0:1], [[1, 16], [0, 16]]), ALU.mult, ["xbcT", "Dc"], ["yg"])
        tt("dve", yg, yg, Ys, ALU.add, ["yg", "Ys"], ["yg"])
        tt("dve", yg, yg, zT, ALU.mult, ["yg", "zT"], ["yg"])
        tt("dve", sq, yg, yg, ALU.mult, ["yg"], ["sq"])
        bs_ = nb()
        mm(PS[bs_][:, 0:256], cstf[:, 3, :], sq.rearrange("p a t -> p (a t)"), True, True, ["cstf", "sq"], ("ps", bs_))
        gsv = stat_g
        S.add("dve", lambda g: g.tensor_reduce(out=gsv.rearrange("p (g t) -> p g t", g=4), in_=fap(PS[bs_][:, 0:1], [[64, 4], [1, 16], [16, 4]]),
                                               op=ALU.add, axis=AX), reads=[("ps", bs_)], writes=["gsv"])
        ts("dve", gsv, gsv, 1.0 / 512, EPS, ALU.mult, ALU.add, ["gsv"], ["gsv"])
        act(gsv, gsv, AF.Sqrt, ["gsv"], ["gsv"])
        S.add("dve", lambda g: g.reciprocal(gsv, gsv), reads=["gsv"], writes=["gsv"])
        tt("dve", fap(yg[:, 0, 0:1], [[64, 4], [16, 4], [1, 16]]), fap(yg[:, 0, 0:1], [[64, 4], [16, 4], [1, 16]]),
           fap(gsv[:, 0:1], [[16, 4], [0, 4], [1, 16]]), ALU.mult, ["yg", "gsv"], ["yg"])
        tt("dve", ynTs, yg, fap(nw[:, 0:1], [[1, 16], [0, 16]]), ALU.mult, ["yg", "nw"], ["ynTs"])
        s_out_proj(w_ssm_out, 16, ynTs, ["ynTs"], C_G + 1024, "last")
        for cbk in range(2):
            slab, wk = wload(w_merge_o, 0, 8, cbk * 512, 512)
            b = nb()
            for k in range(8):
                mm(PS[b][0:P, :], mTs[:, k, :], slab[:, k, :], k == 0, k == 7, [wk, "mTs"], ("ps", b))
            tt("dve", X[0:P, 0, cbk * 512:(cbk + 1) * 512], X[0:P, 0, cbk * 512:(cbk + 1) * 512], PS[b][0:P, :], ALU.add,
               [("X", 0), ("ps", b)], [("X", 0)])
        norm_transpose(X, P, 1, 1, G1, "X", Tn)
        for fb in range(22):
            if fb % 4 == 0:
                ncb = min(4, 22 - fb) * 128
                slg4, wkg = wload(w_gate, 0, 8, fb * 128, ncb)
                slu4, wku = wload(w_up, 0, 8, fb * 128, ncb)
            fo = (fb % 4) * 128
            b1 = nb()
            for k in range(8):
                mm(PS[b1][:, 0:Tn], slg4[:, k, fo:fo + 128], G1[:, k, 0:Tn], k == 0, k == 7, [wkg] + xk, ("ps", b1))
            act(tA, PS[b1][:, 0:Tn], AF.Silu, [("ps", b1)], ["tA"])
            b2 = nb()
            for k in range(8):
                mm(PS[b2][:, 0:Tn], slu4[:, k, fo:fo + 128], G1[:, k, 0:Tn], k == 0, k == 7, [wku] + xk, ("ps", b2))
            tt("dve", hTs[:, fb, :], tA, PS[b2][:, 0:Tn], ALU.mult, ["tA", ("ps", b2)], ["hTs"])
        for cbk in range(2):
            b = nb()
            for half in range(2):
                slab, wk = wload(w_down, half * 11 * 128, 11, cbk * 512, 512)
                for k in range(11):
                    kk = half * 11 + k
                    mm(PS[b][0:P, :], hTs[:, kk, :], slab[:, k, :], kk == 0, kk == 21, [wk, "hTs"], ("ps", b))
            tt("dve", X[0:P, 0, cbk * 512:(cbk + 1) * 512], X[0:P, 0, cbk * 512:(cbk + 1) * 512], PS[b][0:P, :], ALU.add,
               [("X", 0), ("ps", b)], [("X", 0)])
        dma_in(nrm[:], nrm_bc[:, 2, :], "nrm")
        act(junk[0:P, :], X[0:P, 0, :], AF.Square, [("X", 0)], ["junk", "stat0"], accum=stat[0:P, 0:1])
        rstd_from_ssq(0, 1, D, P, "stat0")
        stt(ostg[0][0:P, :], X[0:P, 0, :], stat[0:P, 1:2], nrm[0:P, :], ALU.mult, ALU.mult, [("X", 0), "stat0", "nrm"], [("ostg", 0)])
        dma_out(y_sample[:, :], ostg[0][0:P, :], [("ostg", 0)], slot=("ostg", 0))

    if with_sample:
        sample()
    S.emit(st)
    st.close()
    return nc


_CACHE = {}


def _consts():
    c = np.zeros((128, 5, 128), np.float32)
    c[:, 0, :] = np.eye(128, dtype=np.float32)
    j = np.arange(128)[:, None]
    i = np.arange(128)[None, :]
    c[:, 1, :] = (j <= i).astype(np.float32)
    c[:, 2, :] = np.where(i < j, -30000.0, 0.0).astype(np.float32)
    c[:, 3, :] = 1.0
    return c


def kernel(**inp):
    f = lambda a: np.ascontiguousarray(np.asarray(a, dtype=np.float32))
    if "nc" not in _CACHE:
        _CACHE["nc"] = build()
    nc = _CACHE["nc"]
    bc = lambda v: np.ascontiguousarray(np.broadcast_to(f(v).reshape(1, -1), (128, f(v).size)))
    nrm_bc = np.stack([bc(inp["norm_mix_w"][0]), bc(inp["norm_ffn_w"][0]), bc(inp["norm_final_w"]), bc(inp["norm_mem_w"][0])], 1)
    scw_c = np.ascontiguousarray(f(inp["sc_conv_w"][0]).reshape(3, 8, 128).transpose(2, 1, 0))
    cw_c = np.ascontiguousarray(f(inp["ssm_conv_w"][0]).reshape(4, 24, 128).transpose(2, 1, 0))
    cb_c = np.ascontiguousarray(f(inp["ssm_conv_b"][0]).reshape(24, 128).T)
    nw_c = np.ascontiguousarray(f(inp["ssm_norm_w"][0]).reshape(16, 128).T)
    hp_bc = np.stack([bc(inp["ssm_dt_bias"][0]), bc(inp["ssm_a_log"][0]), bc(inp["ssm_d"][0])], 1)
    shared = {
        "w_in": f(inp["w_in"][0]), "w_sc_out": f(inp["w_sc_out"][0]), "w_ssm_out": f(inp["w_ssm_out"][0]),
        "w_mem_k": f(inp["w_mem_k"][0]), "w_mem_v": f(inp["w_mem_v"][0]), "w_attn_o": f(inp["w_attn_o"][0]),
        "w_merge_o": f(inp["w_merge_o"][0]), "w_ffn_gate": f(inp["w_ffn_gate"][0]), "w_ffn_up": f(inp["w_ffn_up"][0]),
        "w_ffn_down": f(inp["w_ffn_down"][0]),
        "nrm_bc": np.ascontiguousarray(nrm_bc), "scw_c": scw_c, "cw_c": cw_c, "cb_c": cb_c, "nw_c": nw_c,
        "hp_bc": np.ascontiguousarray(hp_bc), "cst": _consts(),
        "Dc_in": np.ascontiguousarray(np.repeat(f(inp["ssm_d"][0]), 64).reshape(16, 128).T),
    }
    import os
    NCORE = int(os.environ.get('NCORE', '8'))
    in_maps = []
    for c in range(NCORE):
        m = dict(shared)
        m["x_prompt"] = f(inp["x_prompt"][c])
        m["mem_prompt"] = f(inp["mem_prompt"][c])
        sl = slice(c * NSB, (c + 1) * NSB)
        m["x_sample"] = f(inp["x_sample"][sl, 0])
        m["cache_k"] = f(inp["cache_mem_k"][0, sl]).reshape(NSB, 256, D)
        m["cache_v"] = f(inp["cache_mem_v"][0, sl]).reshape(NSB, 256, D)
        m["st_conv"] = f(inp["state_conv"][0, sl])
        m["st_sconv"] = f(inp["state_ssm_conv"][0, sl])
        m["st_ssm"] = f(inp["state_ssm"][0, sl]).reshape(NSB, 2048, 128)
        in_maps.append(m)
    res = run_bass_kernel_spmd(nc, in_maps, core_ids=list(range(NCORE)))
    R = list(res.results) + [res.results[0]] * (8 - NCORE)
    y_prompt = np.stack([R[c]["y_prompt"] for c in range(8)], 0)
    pmk = np.stack([R[c]["pmk"].reshape(256, 4, 256) for c in range(8)], 0)[None]
    pmv = np.stack([R[c]["pmv"].reshape(256, 4, 256) for c in range(8)], 0)[None]
    pconv = np.stack([R[c]["pconv"].transpose(2, 1, 0).reshape(2, 1024) for c in range(8)], 0)[None]
    pssmc = np.stack([R[c]["pssmc"].transpose(2, 1, 0).reshape(3, 3072) for c in range(8)], 0)[None]
    pssm = np.stack([R[c]["pssm"].reshape(32, 64, 128) for c in range(8)], 0)[None]
    y_sample = np.concatenate([R[c]["y_sample"] for c in range(8)], 0)[:, None, :]
    sconv = np.concatenate([R[c]["sconv"] for c in range(8)], 0)[None]
    sssmc = np.concatenate([R[c]["sssmc"] for c in range(8)], 0)[None]
    sssm = np.concatenate([R[c]["sssm"].reshape(NSB, 32, 64, 128) for c in range(8)], 0)[None]
    return (y_prompt, y_sample, pmk, pmv, pconv, pssmc, pssm, sconv, sssmc, sssm)
```

```python
import contextlib
import numpy as np
import concourse.bass as bass
import concourse.mybir as mybir
from concourse.bass_utils import run_bass_kernel_spmd

F32 = mybir.dt.float32
BF16 = mybir.dt.bfloat16
AF = mybir.ActivationFunctionType
ALU = mybir.AluOpType
AX = mybir.AxisListType.X

D = 1024
SEQ = 2048
T = 512
NT = SEQ // T
NSB = 16
FF = 2816
EPS = 1e-6
C_SCB, C_SCC, C_SCX, C_Z, C_XBC, C_DT, C_Q, C_G = 0, 1024, 2048, 3072, 5120, 8192, 8224, 9248
INC = 12320
ENGS = ["pe", "act", "dve", "pool", "sp"]
WINDOW = 6


class Op:
    __slots__ = ("idx", "eng", "fn", "deps", "signal", "sem", "val", "pos", "slot", "is_dma")


class Sched:
    def __init__(self, nc):
        self.nc = nc
        self.ops = []
        self.by_eng = {e: [] for e in ENGS}
        self.lastw = {}
        self.readers = {}
        self.slot_last = {}
        self.slot_count = {}
        self.auto_reads = []

    def _chan(self, o):
        return ("dma", o.slot) if o.is_dma else o.eng

    def add(self, eng, fn, reads=(), writes=(), slot=None):
        op = Op()
        op.idx = len(self.ops)
        op.eng = eng
        op.fn = fn
        op.slot = slot
        op.is_dma = slot is not None
        op.signal = False
        op.sem = None
        op.val = 0
        op.pos = len(self.by_eng[eng])
        deps = {}
        if self.auto_reads:
            reads = list(reads) + self.auto_reads
        extra = [("psx", k[1]) for k in reads if isinstance(k, tuple) and k[0] == "ps"]
        if extra:
            writes = list(writes) + extra

        def dep(o):
            if o is None:
                return
            ch = self._chan(o)
            if ch not in deps or deps[ch].idx < o.idx:
                deps[ch] = o

        for k in reads:
            dep(self.lastw.get(k))
        for k in writes:
            dep(self.lastw.get(k))
            for o in self.readers.get(k, {}).values():
                dep(o)
        if op.is_dma:
            dep(self.slot_last.get(slot))
            self.slot_last[slot] = op
            self.slot_count[slot] = self.slot_count.get(slot, 0) + 1
            op.val = 16 * self.slot_count[slot]
        final = []
        for ch, o in deps.items():
            if (not o.is_dma) and o.eng == eng and not op.is_dma:
                if eng == "pe":
                    continue
                if op.pos - o.pos > WINDOW:
                    continue
            o.signal = True
            final.append(o)
        op.deps = final
        ch = self._chan(op)
        for k in reads:
            self.readers.setdefault(k, {})[ch] = op
        for k in writes:
            self.lastw[k] = op
            self.readers[k] = {}
        self.ops.append(op)
        self.by_eng[eng].append(op)
        return op

    def emit(self, stack):
        nc = self.nc
        EPOCH = 2000
        ssem = {}
        for s in self.slot_count:
            ssem[s] = stack.enter_context(nc.semaphore("d_%d" % len(ssem)))
        for e in ENGS:
            c = 0
            cur = None
            for op in self.by_eng[e]:
                if op.is_dma:
                    op.sem = ssem[op.slot]
                elif op.signal:
                    if c % EPOCH == 0:
                        cur = stack.enter_context(nc.semaphore("c_%s_%d" % (e, c // EPOCH)))
                    op.sem = cur
                    op.val = c % EPOCH + 1
                    c += 1
        block = stack.enter_context(nc.Block())

        def run(e, eng):
            water = {}
            for op in self.by_eng[e]:
                need = {}
                for o in op.deps:
                    k = o.sem.num
                    if water.get(k, 0) >= o.val:
                        continue
                    if k not in need or need[k][1] < o.val:
                        need[k] = (o.sem, o.val)
                for k, (s, v) in need.items():
                    eng.wait_ge(s, v)
                    water[k] = v
                ins = op.fn(eng)
                if op.is_dma:
                    ins.then_inc(op.sem, 16)
                elif op.signal:
                    ins.then_inc(op.sem, 1)
            if e == "sp":
                for s, cnt in self.slot_count.items():
                    eng.wait_ge(ssem[s], 16 * cnt)

        @block.tensor
        def _(eng):
            run("pe", eng)

        @block.scalar
        def _(eng):
            run("act", eng)

        @block.vector
        def _(eng):
            run("dve", eng)

        @block.gpsimd
        def _(eng):
            run("pool", eng)

        @block.sync
        def _(eng):
            run("sp", eng)


def fap(ap, dims, off=0):
    return bass.AP(tensor=ap.tensor, offset=ap.offset + off, ap=[list(ap.ap[0])] + [list(d) for d in dims])


def build(with_sample=True, dbg=None, ntiles=NT, branches='ACB', stop=0):
    nc = bass.Bass("TRN2", target_bir_lowering=False)
    dram = {}

    def din(name, shape):
        dram[name] = nc.dram_tensor(name, list(shape), F32, kind="ExternalInput").ap()
        return dram[name]

    def dout(name, shape):
        dram[name] = nc.dram_tensor(name, list(shape), F32, kind="ExternalOutput").ap()
        return dram[name]

    xin = din("x_prompt", [SEQ, D])
    memin = din("mem_prompt", [256, D])
    w_in = din("w_in", [D, INC])
    w_sc_out = din("w_sc_out", [D, D])
    w_ssm_out = din("w_ssm_out", [2048, D])
    w_mem_k = din("w_mem_k", [D, D])
    w_mem_v = din("w_mem_v", [D, D])
    w_attn_o = din("w_attn_o", [D, D])
    w_merge_o = din("w_merge_o", [D, D])
    w_gate = din("w_ffn_gate", [D, FF])
    w_up = din("w_ffn_up", [D, FF])
    w_down = din("w_ffn_down", [FF, D])
    nrm_bc = din("nrm_bc", [128, 4, D])
    scw_c = din("scw_c", [128, 8, 3])
    cw_c = din("cw_c", [128, 24, 4])
    cb_c = din("cb_c", [128, 24])
    nw_c = din("nw_c", [128, 16])
    hp_bc = din("hp_bc", [128, 3, 32])
    cst = din("cst", [128, 5, 128])

    xs_in = din("x_sample", [NSB, D])
    ck_in = din("cache_k", [NSB, 256, D])
    cv_in = din("cache_v", [NSB, 256, D])
    st_conv = din("st_conv", [NSB, 2, D])
    st_sconv = din("st_sconv", [NSB, 3, 3072])
    st_ssm = din("st_ssm", [NSB, 2048, 128])
    Dc_in = din("Dc_in", [128, 16])
    y_sample = dout("y_sample", [NSB, D])
    sconv = dout("sconv", [NSB, 2, D])
    sssmc = dout("sssmc", [NSB, 3, 3072])
    sssm = dout("sssm", [NSB, 2048, 128])
    y_prompt = dout("y_prompt", [SEQ, D])
    pmk = dout("pmk", [256, D])
    pmv = dout("pmv", [256, D])
    pconv = dout("pconv", [128, 8, 2])
    pssmc = dout("pssmc", [128, 24, 3])
    pssm = dout("pssm", [2048, 128])
    if dbg:
        dbg_out = {k: dout("dbg_" + k, shp) for k, shp in dbg.items()}

    st = contextlib.ExitStack()
    E = st.enter_context

    def sb(name, shape, dt=F32):
        return E(nc.sbuf_tensor(name, list(shape), dt))

    S = Sched(nc)
    stop_mode = stop
    X = sb("X", [128, 4, D])
    MG = sb("MG", [128, 8, T])
    G1 = sb("G1", [128, 8, T], BF16)
    G2 = sb("G2", [128, 8, T], BF16)
    G3 = sb("G3", [128, 8, T], BF16)
    HB = sb("HB", [128, 16384], BF16)
    NPAGE = 16
    PAGE = 1024
    WRb = sb("WRb", [128, NPAGE * PAGE], BF16)
    WR = [WRb]
    page_owner = {}
    nrm = sb("nrm", [128, D])
    scw = sb("scw", [128, 8, 3])
    cw = sb("cw", [128, 24, 4])
    cb = sb("cb", [128, 24])
    nw = sb("nw", [128, 16])
    hp = sb("hp", [128, 3, 32])
    cstf = sb("cstf", [128, 5, 128])
    cstb = sb("cstb", [128, 5, 128], BF16)
    abc = sb("abc", [128, 32])
    xnb = [sb("xnb0", [128, D], BF16)] * 2
    junk = sb("junk", [128, D], BF16)
    stat = sb("stat", [128, 64])
    histA = sb("histA", [128, 8, 2], BF16)
    histB = sb("histB", [128, 24, 3], BF16)
    pcv = sb("pcv", [128, 8, 2])
    pscv = sb("pscv", [128, 24, 3])
    ub = [sb("ub%d" % i, [128, 2 + T], BF16) for i in range(2)]
    xb = [sb("xb%d" % i, [128, 3 + T], BF16) for i in range(2)]
    dg = [sb("dg%d" % i, [128, 4, 128], BF16) for i in range(2)]
    sA = [sb("sA%d" % i, [128, T]) for i in range(2)]
    sB = [sb("sB%d" % i, [128, T]) for i in range(2)]
    sC = [sb("sC%d" % i, [128, T], BF16) for i in range(2)]
    KT = sb("KT", [128, 8, 256], BF16)
    Vb = sb("Vb", [128, 2, D], BF16)
    ostg = [sb("ostg%d" % i, [128, D]) for i in range(2)]
    BT = sb("BT", [128, 4, T], BF16)
    CT = sb("CT", [128, 4, T], BF16)
    Btok = sb("Btok", [128, 4, 4, 128], BF16)
    Sf = sb("Sf", [128, 2048])
    Sbf = sb("Sbf", [128, 2048], BF16)
    dtt = sb("dtt", [128, 4, 32])
    lndt = sb("lndt", [128, 4, 32])
    dtab = sb("dtab", [128, 4, 32], BF16)
    acs = sb("acs", [128, 64])
    eac = sb("eac", [128, 64])
    dsd = sb("dsd", [128, 32])
    nbias = sb("nbias", [128, 32])
    nbh4 = sb("nbh4", [128, 4, 32], BF16)
    nbl4 = sb("nbl4", [128, 4, 32], BF16)
    xdd = sb("xdd", [128, 2048], BF16)
    xddg = [xdd[:, 0:512], xdd[:, 512:1024]]
    CBs = sb("CBs", [128, 4, 128], BF16)
    Lt = [sb("Lt%d" % i, [128, 4, 128], BF16) for i in range(2)]
    MT = [sb("MT%d" % i, [128, 4, 128], BF16) for i in range(2)]
    Lt.append(xdd[:, 1024:1536].rearrange("p (a b) -> p a b", a=4))
    MT.append(xdd[:, 1536:2048].rearrange("p (a b) -> p a b", a=4))
    yb = [sb("yb%d" % i, [128, 512]) for i in range(2)]
    ynb = [sb("ynb%d" % i, [128, 512], BF16) for i in range(2)]
    sm = sb("sm", [128, 16])
    PS = [E(nc.psum_tensor("ps%d" % i, [128, 512], F32)) for i in range(8)]

    ident_b = cstb[:, 0, :]
    tri_b = cstb[:, 1, :]
    mneg_b = cstb[:, 2, :]
    ones_b = cstb[:, 3, :]
    ident_f = cstf[:, 0, :]

    PT = HB[:, 0:4096].rearrange("p (h m t) -> p h m t", h=4, m=2)
    zs_v = HB[:, 0:8192].rearrange("p (j c) -> p j c", j=4)
    xs_v = HB[:, 8192:16384].rearrange("p (j c) -> p j c", j=4)
    mT_v = HB[:, 0:4096].rearrange("p (k t) -> p k t", k=8)
    hT_v = HB[:, 4096:4096 + 22 * T].rearrange("p (k t) -> p k t", k=22)

    cnt = {"ps": 0, "w": 0, "misc": 0, "rr": 0}

    def nb():
        i = cnt["ps"] % 8
        cnt["ps"] += 1
        return i

    def mslot():
        cnt["misc"] += 1
        return ("m", cnt["misc"] % 12)

    def ev_eng():
        cnt["rr"] += 1
        return "act" if cnt["rr"] % 2 else "dve"

    def dma_in(dst, src, wkey):
        S.add("sp", lambda g: g.dma_start(out=dst, in_=src), writes=[wkey], slot=mslot())

    def dma_out(dst, src, rkeys, slot=None):
        S.add("sp", lambda g: g.dma_start(out=dst, in_=src), reads=rkeys, slot=slot or mslot())

    def wload(Wap, r0, kch, c0, ncols):
        n = kch * ncols
        npg = (n + PAGE - 1) // PAGE
        start = cnt["w"]
        if start + npg > NPAGE:
            start = 0
        cnt["w"] = (start + npg) % NPAGE
        cnt["wser"] = cnt.get("wser", 0) + 1
        ser = cnt["wser"]
        prev = set()
        for p in range(start, start + npg):
            if p in page_owner:
                prev.add(page_owner[p])
            page_owner[p] = ser
        dst = WRb[:, start * PAGE:start * PAGE + n].rearrange("p (k c) -> p k c", k=kch)
        src = Wap[r0:r0 + kch * 128, c0:c0 + ncols].rearrange("(k p) c -> p k c", p=128)
        S.add("pool", lambda g: g.dma_start(out=dst, in_=src), writes=[("wl", ser)] + [("wl", q) for q in prev], slot=("w", start))
        return dst, ("wl", ser)

    def mm(out, lhsT, rhs, start, stop, reads, pskey):
        if stop_mode == 5:
            return
        S.add("pe", lambda g: g.matmul(out, lhsT, rhs, start=start, stop=stop), reads=reads, writes=[pskey])

    def tr(out, in_, idn, reads, pskey):
        S.add("pe", lambda g: g.transpose(out, in_, idn), reads=reads, writes=[pskey])

    def act(out, in_, func, reads, writes, bias=None, scale=1.0, accum=None):
        kw = {}
        if bias is not None:
            kw["bias"] = bias
        if accum is not None:
            kw["accum_out"] = accum
        S.add("act", lambda g: g.activation(out=out, in_=in_, func=func, scale=scale, **kw), reads=reads, writes=writes)

    def tt(eng, out, in0, in1, op, reads, writes):
        S.add(eng, lambda g: g.tensor_tensor(out=out, in0=in0, in1=in1, op=op), reads=reads, writes=writes)

    def ts(eng, out, in0, s1, s2, op0, op1, reads, writes):
        if op1 is None:
            S.add(eng, lambda g: g.tensor_scalar(out=out, in0=in0, scalar1=s1, scalar2=None, op0=op0), reads=reads, writes=writes)
        else:
            S.add(eng, lambda g: g.tensor_scalar(out=out, in0=in0, scalar1=s1, scalar2=s2, op0=op0, op1=op1), reads=reads, writes=writes)

    def cp(eng, out, in_, reads, writes):
        if eng == "act":
            act(out, in_, AF.Copy, reads, writes)
        else:
            S.add(eng, lambda g: g.tensor_copy(out, in_), reads=reads, writes=writes)

    def stt(out, in0, scalar, in1, op0, op1, reads, writes):
        S.add("dve", lambda g: g.scalar_tensor_tensor(out=out, in0=in0, scalar=scalar, in1=in1, op0=op0, op1=op1), reads=reads, writes=writes)

    def memset(eng, ap, val, wkeys):
        S.add(eng, lambda g: g.memset(ap, val), writes=wkeys)

    def rstd_from_ssq(col_ssq, col_out, n, P, key):
        a = stat[0:P, col_ssq:col_ssq + 1]
        o = stat[0:P, col_out:col_out + 1]
        ts("dve", o, a, 1.0 / n, EPS, ALU.mult, ALU.add, [key], [key])
        act(o, o, AF.Ln, [key], [key])
        act(o, o, AF.Exp, [key], [key], scale=-0.5)

    dma_in(scw[:], scw_c[:, :, :], "scw")
    dma_in(cw[:], cw_c[:, :, :], "cw")
    dma_in(cb[:], cb_c[:, :], "cb")
    dma_in(nw[:], nw_c[:, :], "nw")
    dma_in(hp[:], hp_bc[:, :, :], "hp")
    dma_in(cstf[:], cst[:, :, :], "cstf")
    cp("dve", cstb[:], cstf[:], ["cstf"], ["cstb"])
    act(abc[:], hp[:, 1, :], AF.Exp, ["hp"], ["abc"])
    ts("dve", abc[:], abc[:], -1.0, None, ALU.mult, None, ["abc"], ["abc"])
    memset("dve", histA[:], 0.0, ["histA"])
    memset("dve", histB[:], 0.0, ["histB"])
    memset("dve", Sf[:], 0.0, ["Sf"])
    memset("dve", Sbf[:], 0.0, ["Sbf"])

    def finish():
        S.emit(st)
        st.close()
        return nc
    if stop == 1:
        return finish()
    def norm_transpose(src3, P, nsub, wrow, dstT, tag, Tn):
        dma_in(nrm[:], nrm_bc[:, wrow, :], "nrm")
        xn2 = [(xnb[0], ("xnb", 0)), (ostg[1][:].bitcast(BF16)[:, 0:D], ("ostg", 1))]
        for j in range(nsub):
            act(junk[0:P, :], src3[0:P, j, :], AF.Square, [(tag, j)], ["junk", ("nst", j)], accum=stat[0:P, 48 + j:49 + j])
        for j in range(nsub):
            ts("dve", stat[0:P, 52 + j:53 + j], stat[0:P, 48 + j:49 + j], 1.0 / D, EPS, ALU.mult, ALU.add, [("nst", j)], [("nst", j)])
        for j in range(nsub):
            act(stat[0:P, 52 + j:53 + j], stat[0:P, 52 + j:53 + j], AF.Ln, [("nst", j)], [("nst", j)])
        for j in range(nsub):
            act(stat[0:P, 52 + j:53 + j], stat[0:P, 52 + j:53 + j], AF.Exp, [("nst", j)], [("nst", j)], scale=-0.5)
        for j in range(nsub):
            xn, xkey = xn2[j % 2]
            stt(xn[0:P, :], src3[0:P, j, :], stat[0:P, 52 + j:53 + j], nrm[0:P, :], ALU.mult, ALU.mult,
                [(tag, j), ("nst", j), "nrm"], [xkey])
            b = nb()
            pT = PS[b][:].bitcast(BF16)
            for k in range(8):
                tr(pT[:, k * P:(k + 1) * P] if P == 128 else pT[:, k * 128:k * 128 + P], xn[0:P, k * 128:(k + 1) * 128], ident_b[0:P, 0:P],
                   [xkey, "cstb"], ("ps", b))
            if P == 128:
                cp(ev_eng(), dstT[:, :, j * 128:(j + 1) * 128], pT[:, 0:1024].rearrange("p (k t) -> p k t", k=8),
                   [("ps", b)], [("G1", "all")])
            else:
                cp(ev_eng(), dstT[:, :, 0:P], fap(pT[:, 0:1], [[128, 8], [1, P]]), [("ps", b)], [("G1", "all")])

    def proj_fm(Wap, r0, kch, c0, nblk, srcT, Tn, src_keys, consume, blk_per_slab=4):
        blk = 0
        while blk < nblk:
            nbk = min(blk_per_slab, nblk - blk)
            slab, wk = wload(Wap, r0, kch, c0 + blk * 128, nbk * 128)
            for bb in range(nbk):
                b = nb()
                for k in range(kch):
                    mm(PS[b][:, 0:Tn], slab[:, k, bb * 128:(bb + 1) * 128], srcT[:, k, 0:Tn], k == 0, k == kch - 1,
                       [wk] + src_keys, ("ps", b))
                consume(blk + bb, b)
            blk += nbk

    MEMX = X
    for j in range(2):
        dma_in(X[:, j, :], memin[j * 128:(j + 1) * 128, :], ("X", j))
    norm_transpose(X, 128, 2, 3, G1, "X", 256)
    if stop == 2:
        return finish()
    if stop == 6:
        cp("act", ostg[0][:, 0:512], PS[5][:, :], [("ps", 5)], [("ostg", 0)])
        cp("dve", Vb[:, 0, 0:512], PS[5][:, :], [("ps", 5)], ["Vb"])
        return finish()
    G1k = [("G1", "all")]
    import os
    PARTS = os.environ.get("KV_PARTS", "WMCOK")
    KVW = os.environ.get("KV_WHICH", "kv")
    KVN = int(os.environ.get("KV_NCBK", "2"))
    KVM = int(os.environ.get("KV_NMC", "2"))
    for which, Wm, outd in (("k", w_mem_k, pmk), ("v", w_mem_v, pmv)):
        if which not in KVW:
            continue
        for cbk in range(KVN):
            if "W" in PARTS:
                slab, wk = wload(Wm, 0, 8, cbk * 512, 512)
            else:
                slab, wk = WRb[:, 0:4096].rearrange("p (k c) -> p k c", k=8), ("wl", 0)
            for mc in range(KVM):
                b = nb()
                if "M" in PARTS:
                    for k in range(8):
                        mm(PS[b][:, :], G1[:, k, mc * 128:(mc + 1) * 128], slab[:, k, :], k == 0, k == 7, [wk] + G1k, ("ps", b))
                so = ostg[cnt["rr"] % 2]
                sk = ("ostg", cnt["rr"] % 2)
                cnt["rr"] += 1
                if "C" in PARTS:
                    if os.environ.get("KV_VB", "dve") != "dve_only":
                        cp("act", so[:, 0:512], PS[b][:, :], [("ps", b)], [sk])
                    if which == "v":
                        vbm = os.environ.get("KV_VB", "dve")
                        if vbm in ("dve", "dve_only"):
                            cp("dve", Vb[:, mc, cbk * 512:(cbk + 1) * 512], PS[b][:, :], [("ps", b)], ["Vb"])
                        elif vbm == "act":
                            cp("act", Vb[:, mc, cbk * 512:(cbk + 1) * 512], PS[b][:, :], [("ps", b)], ["Vb"])
                        elif vbm == "dve_sb":
                            cp("dve", Vb[:, mc, cbk * 512:(cbk + 1) * 512], so[:, 0:512], [sk], ["Vb"])
                        elif vbm == "dve_half":
                            cp("dve", Vb[:, mc, cbk * 512:cbk * 512 + 256], PS[b][:, 0:256], [("ps", b)], ["Vb"])
                if "O" in PARTS:
                    dma_out(outd[mc * 128:(mc + 1) * 128, cbk * 512:(cbk + 1) * 512], so[:, 0:512], [sk], slot=sk)
            if which == "k" and "K" in PARTS:
                for bb in range(4):
                    b = nb()
                    if "M" in PARTS:
                        for k in range(8):
                            mm(PS[b][:, 0:256], slab[:, k, bb * 128:(bb + 1) * 128], G1[:, k, 0:256], k == 0, k == 7, [wk] + G1k, ("ps", b))
                    cp(ev_eng(), KT[:, cbk * 4 + bb, :], PS[b][:, 0:256], [("ps", b)], ["KT"])
    if stop == 7:
        return finish()

    xpref = {"done": False}

    def tile(ti, last):
        Tn = T
        P = 128
        NS = 4
        t0 = ti * T
        xk = [("G1", "all")]
        if not xpref["done"]:
            for j in range(NS):
                dma_in(X[:, j, :], xin[t0 + j * 128:t0 + (j + 1) * 128, :], ("X", j))
        xpref["done"] = False
        norm_transpose(X, 128, NS, 0, G1, "X", Tn)

        def gated_merge(bname, mode):
            pass

        def out_proj_and_merge(Wo, kch, srcT, src_keys, gate_c0, mode, bps):
            ybank = {}

            def cons_y(blk, b):
                ybank[blk] = b
                slab, wk = wload(w_in, 0, 8, gate_c0 + blk * 128, 128)
                bg = nb()
                for k in range(8):
                    mm(PS[bg][:, 0:Tn], slab[:, k, 0:128], G1[:, k, 0:Tn], k == 0, k == 7, [wk] + xk, ("ps", bg))
                i = blk % 2
                act(sA[i][:, 0:Tn], PS[bg][:, 0:Tn], AF.Sigmoid, [("ps", bg)], [("sA", i)])
                if mode == "first":
                    tt("dve", MG[:, blk, 0:Tn], sA[i][:, 0:Tn], PS[b][:, 0:Tn], ALU.mult, [("sA", i), ("ps", b)], [("MG", blk)])
                elif mode == "mid":
                    tt("dve", sB[i][:, 0:Tn], sA[i][:, 0:Tn], PS[b][:, 0:Tn], ALU.mult, [("sA", i), ("ps", b)], [("sB", i)])
                    tt("dve", MG[:, blk, 0:Tn], MG[:, blk, 0:Tn], sB[i][:, 0:Tn], ALU.add, [("sB", i), ("MG", blk)], [("MG", blk)])
                else:
                    tt("dve", sB[i][:, 0:Tn], sA[i][:, 0:Tn], PS[b][:, 0:Tn], ALU.mult, [("sA", i), ("ps", b)], [("sB", i)])
                    tt("dve", mT_v[:, blk, 0:Tn], MG[:, blk, 0:Tn], sB[i][:, 0:Tn], ALU.add,
                       [("sB", i), ("MG", blk), "HBphase2"], [("mT", blk)])

            proj_fm(Wo, 0, kch, 0, 8, srcT, Tn, src_keys, cons_y, blk_per_slab=bps)

        def branchA():
            pend = [None]
            for c in range(8):
                i = c % 2
                slc, wkc = wload(w_in, 0, 8, C_SCC + c * 128, 128)
                slx, wkx = wload(w_in, 0, 8, C_SCX + c * 128, 128)
                slb, wkb = wload(w_in, 0, 8, C_SCB + c * 128, 128)
                b1 = nb()
                for k in range(8):
                    mm(PS[b1][:, 0:Tn], slc[:, k, :], G1[:, k, 0:Tn], k == 0, k == 7, [wkc] + xk, ("ps", b1))
                act(sA[i][:, 0:Tn], PS[b1][:, 0:Tn], AF.Copy, [("ps", b1)], [("sA", i)])
                b2 = nb()
                for k in range(8):
                    mm(PS[b2][:, 0:Tn], slx[:, k, :], G1[:, k, 0:Tn], k == 0, k == 7, [wkx] + xk, ("ps", b2))
                b4 = nb()
                for k in range(8):
                    mm(PS[b4][:, 0:Tn], slb[:, k, :], G1[:, k, 0:Tn], k == 0, k == 7, [wkb] + xk, ("ps", b4))
                u = ub[i]
                cp("dve", u[:, 0:2], histA[:, c, :], ["histA"], [("ub", i)])
                tt("dve", u[:, 2:2 + Tn], sA[i][:, 0:Tn], PS[b2][:, 0:Tn], ALU.mult, [("sA", i), ("ps", b2)], [("ub", i)])
                cp("dve", histA[:, c, :], u[:, Tn:Tn + 2], [("ub", i)], ["histA"])
                if last:
                    tt("dve", pcv[:, c, :], sA[i][:, Tn - 2:Tn], PS[b2][:, Tn - 2:Tn], ALU.mult, [("sA", i), ("ps", b2)], ["pcv"])
                for k3 in range(3):
                    ts("dve", dg[i][:, k3, :], ident_b, scw[:, c, k3:k3 + 1], None, ALU.mult, None, ["cstb", "scw"], [("dg", i)])
                act(sB[i][:, 0:Tn], PS[b4][:, 0:Tn], AF.Copy, [("ps", b4)], [("sB", i)])

                def part2(c=c, i=i, u=u):
                    b3 = nb()
                    for k3 in range(3):
                        mm(PS[b3][:, 0:Tn], dg[i][:, k3, :], u[:, k3:k3 + Tn], k3 == 0, k3 == 2, [("dg", i), ("ub", i)], ("ps", b3))
                    tt("dve", G2[:, c, 0:Tn], sB[i][:, 0:Tn], PS[b3][:, 0:Tn], ALU.mult, [("sB", i), ("ps", b3)], [("G2", c)])
                if pend[0]:
                    pend[0]()
                pend[0] = part2
            pend[0]()
            out_proj_and_merge(w_sc_out, 8, G2, [("G2", c) for c in range(8)], C_G, "first", 4)

        def branchC():
            def cons_q(blk, b):
                act(G2[:, blk, 0:Tn], PS[b][:, 0:Tn], AF.Copy, [("ps", b)], [("G2", blk)], scale=1.0 / 16.0)
            proj_fm(w_in, 0, 8, C_Q, 8, G1, Tn, xk, cons_q)
            pbufs = [(sC[0][:, 0:512].rearrange("p (h m) -> p h m", h=2), ("sC", 0)), (sC[1][:, 0:512].rearrange("p (h m) -> p h m", h=2), ("sC", 1))]
            pendc = [None]
            itc = 0
            for j in range(NS):
                for hpair in range(2):
                    pb, pbk = pbufs[itc % 2]
                    so = (itc % 2) * 8
                    itc += 1
                    b = nb()
                    for hh in range(2):
                        h = hpair * 2 + hh
                        for dc in range(2):
                            mm(PS[b][:, hh * 256:(hh + 1) * 256], G2[:, 2 * h + dc, j * 128:(j + 1) * 128], KT[:, 2 * h + dc, :],
                               dc == 0, dc == 1, [("G2", 2 * h + dc), "KT"], ("ps", b))
                    smk = ("sm", so)
                    S.add("dve", lambda g, b=b, so=so: g.tensor_reduce(out=sm[:, so:so + 2], in_=PS[b][:, :].rearrange("p (h m) -> p h m", h=2),
                                                                      op=ALU.max, axis=AX), reads=[("ps", b)], writes=[smk])
                    ts("dve", sm[:, so + 2:so + 4], sm[:, so:so + 2], -1.0, None, ALU.mult, None, [smk], [smk])
                    for hh in range(2):
                        act(pb[:, hh, :], PS[b][:, hh * 256:(hh + 1) * 256], AF.Exp, [("ps", b), smk], [pbk, smk],
                            bias=sm[:, so + 2 + hh:so + 3 + hh], accum=sm[:, so + 4 + hh:so + 5 + hh])
                    S.add("dve", lambda g, so=so: g.reciprocal(sm[:, so + 6:so + 8], sm[:, so + 4:so + 6]), reads=[smk], writes=[smk])
                    for hh in range(2):
                        ts("dve", pb[:, hh, :], pb[:, hh, :], sm[:, so + 6 + hh:so + 7 + hh], None, ALU.mult, None, [pbk, smk], [pbk])

                    def trpart(j=j, hpair=hpair, pb=pb, pbk=pbk):
                        bt = nb()
                        pT = PS[bt][:].bitcast(BF16)
                        for hh in range(2):
                            for mc in range(2):
                                tr(pT[:, (hh * 2 + mc) * 128:(hh * 2 + mc + 1) * 128], pb[:, hh, mc * 128:(mc + 1) * 128], ident_b,
                                   [pbk, "cstb"], ("ps", bt))
                        cp(ev_eng(), PT[:, hpair * 2:hpair * 2 + 2, :, j * 128:(j + 1) * 128],
                           pT[:, 0:512].rearrange("p (h m t) -> p h m t", h=2, m=2), [("ps", bt), "HBphase1"], [("PT", j, hpair)])
                    if pendc[0]:
                        pendc[0]()
                    pendc[0] = trpart
            pendc[0]()
            ptk = [("PT", j, hpv) for j in range(NS) for hpv in range(2)]
            for h in range(4):
                for dblk in range(2):
                    b = nb()
                    for mc in range(2):
                        mm(PS[b][:, 0:Tn], Vb[:, mc, h * 256 + dblk * 128:h * 256 + (dblk + 1) * 128], PT[:, h, mc, 0:Tn],
                           mc == 0, mc == 1, ["Vb"] + ptk, ("ps", b))
                    cp(ev_eng(), G3[:, 2 * h + dblk, 0:Tn], PS[b][:, 0:Tn], [("ps", b)], [("G3", 2 * h + dblk)])
            S.add("dve", lambda g: g.memset(stat[:, 22:23], 0.0), writes=ptk + ["HBphase1"])
            out_proj_and_merge(w_attn_o, 8, G3, [("G3", c) for c in range(8)], C_G + 2048, "mid", 4)

        def branchB():
            for cbk in range(4):
                slab, wk = wload(w_in, 0, 8, C_Z + cbk * 512, 512)
                for j in range(NS):
                    b = nb()
                    for k in range(8):
                        mm(PS[b][:, :], G1[:, k, j * 128:(j + 1) * 128], slab[:, k, :], k == 0, k == 7, [wk] + xk, ("ps", b))
                    act(zs_v[:, j, cbk * 512:(cbk + 1) * 512], PS[b][:, :], AF.Silu, [("ps", b), "HBphase1"], [("zs", j, cbk)])
            slab, wk = wload(w_in, 0, 8, C_DT, 32)
            for j in range(NS):
                b = nb()
                for k in range(8):
                    mm(PS[b][:, 0:32], G1[:, k, j * 128:(j + 1) * 128], slab[:, k, :], k == 0, k == 7, [wk] + xk, ("ps", b))
                tt("dve", dtt[:, j, :], PS[b][:, 0:32], hp[:, 0, :], ALU.add, [("ps", b), "hp"], [("dt", j)])
                act(dtt[:, j, :], dtt[:, j, :], AF.Exp, [("dt", j)], [("dt", j)])
                act(dtt[:, j, :], dtt[:, j, :], AF.Ln, [("dt", j)], [("dt", j)], bias=1.0)
                act(lndt[:, j, :], dtt[:, j, :], AF.Ln, [("dt", j)], [("lndt", j)])
                tt("dve", dtab[:, j, :], dtt[:, j, :], abc[:], ALU.mult, [("dt", j), "abc"], [("dtab", j)])
            pq = []

            def cons_xbc(cc, b):
                i = cc % 2
                xbuf = xb[i]
                cp("dve", xbuf[:, 0:3], histB[:, cc, :], ["histB"], [("xb", i)])
                act(xbuf[:, 3:3 + Tn], PS[b][:, 0:Tn], AF.Copy, [("ps", b)], [("xb", i)])
                cp("dve", histB[:, cc, :], xbuf[:, Tn:Tn + 3], [("xb", i)], ["histB"])
                if last:
                    cp("dve", pscv[:, cc, :], PS[b][:, Tn - 3:Tn], [("ps", b)], ["pscv"])
                for k4 in range(4):
                    ts("dve", dg[i][:, k4, :], ident_b, cw[:, cc, k4:k4 + 1], None, ALU.mult, None, ["cstb", "cw"], [("dg", i)])
                st2 = {}

                def part2(cc=cc, i=i, xbuf=xbuf, st2=st2):
                    b2 = nb()
                    for k4 in range(4):
                        mm(PS[b2][:, 0:Tn], dg[i][:, k4, :], xbuf[:, k4:k4 + Tn], k4 == 0, k4 == 3, [("dg", i), ("xb", i)], ("ps", b2))
                    if cc < 20:
                        dst = sC[i][:, 0:Tn] if cc < 16 else BT[:, cc - 16, 0:Tn]
                        dk = ("sC", i) if cc < 16 else ("BT", cc - 16)
                        act(dst, PS[b2][:, 0:Tn], AF.Silu, [("ps", b2), "cb"], [dk], bias=cb[:, cc:cc + 1])
                        st2["dst"], st2["dk"] = dst, dk
                    else:
                        act(CT[:, cc - 20, 0:Tn], PS[b2][:, 0:Tn], AF.Silu, [("ps", b2), "cb"], [("CT", cc - 20)], bias=cb[:, cc:cc + 1])

                def part3(cc=cc, st2=st2):
                    if cc >= 20:
                        return
                    dst, dk = st2["dst"], st2["dk"]
                    bt = nb()
                    pT = PS[bt][:].bitcast(BF16)
                    for j in range(NS):
                        tr(pT[:, j * 128:(j + 1) * 128], dst[:, j * 128:(j + 1) * 128], ident_b, [dk, "cstb"], ("ps", bt))
                    if cc < 16:
                        cp(ev_eng(), xs_v[:, :, cc * 128:(cc + 1) * 128], pT[:, 0:512].rearrange("p (j c) -> p j c", j=4),
                           [("ps", bt), "HBphase1"], [("xs", cc)])
                    else:
                        cp(ev_eng(), Btok[:, :, cc - 16, :], pT[:, 0:512].rearrange("p (j c) -> p j c", j=4), [("ps", bt)], ["Btok"])
                pq.append([part2, part3])
                if len(pq) >= 2:
                    pq[-2][0]()
                if len(pq) >= 3:
                    pq[-3][1]()
            proj_fm(w_in, 0, 8, C_XBC, 24, G1, Tn, xk, cons_xbc)
            pq[-1][0]()
            pq[-2][1]()
            pq[-1][1]()
            xsk = [("xs", c) for c in range(16)]
            ypre = {}
            for blk2 in range(2):
                ypre[("y", blk2)] = wload(w_ssm_out, 0, 16, blk2 * 256, 256)
                for bb in range(2):
                    ypre[("g", blk2 * 2 + bb)] = wload(w_in, 0, 8, C_G + 1024 + (blk2 * 2 + bb) * 128, 128)
            st4 = junk[:].bitcast(F32)
            eac4 = st4[:, 0:256].rearrange("p (j c) -> p j c", j=4)
            dsd4 = st4[:, 256:384].rearrange("p (j c) -> p j c", j=4)
            nbias4 = st4[:, 384:512].rearrange("p (j c) -> p j c", j=4)
            for j in range(NS):
                ba = 7
                mm(PS[ba][:, 0:32], tri_b, dtab[:, j, :], True, True, ["cstb", ("dtab", j)], ("ps", ba))
                mm(PS[ba][:, 32:64], ones_b, dtab[:, j, :], True, True, ["cstb", ("dtab", j)], ("ps", ba))
                cp("dve", acs[:], PS[ba][:, 0:64], [("ps", ba)], ["acs"])
                act(eac4[:, j, :], acs[:], AF.Exp, ["acs", "junk"], [("eac4", j)])
                tt("dve", dsd[:], acs[:, 32:64], acs[:, 0:32], ALU.subtract, ["acs"], ["dsd"])
                act(dsd[:], dsd[:], AF.Exp, ["dsd"], ["dsd"])
                tt("dve", dsd4[:, j, :], dsd[:], dtt[:, j, :], ALU.mult, ["dsd", ("dt", j), "junk"], [("dsd4", j)])
                tt("dve", nbias4[:, j, :], lndt[:, j, :], acs[:, 0:32], ALU.subtract, [("lndt", j), "acs", "junk"], [("nbias4", j)])
                cp("dve", nbh4[:, j, :], nbias4[:, j, :], [("nbias4", j)], [("nbh", j)])
                tt("dve", nbl4[:, j, :], nbias4[:, j, :], nbh4[:, j, :], ALU.subtract, [("nbias4", j), ("nbh", j)], [("nbh", j)])
            BY = [0, 1]
            BO = [2, 3]
            BL = [4, 5]
            for j in range(NS):
                zk = [("zs", j, c) for c in range(4)]
                bc_ = 7
                for g4 in range(4):
                    mm(PS[bc_][:, g4 * 128:(g4 + 1) * 128], BT[:, g4, j * 128:(j + 1) * 128], CT[:, g4, j * 128:(j + 1) * 128], True, True,
                       [("BT", g4), ("CT", g4)], ("ps", bc_))
                cp("act", CBs[:].rearrange("p g i -> p (g i)"), PS[bc_][:, :], [("ps", bc_)], ["CBs"])

                def stageA(t, j=j):
                    g4, half = t // 2, t % 2
                    by, bo = BY[g4 % 2], BO[g4 % 2]
                    if half == 0:
                        mm(PS[bo][:, :], CT[:, g4, j * 128:(j + 1) * 128], Sbf[:, g4 * 512:(g4 + 1) * 512], True, True,
                           [("CT", g4), ("Sbf", g4)], ("ps", bo))
                    li = t % 3
                    bl = BL[t % 2]
                    h0 = g4 * 8 + half * 4
                    mm(PS[bl][:, :], ident_b, fap(mneg_b, [[0, 4], [1, 128]]), True, False, ["cstb"], ("ps", bl))
                    mm(PS[bl][:, :], ident_b, fap(nbh4[:, j, h0:h0 + 1], [[1, 4], [0, 128]]), False, False, ["cstb", ("nbh", j)], ("ps", bl))
                    mm(PS[bl][:, :], ident_b, fap(nbl4[:, j, h0:h0 + 1], [[1, 4], [0, 128]]), False, False, ["cstb", ("nbh", j)], ("ps", bl))
                    for hh in range(4):
                        h = h0 + hh
                        mm(PS[bl][:, hh * 128:(hh + 1) * 128], fap(dtab[:, j, h:h + 1], [[0, 128]]), tri_b, False, hh == 3,
                           [("dtab", j), "cstb"], ("ps", bl))
                    act(Lt[li][:].rearrange("p a b -> p (a b)"), PS[bl][:, :], AF.Exp, [("ps", bl)], [("Lt", li)])
                    tt("dve", MT[li][:], Lt[li][:], fap(CBs[:, g4, 0:1], [[0, 4], [1, 128]]), ALU.mult, [("Lt", li), "CBs"], [("MT", li)])

                def S1(t, j=j):
                    g4, half = t // 2, t % 2
                    by = BY[g4 % 2]
                    li = t % 3
                    for hh in range(4):
                        h = g4 * 8 + half * 4 + hh
                        hl = half * 4 + hh
                        mm(PS[by][:, hl * 64:(hl + 1) * 64], MT[li][:, hh, :], xs_v[:, j, h * 64:(h + 1) * 64], True, True,
                           [("MT", li)] + xsk, ("ps", by))

                def S2(g4, j=j, zk=zk):
                    by, bo = BY[g4 % 2], BO[g4 % 2]
                    i = g4 % 2
                    tt("dve", sA[i][:, :].rearrange("p (h q) -> p h q", h=8), PS[bo][:, :].rearrange("p (h q) -> p h q", h=8),
                       fap(eac4[:, j, g4 * 8:g4 * 8 + 1], [[1, 8], [0, 64]]), ALU.mult, [("ps", bo), ("eac4", j)], [("sA", i)])
                    tt("dve", sA[i][:, :], sA[i][:, :], PS[by][:, :], ALU.add, [("sA", i), ("ps", by)], [("sA", i)])
                    tt("pool", sB[i][:, :].rearrange("p (h q) -> p h q", h=8), xs_v[:, j, g4 * 512:(g4 + 1) * 512].rearrange("p (h q) -> p h q", h=8),
                       fap(hp[:, 2, g4 * 8:g4 * 8 + 1], [[1, 8], [0, 64]]), ALU.mult, xsk + ["hp"], [("sB", i)])
                    tt("dve", sA[i][:, :], sA[i][:, :], sB[i][:, :], ALU.add, [("sA", i), ("sB", i)], [("sA", i)])
                    tt("dve", yb[i][:, :], sA[i][:, :], zs_v[:, j, g4 * 512:(g4 + 1) * 512], ALU.mult, [("sA", i)] + zk, [("yb", i)])
                    xg = xddg[i]
                    tt("pool", xg[:].rearrange("p (h q) -> p h q", h=8), xs_v[:, j, g4 * 512:(g4 + 1) * 512].rearrange("p (h q) -> p h q", h=8),
                       fap(dsd4[:, j, g4 * 8:g4 * 8 + 1], [[1, 8], [0, 64]]), ALU.mult, xsk + [("dsd4", j)], [("xddg", i)])
                    tt("pool", Sf[:, g4 * 512:(g4 + 1) * 512].rearrange("p (h q) -> p h q", h=8),
                       Sf[:, g4 * 512:(g4 + 1) * 512].rearrange("p (h q) -> p h q", h=8),
                       fap(eac4[:, j, 32 + g4 * 8:32 + g4 * 8 + 1], [[1, 8], [0, 64]]), ALU.mult, [("Sf", g4), ("eac4", j)], [("Sf", g4)])

                def S3(g4):
                    i = g4 % 2
                    act(sB[i][:, :], yb[i][:, :], AF.Square, [("yb", i)], [("sB", i), ("gss", i)], accum=stat[:, 8 + i:9 + i])
                    rstd_from_ssq(8 + i, 12 + i, 512, 128, ("gss", i))

                def S4(g4, j=j):
                    i = g4 % 2
                    act(ynb[i][:, :], yb[i][:, :], AF.Copy, [("yb", i), ("gss", i)], [("ynb", i)], scale=stat[:, 12 + i:13 + i])
                    mm(PS[7][:, :], Btok[:, j, g4, :], xddg[i][:], True, True, ["Btok", ("xddg", i)], ("ps", 7))
                    pT = PS[6][:].bitcast(BF16)
                    for q in range(4):
                        tr(pT[:, q * 128:(q + 1) * 128], ynb[i][:, q * 128:(q + 1) * 128], ident_b, [("ynb", i), "cstb"], ("ps", 6))

                def S5(g4, j=j):
                    pT = PS[6][:].bitcast(BF16)
                    tt("dve", Sf[:, g4 * 512:(g4 + 1) * 512], Sf[:, g4 * 512:(g4 + 1) * 512], PS[7][:, :], ALU.add,
                       [("Sf", g4), ("ps", 7)], [("Sf", g4)])
                    cp("act", Sbf[:, g4 * 512:(g4 + 1) * 512], Sf[:, g4 * 512:(g4 + 1) * 512], [("Sf", g4)], [("Sbf", g4)])
                    for q in range(4):
                        ch = g4 * 4 + q
                        dstb = G2 if ch < 8 else G3
                        dk = ("G2" if ch < 8 else "G3", ch % 8)
                        if q % 2 == 0:
                            ts("dve", dstb[:, ch % 8, j * 128:(j + 1) * 128], pT[:, q * 128:(q + 1) * 128], nw[:, ch:ch + 1], None,
                               ALU.mult, None, [("ps", 6), "nw"], [dk])
                        else:
                            act(dstb[:, ch % 8, j * 128:(j + 1) * 128], pT[:, q * 128:(q + 1) * 128], AF.Copy, [("ps", 6), "nw"], [dk],
                                scale=nw[:, ch:ch + 1])

                for u in range(14):
                    for g4 in range(4):
                        if u == 2 * g4 + 7:
                            S5(g4)
                    for g4 in range(4):
                        if u == 2 * g4 + 6:
                            S4(g4)
                    for g4 in range(4):
                        if u == 2 * g4 + 5:
                            S3(g4)
                    for g4 in range(4):
                        if u == 2 * g4 + 4:
                            S2(g4)
                    if 2 <= u < 10:
                        S1(u - 2)
                    if u < 8:
                        stageA(u)
            if last:
                for q in range(16):
                    b = nb()
                    tr(PS[b][:, 0:128], Sf[:, q * 128:(q + 1) * 128], ident_f, [("Sf", q // 4), "cstf"], ("ps", b))
                    so = ostg[q % 2]
                    cp(ev_eng(), so[:, 0:128], PS[b][:, 0:128], [("ps", b)], [("ostg", q % 2)])
                    dma_out(pssm[q * 128:(q + 1) * 128, :], so[:, 0:128], [("ostg", q % 2)], slot=("ostg", q % 2))
            ybank = {}

            def ynT(k):
                return (G2 if k < 8 else G3)[:, k % 8, 0:Tn]
            S.add("dve", lambda g: g.memset(stat[:, 20:21], 0.0), writes=["HBphase2", "HBphase1"] + xsk + [("zs", j, c) for j in range(4) for c in range(4)])
            for blk2 in range(4):
                slab, wk = ypre.pop(("y", blk2)) if ("y", blk2) in ypre else wload(w_ssm_out, 0, 16, blk2 * 256, 256)
                for bb in range(2):
                    blk = blk2 * 2 + bb
                    b = nb()
                    for k in range(16):
                        mm(PS[b][:, 0:Tn], slab[:, k, bb * 128:(bb + 1) * 128], ynT(k), k == 0, k == 15,
                           [wk, ("G2", k % 8) if k < 8 else ("G3", k % 8)], ("ps", b))
                    slg, wkg = ypre.pop(("g", blk)) if ("g", blk) in ypre else wload(w_in, 0, 8, C_G + 1024 + blk * 128, 128)
                    bg = nb()
                    for k in range(8):
                        mm(PS[bg][:, 0:Tn], slg[:, k, 0:128], G1[:, k, 0:Tn], k == 0, k == 7, [wkg] + xk, ("ps", bg))
                    i = blk % 2
                    act(sA[i][:, 0:Tn], PS[bg][:, 0:Tn], AF.Sigmoid, [("ps", bg)], [("sA", i)])
                    tt("dve", sB[i][:, 0:Tn], sA[i][:, 0:Tn], PS[b][:, 0:Tn], ALU.mult, [("sA", i), ("ps", b)], [("sB", i)])
                    tt("dve", mT_v[:, blk, 0:Tn], MG[:, blk, 0:Tn], sB[i][:, 0:Tn], ALU.add,
                       [("sB", i), ("MG", blk), "HBphase2"], [("mT", blk)])

        if 'A' in branches:
            branchA()
        if 'C' in branches:
            branchC()
        if 'B' in branches:
            branchB()
        if branches != 'ACB':
            return
        mtk = [("mT", c) for c in range(8)]
        dma_in(nrm[:], nrm_bc[:, 1, :], "nrm")
        mslabs = [wload(w_merge_o, 0, 8, cbk * 512, 512) for cbk in range(2)]
        xn2 = [(xnb[0], ("xnb", 0)), (ostg[1][:].bitcast(BF16)[:, 0:D], ("ostg", 1))]
        pendn = [None]
        for j in range(NS):
            for cbk in range(2):
                slab, wk = mslabs[cbk]
                b = nb()
                for k in range(8):
                    mm(PS[b][:, :], mT_v[:, k, j * 128:(j + 1) * 128], slab[:, k, :], k == 0, k == 7, [wk] + mtk, ("ps", b))
                tt("dve", X[:, j, cbk * 512:(cbk + 1) * 512], X[:, j, cbk * 512:(cbk + 1) * 512], PS[b][:, :], ALU.add,
                   [("X", j), ("ps", b)], [("X", j)])
            act(junk[:, :], X[:, j, :], AF.Square, [("X", j)], ["junk", ("nst", j)], accum=stat[:, 48 + j:49 + j])
            ts("dve", stat[:, 52 + j:53 + j], stat[:, 48 + j:49 + j], 1.0 / D, EPS, ALU.mult, ALU.add, [("nst", j)], [("nst", j)])
            act(stat[:, 52 + j:53 + j], stat[:, 52 + j:53 + j], AF.Ln, [("nst", j)], [("nst", j)])
            act(stat[:, 52 + j:53 + j], stat[:, 52 + j:53 + j], AF.Exp, [("nst", j)], [("nst", j)], scale=-0.5)
            xn, xkey = xn2[j % 2]
            stt(xn[:, :], X[:, j, :], stat[:, 52 + j:53 + j], nrm[:, :], ALU.mult, ALU.mult, [("X", j), ("nst", j), "nrm"], [xkey])

            def trp(j=j, xn=xn, xkey=xkey):
                b = nb()
                pT = PS[b][:].bitcast(BF16)
                for k in range(8):
                    tr(pT[:, k * 128:(k + 1) * 128], xn[:, k * 128:(k + 1) * 128], ident_b, [xkey, "cstb"], ("ps", b))
                cp(ev_eng(), G1[:, :, j * 128:(j + 1) * 128], pT[:, 0:1024].rearrange("p (k t) -> p k t", k=8), [("ps", b)], [("G1", "all")])
            if pendn[0]:
                pendn[0]()
            pendn[0] = trp
        pendn[0]()
        for fb in range(22):
            slg, wkg = wload(w_gate, 0, 8, fb * 128, 128)
            slu, wku = wload(w_up, 0, 8, fb * 128, 128)
            b1 = nb()
            for k in range(8):
                mm(PS[b1][:, 0:Tn], slg[:, k, :], G1[:, k, 0:Tn], k == 0, k == 7, [wkg] + xk, ("ps", b1))
            i = fb % 2
            act(sA[i][:, 0:Tn], PS[b1][:, 0:Tn], AF.Silu, [("ps", b1)], [("sA", i)])
            b2 = nb()
            for k in range(8):
                mm(PS[b2][:, 0:Tn], slu[:, k, :], G1[:, k, 0:Tn], k == 0, k == 7, [wku] + xk, ("ps", b2))
            tt("dve", hT_v[:, fb, 0:Tn], sA[i][:, 0:Tn], PS[b2][:, 0:Tn], ALU.mult, [("sA", i), ("ps", b2), "HBphase2"], [("hT", fb)])
        htk = [("hT", c) for c in range(22)]
        for cbk in range(2):
            banks = [nb() for _ in range(NS)]
            for half in range(2):
                slab, wk = wload(w_down, half * 11 * 128, 11, cbk * 512, 512)
                for j in range(NS):
                    for k in range(11):
                        kk = half * 11 + k
                        mm(PS[banks[j]][:, :], hT_v[:, kk, j * 128:(j + 1) * 128], slab[:, k, :], kk == 0, kk == 21,
                           [wk] + htk, ("ps", banks[j]))
            for j in range(NS):
                tt("dve", X[:, j, cbk * 512:(cbk + 1) * 512], X[:, j, cbk * 512:(cbk + 1) * 512], PS[banks[j]][:, :], ALU.add,
                   [("X", j), ("ps", banks[j])], [("X", j)])
        S.add("dve", lambda g: g.memset(stat[:, 21:22], 0.0), writes=["HBphase1", "HBphase2"] + htk + mtk)
        dma_in(nrm[:], nrm_bc[:, 2, :], "nrm")
        for j in range(NS):
            act(junk[:, :], X[:, j, :], AF.Square, [("X", j)], ["junk", "stat0"], accum=stat[:, 0:1])
            rstd_from_ssq(0, 1, D, 128, "stat0")
            so = ostg[j % 2]
            stt(so[:, :], X[:, j, :], stat[:, 1:2], nrm[:, :], ALU.mult, ALU.mult, [("X", j), "stat0", "nrm"], [("ostg", j % 2)])
            dma_out(y_prompt[t0 + j * 128:t0 + (j + 1) * 128, :], so[:, :], [("ostg", j % 2)], slot=("ostg", j % 2))
            if ti + 1 < ntiles:
                dma_in(X[:, j, :], xin[t0 + T + j * 128:t0 + T + (j + 1) * 128, :], ("X", j))
        if ti + 1 < ntiles:
            xpref["done"] = True

    if stop in (3, 4, 5):
        return finish()
    for ti in range(ntiles):
        tile(ti, ti == NT - 1)
    dma_out(pconv[:, :, :], pcv[:], ["pcv"])
    dma_out(pssmc[:, :, :], pscv[:], ["pscv"])
    def sample():
        P = NSB
        Tn = NSB
        allkeys = list(set(S.lastw.keys()) | set(S.readers.keys()))
        S.add("dve", lambda g: g.memset(stat[:, 30:31], 0.0), writes=allkeys + ["SPH"])
        S.auto_reads = ["SPH"]
        xk = [("G1", "all")]
        Dc = dsd[:, 0:16]
        stat_g = eac[:, 0:64]
        dma_in(Dc, Dc_in[:, :], "Dc")
        Xf = X[:].rearrange("p j c -> p (j c)")
        Xs3 = X
        xtok2 = Xf[0:P, 1024:3072]
        bctok = Xf[0:P, 3072:4096]
        MGf = MG[:].rearrange("p k t -> p (k t)")
        t2 = MGf[:, 0:2048]
        prod = MGf[:, 2048:4096]
        HBf = HB[:].bitcast(F32)
        Kb = HBf[:, 0:2048]
        Vs = HBf[:, 2048:4096]
        St = HBf[:, 4096:6144]
        t1 = HBf[:, 6144:8192]
        G3f = G3[:].rearrange("p k t -> p (k t)").bitcast(F32)
        G2b = G2[:].rearrange("p k t -> p (k t)")
        G2f = G2b.bitcast(F32)

        def v3(ap, n):
            return ap.rearrange("p (a b) -> p a b", b=n)
        hsA = v3(G3f[:, 0:256], 16)
        hsB = v3(G3f[:, 256:1408], 16)
        uT = v3(G3f[:, 1408:1536], 16)
        xbcT = v3(G3f[:, 1536:1920], 16)
        MGs = v3(G3f[:, 1920:2048], 16)
        zT = v3(G2f[:, 0:256], 16)
        dtT = v3(G2f[:, 256:512], 16)
        decT = v3(G2f[:, 512:768], 16)
        Ys = v3(G2f[:, 768:1024], 16)
        yg = v3(G2f[:, 1024:1280], 16)
        sq = v3(G2f[:, 1280:1536], 16)
        vTs = v3(G2b[:, 3072:3200], 16)
        qTs = v3(G2b[:, 3200:3328], 16)
        oTs = v3(G2b[:, 3328:3456], 16)
        ynTs = v3(G2b[:, 3456:3712], 16)
        mTs = v3(G2b[:, 3712:3840], 16)
        hTs = v3(sC[0][:, 0:352], 16)
        xraw = v3(sA[1][:, 0:384], 16)
        Vaug = Sf[:].bitcast(BF16)[:, 0:2056].rearrange("p (m h d) -> p m h d", m=2, h=4)
        q_tok = xdd[0:P, 0:1024]
        o_tokb = xdd[0:P, 1024:2048]
        o_raw = Xf[0:P, 1024:2048]
        tA = sA[0][:, 0:Tn]
        tB = sB[0][:, 0:Tn]
        tC = sB[1][:, 0:Tn]
        dts = dtt[0:P, 0, :]
        dtas = dtt[0:P, 1, :]
        decs = dtt[0:P, 2, :]
        ssc = acs[:, 0:8]
        sp8 = sm[:, 0:8]
        p8 = CBs[:, 0, 0:8]
        den1 = stat[0:1, 40:44]
        dent = stat[0:P, 44:48]
        idf16 = ident_f[0:P, 0:P]
        idb16 = ident_b[0:P, 0:P]

        dma_in(X[0:P, 0, :], xs_in[:, :], ("X", 0))
        norm_transpose(X, P, 1, 0, G1, "X", Tn)

        def load_T(src_ap, dst3, row0, tag):
            i = cnt["rr"] % 2
            cnt["rr"] += 1
            so = ostg[i]
            S.add("sp", lambda g: g.dma_start(out=so[0:P, :], in_=src_ap), writes=[("ostg", i)], slot=("ostg", i))
            b = nb()
            for c in range(8):
                tr(PS[b][:, c * 16:(c + 1) * 16], so[0:P, c * 128:(c + 1) * 128], idf16, [("ostg", i), "cstf"], ("ps", b))
            cp(ev_eng(), dst3[:, row0:row0 + 8, :], PS[b][:, 0:128].rearrange("p (c t) -> p c t", c=8), [("ps", b)], [tag])
        for r in range(2):
            load_T(st_conv[:, r, :], hsA, r * 8, "hsA")
        for r in range(3):
            for q3 in range(3):
                load_T(st_sconv[:, r, q3 * 1024:(q3 + 1) * 1024], hsB, r * 24 + q3 * 8, "hsB")
        S.add("sp", lambda g: g.dma_start(out=sconv[:, 0, :], in_=st_conv[:, 1, :]), slot=mslot())
        S.add("sp", lambda g: g.dma_start(out=sssmc[:, 0:2, :], in_=st_sconv[:, 1:3, :]), slot=mslot())

        def store_T(src3, row0, dst_ap, skeys):
            i = cnt["rr"] % 2
            cnt["rr"] += 1
            so = ostg[i]
            for half in range(2):
                b = nb()
                for c in range(4):
                    tr(PS[b][0:P, c * 128:(c + 1) * 128], src3[:, row0 + half * 4 + c, :], ident_f, skeys + ["cstf"], ("ps", b))
                cp(ev_eng(), so[0:P, half * 512:(half + 1) * 512], PS[b][0:P, :], [("ps", b)], [("ostg", i)])
            S.add("sp", lambda g: g.dma_start(out=dst_ap, in_=so[0:P, :]), reads=[("ostg", i)], slot=("ostg", i))

        def s_out_proj(Wo, kch, srcT, src_keys, gate_c0, mode):
            bps = 4 if kch == 8 else 2
            for blk0 in range(0, 8, bps):
                slab, wk = wload(Wo, 0, kch, blk0 * 128, bps * 128)
                if blk0 % 4 == 0:
                    slg, wkg = wload(w_in, 0, 8, gate_c0 + blk0 * 128, 512)
                for bb in range(bps):
                    blk = blk0 + bb
                    b = nb()
                    for k in range(kch):
                        mm(PS[b][:, 0:Tn], slab[:, k, bb * 128:(bb + 1) * 128], srcT[:, k, :], k == 0, k == kch - 1, [wk] + src_keys, ("ps", b))
                    bg = nb()
                    go = (blk % 4) * 128
                    for k in range(8):
                        mm(PS[bg][:, 0:Tn], slg[:, k, go:go + 128], G1[:, k, 0:Tn], k == 0, k == 7, [wkg] + xk, ("ps", bg))
                    act(tA, PS[bg][:, 0:Tn], AF.Sigmoid, [("ps", bg)], ["tA"])
                    if mode == "first":
                        tt("dve", MGs[:, blk, :], tA, PS[b][:, 0:Tn], ALU.mult, ["tA", ("ps", b)], ["MGs"])
                    else:
                        tt("dve", tC, tA, PS[b][:, 0:Tn], ALU.mult, ["tA", ("ps", b)], ["tC"])
                        if mode == "mid":
                            tt("dve", MGs[:, blk, :], MGs[:, blk, :], tC, ALU.add, ["tC", "MGs"], ["MGs"])
                        else:
                            tt("dve", mTs[:, blk, :], MGs[:, blk, :], tC, ALU.add, ["tC", "MGs"], ["mTs"])

        for c in range(8):
            if c % 4 == 0:
                slc4, wkc = wload(w_in, 0, 8, C_SCC + c * 128, 512)
                slx4, wkx = wload(w_in, 0, 8, C_SCX + c * 128, 512)
                slb4, wkb = wload(w_in, 0, 8, C_SCB + c * 128, 512)
            co = (c % 4) * 128
            b1 = nb()
            for k in range(8):
                mm(PS[b1][:, 0:Tn], slc4[:, k, co:co + 128], G1[:, k, 0:Tn], k == 0, k == 7, [wkc] + xk, ("ps", b1))
            act(tA, PS[b1][:, 0:Tn], AF.Copy, [("ps", b1)], ["tA"])
            b2 = nb()
            for k in range(8):
                mm(PS[b2][:, 0:Tn], slx4[:, k, co:co + 128], G1[:, k, 0:Tn], k == 0, k == 7, [wkx] + xk, ("ps", b2))
            tt("dve", uT[:, c, :], tA, PS[b2][:, 0:Tn], ALU.mult, ["tA", ("ps", b2)], ["uT"])
            ts("dve", tB, hsA[:, c, :], scw[:, c, 0:1], None, ALU.mult, None, ["hsA", "scw"], ["tB"])
            stt(tB, hsA[:, 8 + c, :], scw[:, c, 1:2], tB, ALU.mult, ALU.add, ["hsA", "scw", "tB"], ["tB"])
            stt(tB, uT[:, c, :], scw[:, c, 2:3], tB, ALU.mult, ALU.add, ["uT", "scw", "tB"], ["tB"])
            b4 = nb()
            for k in range(8):
                mm(PS[b4][:, 0:Tn], slb4[:, k, co:co + 128], G1[:, k, 0:Tn], k == 0, k == 7, [wkb] + xk, ("ps", b4))
            tt("dve", vTs[:, c, :], tB, PS[b4][:, 0:Tn], ALU.mult, ["tB", ("ps", b4)], ["vTs"])
        store_T(uT, 0, sconv[:, 1, :], ["uT"])
        s_out_proj(w_sc_out, 8, vTs, ["vTs"], C_G, "first")

        def cons_q(blk, b):
            act(qTs[:, blk, :], PS[b][:, 0:Tn], AF.Copy, [("ps", b)], ["qTs"], scale=1.0 / 16.0)
        proj_fm(w_in, 0, 8, C_Q, 8, G1, Tn, xk, cons_q)
        bq = nb()
        pTq = PS[bq][:].bitcast(BF16)
        for blk in range(8):
            tr(pTq[0:P, blk * 128:(blk + 1) * 128], qTs[:, blk, :], ident_b, ["qTs", "cstb"], ("ps", bq))
        cp("act", q_tok, pTq[0:P, 0:1024], [("ps", bq)], ["q_tok"])
        memset("dve", Vaug[:, :, :, 256:257], 1.0, ["Vaug1"])
        HQ = [HBf[:, 0:2048], HBf[:, 2048:4096], HBf[:, 4096:6144], HBf[:, 6144:8192]]
        HK = ["hbA", "hbB", "hbC", "hbD"]
        den1s = [stat[0:1, 40:44], stat[0:1, 60:64]]

        def kv_load(bi):
            i = bi % 2
            S.add("sp", lambda g: g.dma_start(out=HQ[2 * i].rearrange("p (m c) -> p m c", m=2),
                                              in_=ck_in[bi].rearrange("(m p) c -> p m c", p=128)), writes=[HK[2 * i]], slot=("kb", i))
            S.add("sp", lambda g: g.dma_start(out=HQ[2 * i + 1].rearrange("p (m c) -> p m c", m=2),
                                              in_=cv_in[bi].rearrange("(m p) c -> p m c", p=128)), writes=[HK[2 * i + 1]], slot=("vs", i))
        kv_load(0)
        for bi in range(P):
            if bi + 1 < P:
                kv_load(bi + 1)
            i = bi % 2
            Kb_, Vs_, kk, vk = HQ[2 * i], HQ[2 * i + 1], HK[2 * i], HK[2 * i + 1]
            oh = fap(ident_b[0:P, bi:bi + 1], [[0, 128]])
            bq0 = nb()
            mm(PS[bq0][:, :], oh, q_tok[:, 0:512], True, True, ["cstb", "q_tok"], ("ps", bq0))
            bq1 = nb()
            mm(PS[bq1][:, :], oh, q_tok[:, 512:1024], True, True, ["cstb", "q_tok"], ("ps", bq1))
            Kb3 = Kb_.rearrange("p (m c) -> p m c", m=2)
            pr3 = prod.rearrange("p (m c) -> p m c", m=2)
            tt("dve", pr3[:, :, 0:512], Kb3[:, :, 0:512], fap(PS[bq0][:, 0:1], [[0, 2], [1, 512]]), ALU.mult, [kk, ("ps", bq0)], ["prod"])
            tt("dve", pr3[:, :, 512:1024], Kb3[:, :, 512:1024], fap(PS[bq1][:, 0:1], [[0, 2], [1, 512]]), ALU.mult, [kk, ("ps", bq1)], ["prod"])
            S.add("dve", lambda g: g.tensor_reduce(out=ssc, in_=prod.rearrange("p (a d) -> p a d", d=256), op=ALU.add, axis=AX),
                  reads=["prod"], writes=["ssc"])
            ts("dve", ssc, ssc, 80.0, None, ALU.min, None, ["ssc"], ["ssc"])
            act(p8, ssc, AF.Exp, ["ssc"], ["p8"])
            act(Vaug[:, :, :, 0:256], Vs_.rearrange("p (m h d) -> p m h d", m=2, h=4), AF.Copy, [vk], ["Vaug"])
            bos = [nb(), nb(), nb(), nb()]
            for h in range(4):
                bo = bos[h]
                for mc in range(2):
                    mm(PS[bo][0:1, 0:257], p8[:, mc * 4 + h:mc * 4 + h + 1], Vaug[:, mc, h, :], mc == 0, mc == 1,
                       ["p8", "Vaug", "Vaug1"], ("ps", bo))
            orow = ostg[i][0:1, :]
            den1 = den1s[i]
            for h in range(4):
                bo = bos[h]
                cp("act", orow[:, h * 256:(h + 1) * 256], PS[bo][0:1, 0:256], [("ps", bo)], [("ostg", i)])
                cp("act", den1[:, h:h + 1], PS[bo][0:1, 256:257], [("ps", bo)], [("den1", i)])
            S.add("sp", lambda g, bi=bi, orow=orow: g.dma_start(out=o_raw[bi:bi + 1, :], in_=orow), reads=[("ostg", i)], writes=["o_raw"], slot=("ostg", i))
            S.add("sp", lambda g, bi=bi, den1=den1: g.dma_start(out=dent[bi:bi + 1, :], in_=den1), reads=[("den1", i)], writes=["dent"], slot=("den", i))
        S.add("dve", lambda g: g.reciprocal(dent, dent), reads=["dent"], writes=["dent"])
        tt("dve", o_tokb.rearrange("p (h d) -> p h d", h=4), o_raw.rearrange("p (h d) -> p h d", h=4),
           fap(dent[:, 0:1], [[1, 4], [0, 256]]), ALU.mult, ["o_raw", "dent"], ["o_tokb"])
        bo_ = nb()
        pTo = PS[bo_][:].bitcast(BF16)
        for blk in range(8):
            tr(pTo[:, blk * 16:(blk + 1) * 16], o_tokb[:, blk * 128:(blk + 1) * 128], idb16, ["o_tokb", "cstb"], ("ps", bo_))
        cp("act", oTs, pTo[:, 0:128].rearrange("p (k t) -> p k t", k=8), [("ps", bo_)], ["oTs"])
        s_out_proj(w_attn_o, 8, oTs, ["oTs"], C_G + 2048, "mid")

        def cons_z(blk, b):
            act(zT[:, blk, :], PS[b][:, 0:Tn], AF.Silu, [("ps", b)], ["zT"])
        proj_fm(w_in, 0, 8, C_Z, 16, G1, Tn, xk, cons_z)
        slab, wk = wload(w_in, 0, 8, C_DT, 32)
        b = nb()
        for k in range(8):
            mm(PS[b][0:P, 0:32], G1[:, k, 0:Tn], slab[:, k, :], k == 0, k == 7, [wk] + xk, ("ps", b))
        tt("dve", dts, PS[b][0:P, 0:32], hp[0:P, 0, :], ALU.add, [("ps", b), "hp"], ["dts"])
        act(dts, dts, AF.Exp, ["dts"], ["dts"])
        act(dts, dts, AF.Ln, ["dts"], ["dts"], bias=1.0)
        tt("dve", dtas, dts, abc[0:P, :], ALU.mult, ["dts", "abc"], ["dtas"])
        act(decs, dtas, AF.Exp, ["dtas"], ["decs"])

        def cons_xbc(cc, b):
            act(xraw[:, cc, :], PS[b][:, 0:Tn], AF.Copy, [("ps", b)], ["xraw"])
            ts("dve", tB, hsB[:, cc, :], cw[:, cc, 0:1], None, ALU.mult, None, ["hsB", "cw"], ["tB"])
            stt(tB, hsB[:, 24 + cc, :], cw[:, cc, 1:2], tB, ALU.mult, ALU.add, ["hsB", "cw", "tB"], ["tB"])
            stt(tB, hsB[:, 48 + cc, :], cw[:, cc, 2:3], tB, ALU.mult, ALU.add, ["hsB", "cw", "tB"], ["tB"])
            stt(tB, xraw[:, cc, :], cw[:, cc, 3:4], tB, ALU.mult, ALU.add, ["xraw", "cw", "tB"], ["tB"])
            act(xbcT[:, cc, :], tB, AF.Silu, ["tB", "cb"], ["xbcT"], bias=cb[:, cc:cc + 1])
        proj_fm(w_in, 0, 8, C_XBC, 24, G1, Tn, xk, cons_xbc)
        for q3 in range(3):
            store_T(xraw, q3 * 8, sssmc[:, 2, q3 * 1024:(q3 + 1) * 1024], ["xraw"])
        for half in range(2):
            b = nb()
            for c in range(4):
                tr(PS[b][0:P, c * 128:(c + 1) * 128], xbcT[:, 16 + half * 4 + c, :], ident_f, ["xbcT", "cstf"], ("ps", b))
            cp(ev_eng(), bctok[:, half * 512:(half + 1) * 512], PS[b][0:P, :], [("ps", b)], ["bctok"])

        def bcast_T(src16, dstT, key):
            cp("dve", xtok2.rearrange("p (h q) -> p h q", h=32), fap(src16[:, 0:1], [[1, 32], [0, 64]]), [key], ["xtok2"])
            b = nb()
            for hpi in range(16):
                tr(PS[b][:, hpi * 16:(hpi + 1) * 16], xtok2[:, hpi * 128:(hpi + 1) * 128], idf16, ["xtok2", "cstf"], ("ps", b))
            cp(ev_eng(), dstT, PS[b][:, 0:256].rearrange("p (a t) -> p a t", a=16), [("ps", b)], [key + "T"])
        bcast_T(dts, dtT, "dts")
        bcast_T(decs, decT, "decs")
        tt("dve", dtT, dtT, xbcT[:, 0:16, :], ALU.mult, ["dtsT", "xbcT"], ["dtsT"])
        def st_load(bi):
            i = bi % 2
            S.add("sp", lambda g: g.dma_start(out=HQ[2 * i].rearrange("p (a n) -> p a n", a=16),
                                              in_=st_ssm[bi].rearrange("(a q) n -> q a n", q=128)), writes=[HK[2 * i]], slot=("kb", i))
        st_load(0)
        for bi in range(P):
            if bi + 1 < P:
                st_load(bi + 1)
            i = bi % 2
            St_, t1_, sk_, tk_ = HQ[2 * i], HQ[2 * i + 1], HK[2 * i], HK[2 * i + 1]
            oh = fap(ident_f[0:P, bi:bi + 1], [[0, 128]])
            bB = nb()
            mm(PS[bB][:, :], oh, bctok[:, 0:512], True, True, ["cstf", "bctok"], ("ps", bB))
            bC = nb()
            mm(PS[bC][:, :], oh, bctok[:, 512:1024], True, True, ["cstf", "bctok"], ("ps", bC))
            tt("dve", t1_.rearrange("p (a n) -> p a n", a=16), St_.rearrange("p (a n) -> p a n", a=16),
               fap(decT[:, 0, bi:bi + 1], [[16, 16], [0, 128]]), ALU.mult, [sk_, "decsT"], [tk_])
            tt("dve", fap(t2[:, 0:1], [[512, 4], [128, 4], [1, 128]]), fap(PS[bB][:, 0:1], [[128, 4], [0, 4], [1, 128]]),
               fap(dtT[:, 0, bi:bi + 1], [[64, 4], [16, 4], [0, 128]]), ALU.mult, [("ps", bB), "dtsT"], ["t2"])
            tt("pool", t1_, t1_, t2, ALU.add, [tk_, "t2"], [tk_])
            S.add("pool", lambda g, bi=bi, t1_=t1_: g.dma_start(out=sssm[bi].rearrange("(a q) n -> q a n", q=128),
                                                             in_=t1_.rearrange("p (a n) -> p a n", a=16)), reads=[tk_], slot=("sto", i))
            tt("dve", fap(t2[:, 0:1], [[512, 4], [128, 4], [1, 128]]), fap(t1_[:, 0:1], [[512, 4], [128, 4], [1, 128]]),
               fap(PS[bC][:, 0:1], [[128, 4], [0, 4], [1, 128]]), ALU.mult, [tk_, ("ps", bC)], ["t2"])
            S.add("dve", lambda g, bi=bi: g.tensor_reduce(out=fap(Ys[:, 0, bi:bi + 1], [[16, 16]]), in_=t2.rearrange("p (a n) -> p a n", a=16),
                                                          op=ALU.add, axis=AX), reads=["t2"], writes=["Ys"])
        tt("dve", yg, xbcT[:, 0:16, :], fap(Dc[:, 0:1], [[1, 16], [0, 16]]), ALU.mult, ["xbcT", "Dc"], ["yg"])
        tt("dve", yg, yg, Ys, ALU.add, ["yg", "Ys"], ["yg"])
        tt("dve", yg, yg, zT, ALU.mult, ["yg", "zT"], ["yg"])
        tt("dve", sq, yg, yg, ALU.mult, ["yg"], ["sq"])
        bs_ = nb()
        mm(PS[bs_][:, 0:256], cstf[:, 3, :], sq.rearrange("p a t -> p (a t)"), True, True, ["cstf", "sq"], ("ps", bs_))
        gsv = stat_g
        S.add("dve", lambda g: g.tensor_reduce(out=gsv.rearrange("p (g t) -> p g t", g=4), in_=fap(PS[bs_][:, 0:1], [[64, 4], [1, 16], [16, 4]]),
                                               op=ALU.add, axis=AX), reads=[("ps", bs_)], writes=["gsv"])
        ts("dve", gsv, gsv, 1.0 / 512, EPS, ALU.mult, ALU.add, ["gsv"], ["gsv"])
        act(gsv, gsv, AF.Sqrt, ["gsv"], ["gsv"])
        S.add("dve", lambda g: g.reciprocal(gsv, gsv), reads=["gsv"], writes=["gsv"])
        tt("dve", fap(yg[:, 0, 0:1], [[64, 4], [16, 4], [1, 16]]), fap(yg[:, 0, 0:1], [[64, 4], [16, 4], [1, 16]]),
           fap(gsv[:, 0:1], [[16, 4], [0, 4], [1, 16]]), ALU.mult, ["yg", "gsv"], ["yg"])
        tt("dve", ynTs, yg, fap(nw[:, 0:1], [[1, 16], [0, 16]]), ALU.mult, ["yg", "nw"], ["ynTs"])
        s_out_proj(w_ssm_out, 16, ynTs, ["ynTs"], C_G + 1024, "last")
        for cbk in range(2):
            slab, wk = wload(w_merge_o, 0, 8, cbk * 512, 512)
            b = nb()
            for k in range(8):
                mm(PS[b][0:P, :], mTs[:, k, :], slab[:, k, :], k == 0, k == 7, [wk, "mTs"], ("ps", b))
            tt("dve", X[0:P, 0, cbk * 512:(cbk + 1) * 512], X[0:P, 0, cbk * 512:(cbk + 1) * 512], PS[b][0:P, :], ALU.add,
               [("X", 0), ("ps", b)], [("X", 0)])
        norm_transpose(X, P, 1, 1, G1, "X", Tn)
        for fb in range(22):
            if fb % 4 == 0:
                ncb = min(4, 22 - fb) * 128
                slg4, wkg = wload(w_gate, 0, 8, fb * 128, ncb)
                slu4, wku = wload(w_up, 0, 8, fb * 128, ncb)
            fo = (fb % 4) * 128
            b1 = nb()
            for k in range(8):
                mm(PS[b1][:, 0:Tn], slg4[:, k, fo:fo + 128], G1[:, k, 0:Tn], k == 0, k == 7, [wkg] + xk, ("ps", b1))
            act(tA, PS[b1][:, 0:Tn], AF.Silu, [("ps", b1)], ["tA"])
            b2 = nb()
            for k in range(8):
                mm(PS[b2][:, 0:Tn], slu4[:, k, fo:fo + 128], G1[:, k, 0:Tn], k == 0, k == 7, [wku] + xk, ("ps", b2))
            tt("dve", hTs[:, fb, :], tA, PS[b2][:, 0:Tn], ALU.mult, ["tA", ("ps", b2)], ["hTs"])
        for cbk in range(2):
            b = nb()
            for half in range(2):
                slab, wk = wload(w_down, half * 11 * 128, 11, cbk * 512, 512)
                for k in range(11):
                    kk = half * 11 + k
                    mm(PS[b][0:P, :], hTs[:, kk, :], slab[:, k, :], kk == 0, kk == 21, [wk, "hTs"], ("ps", b))
            tt("dve", X[0:P, 0, cbk * 512:(cbk + 1) * 512], X[0:P, 0, cbk * 512:(cbk + 1) * 512], PS[b][0:P, :], ALU.add,
               [("X", 0), ("ps", b)], [("X", 0)])
        dma_in(nrm[:], nrm_bc[:, 2, :], "nrm")
        act(junk[0:P, :], X[0:P, 0, :], AF.Square, [("X", 0)], ["junk", "stat0"], accum=stat[0:P, 0:1])
        rstd_from_ssq(0, 1, D, P, "stat0")
        stt(ostg[0][0:P, :], X[0:P, 0, :], stat[0:P, 1:2], nrm[0:P, :], ALU.mult, ALU.mult, [("X", 0), "stat0", "nrm"], [("ostg", 0)])
        dma_out(y_sample[:, :], ostg[0][0:P, :], [("ostg", 0)], slot=("ostg", 0))

    if with_sample:
        sample()
    S.emit(st)
    st.close()
    return nc


_CACHE = {}


def _consts():
    c = np.zeros((128, 5, 128), np.float32)
    c[:, 0, :] = np.eye(128, dtype=np.float32)
    j = np.arange(128)[:, None]
    i = np.arange(128)[None, :]
    c[:, 1, :] = (j <= i).astype(np.float32)
    c[:, 2, :] = np.where(i < j, -30000.0, 0.0).astype(np.float32)
    c[:, 3, :] = 1.0
    return c


def kernel(**inp):
    f = lambda a: np.ascontiguousarray(np.asarray(a, dtype=np.float32))
    if "nc" not in _CACHE:
        _CACHE["nc"] = build()
    nc = _CACHE["nc"]
    bc = lambda v: np.ascontiguousarray(np.broadcast_to(f(v).reshape(1, -1), (128, f(v).size)))
    nrm_bc = np.stack([bc(inp["norm_mix_w"][0]), bc(inp["norm_ffn_w"][0]), bc(inp["norm_final_w"]), bc(inp["norm_mem_w"][0])], 1)
    scw_c = np.ascontiguousarray(f(inp["sc_conv_w"][0]).reshape(3, 8, 128).transpose(2, 1, 0))
    cw_c = np.ascontiguousarray(f(inp["ssm_conv_w"][0]).reshape(4, 24, 128).transpose(2, 1, 0))
    cb_c = np.ascontiguousarray(f(inp["ssm_conv_b"][0]).reshape(24, 128).T)
    nw_c = np.ascontiguousarray(f(inp["ssm_norm_w"][0]).reshape(16, 128).T)
    hp_bc = np.stack([bc(inp["ssm_dt_bias"][0]), bc(inp["ssm_a_log"][0]), bc(inp["ssm_d"][0])], 1)
    shared = {
        "w_in": f(inp["w_in"][0]), "w_sc_out": f(inp["w_sc_out"][0]), "w_ssm_out": f(inp["w_ssm_out"][0]),
        "w_mem_k": f(inp["w_mem_k"][0]), "w_mem_v": f(inp["w_mem_v"][0]), "w_attn_o": f(inp["w_attn_o"][0]),
        "w_merge_o": f(inp["w_merge_o"][0]), "w_ffn_gate": f(inp["w_ffn_gate"][0]), "w_ffn_up": f(inp["w_ffn_up"][0]),
        "w_ffn_down": f(inp["w_ffn_down"][0]),
        "nrm_bc": np.ascontiguousarray(nrm_bc), "scw_c": scw_c, "cw_c": cw_c, "cb_c": cb_c, "nw_c": nw_c,
        "hp_bc": np.ascontiguousarray(hp_bc), "cst": _consts(),
        "Dc_in": np.ascontiguousarray(np.repeat(f(inp["ssm_d"][0]), 64).reshape(16, 128).T),
    }
    import os
    NCORE = int(os.environ.get('NCORE', '8'))
    in_maps = []
    for c in range(NCORE):
        m = dict(shared)
        m["x_prompt"] = f(inp["x_prompt"][c])
        m["mem_prompt"] = f(inp["mem_prompt"][c])
        sl = slice(c * NSB, (c + 1) * NSB)
        m["x_sample"] = f(inp["x_sample"][sl, 0])
        m["cache_k"] = f(inp["cache_mem_k"][0, sl]).reshape(NSB, 256, D)
        m["cache_v"] = f(inp["cache_mem_v"][0, sl]).reshape(NSB, 256, D)
        m["st_conv"] = f(inp["state_conv"][0, sl])
        m["st_sconv"] = f(inp["state_ssm_conv"][0, sl])
        m["st_ssm"] = f(inp["state_ssm"][0, sl]).reshape(NSB, 2048, 128)
        in_maps.append(m)
    res = run_bass_kernel_spmd(nc, in_maps, core_ids=list(range(NCORE)))
    R = list(res.results) + [res.results[0]] * (8 - NCORE)
    y_prompt = np.stack([R[c]["y_prompt"] for c in range(8)], 0)
    pmk = np.stack([R[c]["pmk"].reshape(256, 4, 256) for c in range(8)], 0)[None]
    pmv = np.stack([R[c]["pmv"].reshape(256, 4, 256) for c in range(8)], 0)[None]
    pconv = np.stack([R[c]["pconv"].transpose(2, 1, 0).reshape(2, 1024) for c in range(8)], 0)[None]
    pssmc = np.stack([R[c]["pssmc"].transpose(2, 1, 0).reshape(3, 3072) for c in range(8)], 0)[None]
    pssm = np.stack([R[c]["pssm"].reshape(32, 64, 128) for c in range(8)], 0)[None]
    y_sample = np.concatenate([R[c]["y_sample"] for c in range(8)], 0)[:, None, :]
    sconv = np.concatenate([R[c]["sconv"] for c in range(8)], 0)[None]
    sssmc = np.concatenate([R[c]["sssmc"] for c in range(8)], 0)[None]
    sssm = np.concatenate([R[c]["sssm"].reshape(NSB, 32, 64, 128) for c in range(8)], 0)[None]
    return (y_prompt, y_sample, pmk, pmv, pconv, pssmc, pssm, sconv, sssmc, sssm)
```

```python
import contextlib
import numpy as np
import concourse.bass as bass
import concourse.mybir as mybir
from concourse.bass_utils import run_bass_kernel_spmd

F32 = mybir.dt.float32
BF16 = mybir.dt.bfloat16
AF = mybir.ActivationFunctionType
ALU = mybir.AluOpType
AX = mybir.AxisListType.X

D = 1024
SEQ = 2048
T = 512
NT = SEQ // T
NSB = 16
FF = 2816
EPS = 1e-6
C_SCB, C_SCC, C_SCX, C_Z, C_XBC, C_DT, C_Q, C_G = 0, 1024, 2048, 3072, 5120, 8192, 8224, 9248
INC = 12320
ENGS = ["pe", "act", "dve", "pool", "sp"]
WINDOW = 6


class Op:
    __slots__ = ("idx", "eng", "fn", "deps", "signal", "sem", "val", "pos", "slot", "is_dma")


class Sched:
    def __init__(self, nc):
        self.nc = nc
        self.ops = []
        self.by_eng = {e: [] for e in ENGS}
        self.lastw = {}
        self.readers = {}
        self.slot_last = {}
        self.slot_count = {}
        self.auto_reads = []

    def _chan(self, o):
        return ("dma", o.slot) if o.is_dma else o.eng

    def add(self, eng, fn, reads=(), writes=(), slot=None):
        op = Op()
        op.idx = len(self.ops)
        op.eng = eng
        op.fn = fn
        op.slot = slot
        op.is_dma = slot is not None
        op.signal = False
        op.sem = None
        op.val = 0
        op.pos = len(self.by_eng[eng])
        deps = {}
        if self.auto_reads:
            reads = list(reads) + self.auto_reads
        extra = [("psx", k[1]) for k in reads if isinstance(k, tuple) and k[0] == "ps"]
        if extra:
            writes = list(writes) + extra

        def dep(o):
            if o is None:
                return
            ch = self._chan(o)
            if ch not in deps or deps[ch].idx < o.idx:
                deps[ch] = o

        for k in reads:
            dep(self.lastw.get(k))
        for k in writes:
            dep(self.lastw.get(k))
            for o in self.readers.get(k, {}).values():
                dep(o)
        if op.is_dma:
            dep(self.slot_last.get(slot))
            self.slot_last[slot] = op
            self.slot_count[slot] = self.slot_count.get(slot, 0) + 1
            op.val = 16 * self.slot_count[slot]
        final = []
        for ch, o in deps.items():
            if (not o.is_dma) and o.eng == eng and not op.is_dma:
                if eng == "pe":
                    continue
                if op.pos - o.pos > WINDOW:
                    continue
            o.signal = True
            final.append(o)
        op.deps = final
        ch = self._chan(op)
        for k in reads:
            self.readers.setdefault(k, {})[ch] = op
        for k in writes:
            self.lastw[k] = op
            self.readers[k] = {}
        self.ops.append(op)
        self.by_eng[eng].append(op)
        return op

    def emit(self, stack):
        nc = self.nc
        EPOCH = 2000
        ssem = {}
        for s in self.slot_count:
            ssem[s] = stack.enter_context(nc.semaphore("d_%d" % len(ssem)))
        for e in ENGS:
            c = 0
            cur = None
            for op in self.by_eng[e]:
                if op.is_dma:
                    op.sem = ssem[op.slot]
                elif op.signal:
                    if c % EPOCH == 0:
                        cur = stack.enter_context(nc.semaphore("c_%s_%d" % (e, c // EPOCH)))
                    op.sem = cur
                    op.val = c % EPOCH + 1
                    c += 1
        block = stack.enter_context(nc.Block())

        def run(e, eng):
            water = {}
            for op in self.by_eng[e]:
                need = {}
                for o in op.deps:
                    k = o.sem.num
                    if water.get(k, 0) >= o.val:
                        continue
                    if k not in need or need[k][1] < o.val:
                        need[k] = (o.sem, o.val)
                for k, (s, v) in need.items():
                    eng.wait_ge(s, v)
                    water[k] = v
                ins = op.fn(eng)
                if op.is_dma:
                    ins.then_inc(op.sem, 16)
                elif op.signal:
                    ins.then_inc(op.sem, 1)
            if e == "sp":
                for s, cnt in self.slot_count.items():
                    eng.wait_ge(ssem[s], 16 * cnt)

        @block.tensor
        def _(eng):
            run("pe", eng)

        @block.scalar
        def _(eng):
            run("act", eng)

        @block.vector
        def _(eng):
            run("dve", eng)

        @block.gpsimd
        def _(eng):
            run("pool", eng)

        @block.sync
        def _(eng):
            run("sp", eng)


def fap(ap, dims, off=0):
    return bass.AP(tensor=ap.tensor, offset=ap.offset + off, ap=[list(ap.ap[0])] + [list(d) for d in dims])


def build(with_sample=True, dbg=None, ntiles=NT, branches='ACB', stop=0):
    nc = bass.Bass("TRN2", target_bir_lowering=False)
    dram = {}

    def din(name, shape):
        dram[name] = nc.dram_tensor(name, list(shape), F32, kind="ExternalInput").ap()
        return dram[name]

    def dout(name, shape):
        dram[name] = nc.dram_tensor(name, list(shape), F32, kind="ExternalOutput").ap()
        return dram[name]

    xin = din("x_prompt", [SEQ, D])
    memin = din("mem_prompt", [256, D])
    w_in = din("w_in", [D, INC])
    w_sc_out = din("w_sc_out", [D, D])
    w_ssm_out = din("w_ssm_out", [2048, D])
    w_mem_k = din("w_mem_k", [D, D])
    w_mem_v = din("w_mem_v", [D, D])
    w_attn_o = din("w_attn_o", [D, D])
    w_merge_o = din("w_merge_o", [D, D])
    w_gate = din("w_ffn_gate", [D, FF])
    w_up = din("w_ffn_up", [D, FF])
    w_down = din("w_ffn_down", [FF, D])
    nrm_bc = din("nrm_bc", [128, 4, D])
    scw_c = din("scw_c", [128, 8, 3])
    cw_c = din("cw_c", [128, 24, 4])
    cb_c = din("cb_c", [128, 24])
    nw_c = din("nw_c", [128, 16])
    hp_bc = din("hp_bc", [128, 3, 32])
    cst = din("cst", [128, 5, 128])

    xs_in = din("x_sample", [NSB, D])
    ck_in = din("cache_k", [NSB, 256, D])
    cv_in = din("cache_v", [NSB, 256, D])
    st_conv = din("st_conv", [NSB, 2, D])
    st_sconv = din("st_sconv", [NSB, 3, 3072])
    st_ssm = din("st_ssm", [NSB, 2048, 128])
    Dc_in = din("Dc_in", [128, 16])
    y_sample = dout("y_sample", [NSB, D])
    sconv = dout("sconv", [NSB, 2, D])
    sssmc = dout("sssmc", [NSB, 3, 3072])
    sssm = dout("sssm", [NSB, 2048, 128])
    y_prompt = dout("y_prompt", [SEQ, D])
    pmk = dout("pmk", [256, D])
    pmv = dout("pmv", [256, D])
    pconv = dout("pconv", [128, 8, 2])
    pssmc = dout("pssmc", [128, 24, 3])
    pssm = dout("pssm", [2048, 128])
    if dbg:
        dbg_out = {k: dout("dbg_" + k, shp) for k, shp in dbg.items()}

    st = contextlib.ExitStack()
    E = st.enter_context

    def sb(name, shape, dt=F32):
        return E(nc.sbuf_tensor(name, list(shape), dt))

    S = Sched(nc)
    stop_mode = stop
    X = sb("X", [128, 4, D])
    MG = sb("MG", [128, 8, T])
    G1 = sb("G1", [128, 8, T], BF16)
    G2 = sb("G2", [128, 8, T], BF16)
    G3 = sb("G3", [128, 8, T], BF16)
    HB = sb("HB", [128, 16384], BF16)
    NPAGE = 16
    PAGE = 1024
    WRb = sb("WRb", [128, NPAGE * PAGE], BF16)
    WR = [WRb]
    page_owner = {}
    nrm = sb("nrm", [128, D])
    scw = sb("scw", [128, 8, 3])
    cw = sb("cw", [128, 24, 4])
    cb = sb("cb", [128, 24])
    nw = sb("nw", [128, 16])
    hp = sb("hp", [128, 3, 32])
    cstf = sb("cstf", [128, 5, 128])
    cstb = sb("cstb", [128, 5, 128], BF16)
    abc = sb("abc", [128, 32])
    xnb = [sb("xnb0", [128, D], BF16)] * 2
    junk = sb("junk", [128, D], BF16)
    stat = sb("stat", [128, 64])
    histA = sb("histA", [128, 8, 2], BF16)
    histB = sb("histB", [128, 24, 3], BF16)
    pcv = sb("pcv", [128, 8, 2])
    pscv = sb("pscv", [128, 24, 3])
    ub = [sb("ub%d" % i, [128, 2 + T], BF16) for i in range(2)]
    xb = [sb("xb%d" % i, [128, 3 + T], BF16) for i in range(2)]
    dg = [sb("dg%d" % i, [128, 4, 128], BF16) for i in range(2)]
    sA = [sb("sA%d" % i, [128, T]) for i in range(2)]
    sB = [sb("sB%d" % i, [128, T]) for i in range(2)]
    sC = [sb("sC%d" % i, [128, T], BF16) for i in range(2)]
    KT = sb("KT", [128, 8, 256], BF16)
    Vb = sb("Vb", [128, 2, D], BF16)
    ostg = [sb("ostg%d" % i, [128, D]) for i in range(2)]
    BT = sb("BT", [128, 4, T], BF16)
    CT = sb("CT", [128, 4, T], BF16)
    Btok = sb("Btok", [128, 4, 4, 128], BF16)
    Sf = sb("Sf", [128, 2048])
    Sbf = sb("Sbf", [128, 2048], BF16)
    dtt = sb("dtt", [128, 4, 32])
    lndt = sb("lndt", [128, 4, 32])
    dtab = sb("dtab", [128, 4, 32], BF16)
    acs = sb("acs", [128, 64])
    eac = sb("eac", [128, 64])
    dsd = sb("dsd", [128, 32])
    nbias = sb("nbias", [128, 32])
    nbh4 = sb("nbh4", [128, 4, 32], BF16)
    nbl4 = sb("nbl4", [128, 4, 32], BF16)
    xdd = sb("xdd", [128, 2048], BF16)
    xddg = [xdd[:, 0:512], xdd[:, 512:1024]]
    CBs = sb("CBs", [128, 4, 128], BF16)
    Lt = [sb("Lt%d" % i, [128, 4, 128], BF16) for i in range(2)]
    MT = [sb("MT%d" % i, [128, 4, 128], BF16) for i in range(2)]
    Lt.append(xdd[:, 1024:1536].rearrange("p (a b) -> p a b", a=4))
    MT.append(xdd[:, 1536:2048].rearrange("p (a b) -> p a b", a=4))
    yb = [sb("yb%d" % i, [128, 512]) for i in range(2)]
    ynb = [sb("ynb%d" % i, [128, 512], BF16) for i in range(2)]
    sm = sb("sm", [128, 16])
    PS = [E(nc.psum_tensor("ps%d" % i, [128, 512], F32)) for i in range(8)]

    ident_b = cstb[:, 0, :]
    tri_b = cstb[:, 1, :]
    mneg_b = cstb[:, 2, :]
    ones_b = cstb[:, 3, :]
    ident_f = cstf[:, 0, :]

    PT = HB[:, 0:4096].rearrange("p (h m t) -> p h m t", h=4, m=2)
    zs_v = HB[:, 0:8192].rearrange("p (j c) -> p j c", j=4)
    xs_v = HB[:, 8192:16384].rearrange("p (j c) -> p j c", j=4)
    mT_v = HB[:, 0:4096].rearrange("p (k t) -> p k t", k=8)
    hT_v = HB[:, 4096:4096 + 22 * T].rearrange("p (k t) -> p k t", k=22)

    cnt = {"ps": 0, "w": 0, "misc": 0, "rr": 0}

    def nb():
        i = cnt["ps"] % 8
        cnt["ps"] += 1
        return i

    def mslot():
        cnt["misc"] += 1
        return ("m", cnt["misc"] % 12)

    def ev_eng():
        cnt["rr"] += 1
        return "act" if cnt["rr"] % 2 else "dve"

    def dma_in(dst, src, wkey):
        S.add("sp", lambda g: g.dma_start(out=dst, in_=src), writes=[wkey], slot=mslot())

    def dma_out(dst, src, rkeys, slot=None):
        S.add("sp", lambda g: g.dma_start(out=dst, in_=src), reads=rkeys, slot=slot or mslot())

    def wload(Wap, r0, kch, c0, ncols):
        n = kch * ncols
        npg = (n + PAGE - 1) // PAGE
        start = cnt["w"]
        if start + npg > NPAGE:
            start = 0
        cnt["w"] = (start + npg) % NPAGE
        cnt["wser"] = cnt.get("wser", 0) + 1
        ser = cnt["wser"]
        prev = set()
        for p in range(start, start + npg):
            if p in page_owner:
                prev.add(page_owner[p])
            page_owner[p] = ser
        dst = WRb[:, start * PAGE:start * PAGE + n].rearrange("p (k c) -> p k c", k=kch)
        src = Wap[r0:r0 + kch * 128, c0:c0 + ncols].rearrange("(k p) c -> p k c", p=128)
        S.add("pool", lambda g: g.dma_start(out=dst, in_=src), writes=[("wl", ser)] + [("wl", q) for q in prev], slot=("w", start))
        return dst, ("wl", ser)

    def mm(out, lhsT, rhs, start, stop, reads, pskey):
        if stop_mode == 5:
            return
        S.add("pe", lambda g: g.matmul(out, lhsT, rhs, start=start, stop=stop), reads=reads, writes=[pskey])

    def tr(out, in_, idn, reads, pskey):
        S.add("pe", lambda g: g.transpose(out, in_, idn), reads=reads, writes=[pskey])

    def act(out, in_, func, reads, writes, bias=None, scale=1.0, accum=None):
        kw = {}
        if bias is not None:
            kw["bias"] = bias
        if accum is not None:
            kw["accum_out"] = accum
        S.add("act", lambda g: g.activation(out=out, in_=in_, func=func, scale=scale, **kw), reads=reads, writes=writes)

    def tt(eng, out, in0, in1, op, reads, writes):
        S.add(eng, lambda g: g.tensor_tensor(out=out, in0=in0, in1=in1, op=op), reads=reads, writes=writes)

    def ts(eng, out, in0, s1, s2, op0, op1, reads, writes):
        if op1 is None:
            S.add(eng, lambda g: g.tensor_scalar(out=out, in0=in0, scalar1=s1, scalar2=None, op0=op0), reads=reads, writes=writes)
        else:
            S.add(eng, lambda g: g.tensor_scalar(out=out, in0=in0, scalar1=s1, scalar2=s2, op0=op0, op1=op1), reads=reads, writes=writes)

    def cp(eng, out, in_, reads, writes):
        if eng == "act":
            act(out, in_, AF.Copy, reads, writes)
        else:
            S.add(eng, lambda g: g.tensor_copy(out, in_), reads=reads, writes=writes)

    def stt(out, in0, scalar, in1, op0, op1, reads, writes):
        S.add("dve", lambda g: g.scalar_tensor_tensor(out=out, in0=in0, scalar=scalar, in1=in1, op0=op0, op1=op1), reads=reads, writes=writes)

    def memset(eng, ap, val, wkeys):
        S.add(eng, lambda g: g.memset(ap, val), writes=wkeys)

    def rstd_from_ssq(col_ssq, col_out, n, P, key):
        a = stat[0:P, col_ssq:col_ssq + 1]
        o = stat[0:P, col_out:col_out + 1]
        ts("dve", o, a, 1.0 / n, EPS, ALU.mult, ALU.add, [key], [key])
        act(o, o, AF.Ln, [key], [key])
        act(o, o, AF.Exp, [key], [key], scale=-0.5)

    dma_in(scw[:], scw_c[:, :, :], "scw")
    dma_in(cw[:], cw_c[:, :, :], "cw")
    dma_in(cb[:], cb_c[:, :], "cb")
    dma_in(nw[:], nw_c[:, :], "nw")
    dma_in(hp[:], hp_bc[:, :, :], "hp")
    dma_in(cstf[:], cst[:, :, :], "cstf")
    cp("dve", cstb[:], cstf[:], ["cstf"], ["cstb"])
    act(abc[:], hp[:, 1, :], AF.Exp, ["hp"], ["abc"])
    ts("dve", abc[:], abc[:], -1.0, None, ALU.mult, None, ["abc"], ["abc"])
    memset("dve", histA[:], 0.0, ["histA"])
    memset("dve", histB[:], 0.0, ["histB"])
    memset("dve", Sf[:], 0.0, ["Sf"])
    memset("dve", Sbf[:], 0.0, ["Sbf"])

    def finish():
        S.emit(st)
        st.close()
        return nc
    if stop == 1:
        return finish()
    def norm_transpose(src3, P, nsub, wrow, dstT, tag, Tn, load=True):
        if load:
            dma_in(nrm[:], nrm_bc[:, wrow, :], "nrm")
        xn2 = [(xnb[0], ("xnb", 0)), (ostg[1][:].bitcast(BF16)[:, 0:D], ("ostg", 1))]
        for j in range(nsub):
            act(junk[0:P, :], src3[0:P, j, :], AF.Square, [(tag, j)], ["junk", ("nst", j)], accum=stat[0:P, 48 + j:49 + j])
        for j in range(nsub):
            ts("dve", stat[0:P, 52 + j:53 + j], stat[0:P, 48 + j:49 + j], 1.0 / D, EPS, ALU.mult, ALU.add, [("nst", j)], [("nst", j)])
        for j in range(nsub):
            act(stat[0:P, 52 + j:53 + j], stat[0:P, 52 + j:53 + j], AF.Ln, [("nst", j)], [("nst", j)])
        for j in range(nsub):
            act(stat[0:P, 52 + j:53 + j], stat[0:P, 52 + j:53 + j], AF.Exp, [("nst", j)], [("nst", j)], scale=-0.5)
        for j in range(nsub):
            xn, xkey = xn2[j % 2]
            stt(xn[0:P, :], src3[0:P, j, :], stat[0:P, 52 + j:53 + j], nrm[0:P, :], ALU.mult, ALU.mult,
                [(tag, j), ("nst", j), "nrm"], [xkey])
            b = nb()
            pT = PS[b][:].bitcast(BF16)
            for k in range(8):
                tr(pT[:, k * P:(k + 1) * P] if P == 128 else pT[:, k * 128:k * 128 + P], xn[0:P, k * 128:(k + 1) * 128], ident_b[0:P, 0:P],
                   [xkey, "cstb"], ("ps", b))
            if P == 128:
                cp(ev_eng(), dstT[:, :, j * 128:(j + 1) * 128], pT[:, 0:1024].rearrange("p (k t) -> p k t", k=8),
                   [("ps", b)], [("G1", "all")])
            else:
                cp(ev_eng(), dstT[:, :, 0:P], fap(pT[:, 0:1], [[128, 8], [1, P]]), [("ps", b)], [("G1", "all")])

    def proj_fm(Wap, r0, kch, c0, nblk, srcT, Tn, src_keys, consume, blk_per_slab=4):
        blk = 0
        while blk < nblk:
            nbk = min(blk_per_slab, nblk - blk)
            slab, wk = wload(Wap, r0, kch, c0 + blk * 128, nbk * 128)
            for bb in range(nbk):
                b = nb()
                for k in range(kch):
                    mm(PS[b][:, 0:Tn], slab[:, k, bb * 128:(bb + 1) * 128], srcT[:, k, 0:Tn], k == 0, k == kch - 1,
                       [wk] + src_keys, ("ps", b))
                consume(blk + bb, b)
            blk += nbk

    MEMX = X
    for j in range(2):
        dma_in(X[:, j, :], memin[j * 128:(j + 1) * 128, :], ("X", j))
    norm_transpose(X, 128, 2, 3, G1, "X", 256)
    if stop == 2:
        return finish()
    if stop == 6:
        cp("act", ostg[0][:, 0:512], PS[5][:, :], [("ps", 5)], [("ostg", 0)])
        cp("dve", Vb[:, 0, 0:512], PS[5][:, :], [("ps", 5)], ["Vb"])
        return finish()
    G1k = [("G1", "all")]
    import os
    PARTS = os.environ.get("KV_PARTS", "WMCOK")
    KVW = os.environ.get("KV_WHICH", "kv")
    KVN = int(os.environ.get("KV_NCBK", "2"))
    KVM = int(os.environ.get("KV_NMC", "2"))
    for which, Wm, outd in (("k", w_mem_k, pmk), ("v", w_mem_v, pmv)):
        if which not in KVW:
            continue
        for cbk in range(KVN):
            if "W" in PARTS:
                slab, wk = wload(Wm, 0, 8, cbk * 512, 512)
            else:
                slab, wk = WRb[:, 0:4096].rearrange("p (k c) -> p k c", k=8), ("wl", 0)
            for mc in range(KVM):
                b = nb()
                if "M" in PARTS:
                    for k in range(8):
                        mm(PS[b][:, :], G1[:, k, mc * 128:(mc + 1) * 128], slab[:, k, :], k == 0, k == 7, [wk] + G1k, ("ps", b))
                so = ostg[cnt["rr"] % 2]
                sk = ("ostg", cnt["rr"] % 2)
                cnt["rr"] += 1
                if "C" in PARTS:
                    if os.environ.get("KV_VB", "dve") != "dve_only":
                        cp("act", so[:, 0:512], PS[b][:, :], [("ps", b)], [sk])
                    if which == "v":
                        vbm = os.environ.get("KV_VB", "dve")
                        if vbm in ("dve", "dve_only"):
                            cp("dve", Vb[:, mc, cbk * 512:(cbk + 1) * 512], PS[b][:, :], [("ps", b)], ["Vb"])
                        elif vbm == "act":
                            cp("act", Vb[:, mc, cbk * 512:(cbk + 1) * 512], PS[b][:, :], [("ps", b)], ["Vb"])
                        elif vbm == "dve_sb":
                            cp("dve", Vb[:, mc, cbk * 512:(cbk + 1) * 512], so[:, 0:512], [sk], ["Vb"])
                        elif vbm == "dve_half":
                            cp("dve", Vb[:, mc, cbk * 512:cbk * 512 + 256], PS[b][:, 0:256], [("ps", b)], ["Vb"])
                if "O" in PARTS:
                    dma_out(outd[mc * 128:(mc + 1) * 128, cbk * 512:(cbk + 1) * 512], so[:, 0:512], [sk], slot=sk)
            if which == "k" and "K" in PARTS:
                for bb in range(4):
                    b = nb()
                    if "M" in PARTS:
                        for k in range(8):
                            mm(PS[b][:, 0:256], slab[:, k, bb * 128:(bb + 1) * 128], G1[:, k, 0:256], k == 0, k == 7, [wk] + G1k, ("ps", b))
                    cp(ev_eng(), KT[:, cbk * 4 + bb, :], PS[b][:, 0:256], [("ps", b)], ["KT"])
    if stop == 7:
        return finish()

    xpref = {"done": False}

    def tile(ti, last):
        Tn = T
        P = 128
        NS = 4
        t0 = ti * T
        xk = [("G1", "all")]
        if not xpref["done"]:
            for j in range(NS):
                dma_in(X[:, j, :], xin[t0 + j * 128:t0 + (j + 1) * 128, :], ("X", j))
        norm_transpose(X, 128, NS, 0, G1, "X", Tn, load=not xpref["done"])
        xpref["done"] = False
        dma_in(nrm[:], nrm_bc[:, 1, :], "nrm")

        def gated_merge(bname, mode):
            pass

        def out_proj_and_merge(Wo, kch, srcT, src_keys, gate_c0, mode, bps):
            ybank = {}

            def cons_y(blk, b):
                ybank[blk] = b
                slab, wk = wload(w_in, 0, 8, gate_c0 + blk * 128, 128)
                bg = nb()
                for k in range(8):
                    mm(PS[bg][:, 0:Tn], slab[:, k, 0:128], G1[:, k, 0:Tn], k == 0, k == 7, [wk] + xk, ("ps", bg))
                i = blk % 2
                act(sA[i][:, 0:Tn], PS[bg][:, 0:Tn], AF.Sigmoid, [("ps", bg)], [("sA", i)])
                if mode == "first":
                    tt("dve", MG[:, blk, 0:Tn], sA[i][:, 0:Tn], PS[b][:, 0:Tn], ALU.mult, [("sA", i), ("ps", b)], [("MG", blk)])
                elif mode == "mid":
                    tt("dve", sB[i][:, 0:Tn], sA[i][:, 0:Tn], PS[b][:, 0:Tn], ALU.mult, [("sA", i), ("ps", b)], [("sB", i)])
                    tt("dve", MG[:, blk, 0:Tn], MG[:, blk, 0:Tn], sB[i][:, 0:Tn], ALU.add, [("sB", i), ("MG", blk)], [("MG", blk)])
                else:
                    tt("dve", sB[i][:, 0:Tn], sA[i][:, 0:Tn], PS[b][:, 0:Tn], ALU.mult, [("sA", i), ("ps", b)], [("sB", i)])
                    tt("dve", mT_v[:, blk, 0:Tn], MG[:, blk, 0:Tn], sB[i][:, 0:Tn], ALU.add,
                       [("sB", i), ("MG", blk), "HBphase2"], [("mT", blk)])

            proj_fm(Wo, 0, kch, 0, 8, srcT, Tn, src_keys, cons_y, blk_per_slab=bps)

        def branchA():
            pend = [None]
            for c in range(8):
                i = c % 2
                slc, wkc = wload(w_in, 0, 8, C_SCC + c * 128, 128)
                slx, wkx = wload(w_in, 0, 8, C_SCX + c * 128, 128)
                slb, wkb = wload(w_in, 0, 8, C_SCB + c * 128, 128)
                b1 = nb()
                for k in range(8):
                    mm(PS[b1][:, 0:Tn], slc[:, k, :], G1[:, k, 0:Tn], k == 0, k == 7, [wkc] + xk, ("ps", b1))
                act(sA[i][:, 0:Tn], PS[b1][:, 0:Tn], AF.Copy, [("ps", b1)], [("sA", i)])
                b2 = nb()
                for k in range(8):
                    mm(PS[b2][:, 0:Tn], slx[:, k, :], G1[:, k, 0:Tn], k == 0, k == 7, [wkx] + xk, ("ps", b2))
                b4 = nb()
                for k in range(8):
                    mm(PS[b4][:, 0:Tn], slb[:, k, :], G1[:, k, 0:Tn], k == 0, k == 7, [wkb] + xk, ("ps", b4))
                u = ub[i]
                cp("dve", u[:, 0:2], histA[:, c, :], ["histA"], [("ub", i)])
                tt("dve", u[:, 2:2 + Tn], sA[i][:, 0:Tn], PS[b2][:, 0:Tn], ALU.mult, [("sA", i), ("ps", b2)], [("ub", i)])
                cp("dve", histA[:, c, :], u[:, Tn:Tn + 2], [("ub", i)], ["histA"])
                if last:
                    tt("dve", pcv[:, c, :], sA[i][:, Tn - 2:Tn], PS[b2][:, Tn - 2:Tn], ALU.mult, [("sA", i), ("ps", b2)], ["pcv"])
                for k3 in range(3):
                    ts("dve", dg[i][:, k3, :], ident_b, scw[:, c, k3:k3 + 1], None, ALU.mult, None, ["cstb", "scw"], [("dg", i)])
                act(sB[i][:, 0:Tn], PS[b4][:, 0:Tn], AF.Copy, [("ps", b4)], [("sB", i)])

                def part2(c=c, i=i, u=u):
                    b3 = nb()
                    for k3 in range(3):
                        mm(PS[b3][:, 0:Tn], dg[i][:, k3, :], u[:, k3:k3 + Tn], k3 == 0, k3 == 2, [("dg", i), ("ub", i)], ("ps", b3))
                    tt("dve", G2[:, c, 0:Tn], sB[i][:, 0:Tn], PS[b3][:, 0:Tn], ALU.mult, [("sB", i), ("ps", b3)], [("G2", c)])
                if pend[0]:
                    pend[0]()
                pend[0] = part2
            pend[0]()
            out_proj_and_merge(w_sc_out, 8, G2, [("G2", c) for c in range(8)], C_G, "first", 4)

        def branchC():
            def cons_q(blk, b):
                act(G2[:, blk, 0:Tn], PS[b][:, 0:Tn], AF.Copy, [("ps", b)], [("G2", blk)], scale=1.0 / 16.0)
            proj_fm(w_in, 0, 8, C_Q, 8, G1, Tn, xk, cons_q)
            pbufs = [(sC[0][:, 0:512].rearrange("p (h m) -> p h m", h=2), ("sC", 0)), (sC[1][:, 0:512].rearrange("p (h m) -> p h m", h=2), ("sC", 1))]
            pendc = [None]
            itc = 0
            for j in range(NS):
                for hpair in range(2):
                    pb, pbk = pbufs[itc % 2]
                    so = (itc % 2) * 8
                    itc += 1
                    b = nb()
                    for hh in range(2):
                        h = hpair * 2 + hh
                        for dc in range(2):
                            mm(PS[b][:, hh * 256:(hh + 1) * 256], G2[:, 2 * h + dc, j * 128:(j + 1) * 128], KT[:, 2 * h + dc, :],
                               dc == 0, dc == 1, [("G2", 2 * h + dc), "KT"], ("ps", b))
                    smk = ("sm", so)
                    S.add("dve", lambda g, b=b, so=so: g.tensor_reduce(out=sm[:, so:so + 2], in_=PS[b][:, :].rearrange("p (h m) -> p h m", h=2),
                                                                      op=ALU.max, axis=AX), reads=[("ps", b)], writes=[smk])
                    ts("dve", sm[:, so + 2:so + 4], sm[:, so:so + 2], -1.0, None, ALU.mult, None, [smk], [smk])
                    for hh in range(2):
                        act(pb[:, hh, :], PS[b][:, hh * 256:(hh + 1) * 256], AF.Exp, [("ps", b), smk], [pbk, smk],
                            bias=sm[:, so + 2 + hh:so + 3 + hh], accum=sm[:, so + 4 + hh:so + 5 + hh])
                    S.add("dve", lambda g, so=so: g.reciprocal(sm[:, so + 6:so + 8], sm[:, so + 4:so + 6]), reads=[smk], writes=[smk])
                    for hh in range(2):
                        ts("dve", pb[:, hh, :], pb[:, hh, :], sm[:, so + 6 + hh:so + 7 + hh], None, ALU.mult, None, [pbk, smk], [pbk])

                    def trpart(j=j, hpair=hpair, pb=pb, pbk=pbk):
                        bt = nb()
                        pT = PS[bt][:].bitcast(BF16)
                        for hh in range(2):
                            for mc in range(2):
                                tr(pT[:, (hh * 2 + mc) * 128:(hh * 2 + mc + 1) * 128], pb[:, hh, mc * 128:(mc + 1) * 128], ident_b,
                                   [pbk, "cstb"], ("ps", bt))
                        cp(ev_eng(), PT[:, hpair * 2:hpair * 2 + 2, :, j * 128:(j + 1) * 128],
                           pT[:, 0:512].rearrange("p (h m t) -> p h m t", h=2, m=2), [("ps", bt), "HBphase1"], [("PT", j, hpair)])
                    if pendc[0]:
                        pendc[0]()
                    pendc[0] = trpart
            pendc[0]()
            ptk = [("PT", j, hpv) for j in range(NS) for hpv in range(2)]
            for h in range(4):
                for dblk in range(2):
                    b = nb()
                    for mc in range(2):
                        mm(PS[b][:, 0:Tn], Vb[:, mc, h * 256 + dblk * 128:h * 256 + (dblk + 1) * 128], PT[:, h, mc, 0:Tn],
                           mc == 0, mc == 1, ["Vb"] + ptk, ("ps", b))
                    cp(ev_eng(), G3[:, 2 * h + dblk, 0:Tn], PS[b][:, 0:Tn], [("ps", b)], [("G3", 2 * h + dblk)])
            S.add("dve", lambda g: g.memset(stat[:, 22:23], 0.0), writes=ptk + ["HBphase1"])
            out_proj_and_merge(w_attn_o, 8, G3, [("G3", c) for c in range(8)], C_G + 2048, "mid", 4)

        def branchB():
            for cbk in range(4):
                slab, wk = wload(w_in, 0, 8, C_Z + cbk * 512, 512)
                for j in range(NS):
                    b = nb()
                    for k in range(8):
                        mm(PS[b][:, :], G1[:, k, j * 128:(j + 1) * 128], slab[:, k, :], k == 0, k == 7, [wk] + xk, ("ps", b))
                    act(zs_v[:, j, cbk * 512:(cbk + 1) * 512], PS[b][:, :], AF.Silu, [("ps", b), "HBphase1"], [("zs", j, cbk)])
            slab, wk = wload(w_in, 0, 8, C_DT, 32)
            for j in range(NS):
                b = nb()
                for k in range(8):
                    mm(PS[b][:, 0:32], G1[:, k, j * 128:(j + 1) * 128], slab[:, k, :], k == 0, k == 7, [wk] + xk, ("ps", b))
                tt("dve", dtt[:, j, :], PS[b][:, 0:32], hp[:, 0, :], ALU.add, [("ps", b), "hp"], [("dt", j)])
                act(dtt[:, j, :], dtt[:, j, :], AF.Exp, [("dt", j)], [("dt", j)])
                act(dtt[:, j, :], dtt[:, j, :], AF.Ln, [("dt", j)], [("dt", j)], bias=1.0)
                act(lndt[:, j, :], dtt[:, j, :], AF.Ln, [("dt", j)], [("lndt", j)])
                tt("dve", dtab[:, j, :], dtt[:, j, :], abc[:], ALU.mult, [("dt", j), "abc"], [("dtab", j)])
            pq = []

            def cons_xbc(cc, b):
                i = cc % 2
                xbuf = xb[i]
                cp("dve", xbuf[:, 0:3], histB[:, cc, :], ["histB"], [("xb", i)])
                act(xbuf[:, 3:3 + Tn], PS[b][:, 0:Tn], AF.Copy, [("ps", b)], [("xb", i)])
                cp("dve", histB[:, cc, :], xbuf[:, Tn:Tn + 3], [("xb", i)], ["histB"])
                if last:
                    cp("dve", pscv[:, cc, :], PS[b][:, Tn - 3:Tn], [("ps", b)], ["pscv"])
                for k4 in range(4):
                    ts("dve", dg[i][:, k4, :], ident_b, cw[:, cc, k4:k4 + 1], None, ALU.mult, None, ["cstb", "cw"], [("dg", i)])
                st2 = {}

                def part2(cc=cc, i=i, xbuf=xbuf, st2=st2):
                    b2 = nb()
                    for k4 in range(4):
                        mm(PS[b2][:, 0:Tn], dg[i][:, k4, :], xbuf[:, k4:k4 + Tn], k4 == 0, k4 == 3, [("dg", i), ("xb", i)], ("ps", b2))
                    if cc < 20:
                        dst = sC[i][:, 0:Tn] if cc < 16 else BT[:, cc - 16, 0:Tn]
                        dk = ("sC", i) if cc < 16 else ("BT", cc - 16)
                        act(dst, PS[b2][:, 0:Tn], AF.Silu, [("ps", b2), "cb"], [dk], bias=cb[:, cc:cc + 1])
                        st2["dst"], st2["dk"] = dst, dk
                    else:
                        act(CT[:, cc - 20, 0:Tn], PS[b2][:, 0:Tn], AF.Silu, [("ps", b2), "cb"], [("CT", cc - 20)], bias=cb[:, cc:cc + 1])

                def part3(cc=cc, st2=st2):
                    if cc >= 20:
                        return
                    dst, dk = st2["dst"], st2["dk"]
                    bt = nb()
                    pT = PS[bt][:].bitcast(BF16)
                    for j in range(NS):
                        tr(pT[:, j * 128:(j + 1) * 128], dst[:, j * 128:(j + 1) * 128], ident_b, [dk, "cstb"], ("ps", bt))
                    if cc < 16:
                        cp(ev_eng(), xs_v[:, :, cc * 128:(cc + 1) * 128], pT[:, 0:512].rearrange("p (j c) -> p j c", j=4),
                           [("ps", bt), "HBphase1"], [("xs", cc)])
                    else:
                        cp(ev_eng(), Btok[:, :, cc - 16, :], pT[:, 0:512].rearrange("p (j c) -> p j c", j=4), [("ps", bt)], ["Btok"])
                pq.append([part2, part3])
                if len(pq) >= 2:
                    pq[-2][0]()
                if len(pq) >= 3:
                    pq[-3][1]()
            proj_fm(w_in, 0, 8, C_XBC, 24, G1, Tn, xk, cons_xbc)
            pq[-1][0]()
            pq[-2][1]()
            pq[-1][1]()
            xsk = [("xs", c) for c in range(16)]
            ypre = {}
            for blk2 in range(2):
                ypre[("y", blk2)] = wload(w_ssm_out, 0, 16, blk2 * 256, 256)
                for bb in range(2):
                    ypre[("g", blk2 * 2 + bb)] = wload(w_in, 0, 8, C_G + 1024 + (blk2 * 2 + bb) * 128, 128)
            st4 = junk[:].bitcast(F32)
            eac4 = st4[:, 0:256].rearrange("p (j c) -> p j c", j=4)
            dsd4 = st4[:, 256:384].rearrange("p (j c) -> p j c", j=4)
            nbias4 = st4[:, 384:512].rearrange("p (j c) -> p j c", j=4)
            for j in range(NS):
                ba = 7
                mm(PS[ba][:, 0:32], tri_b, dtab[:, j, :], True, True, ["cstb", ("dtab", j)], ("ps", ba))
                mm(PS[ba][:, 32:64], ones_b, dtab[:, j, :], True, True, ["cstb", ("dtab", j)], ("ps", ba))
                cp("dve", acs[:], PS[ba][:, 0:64], [("ps", ba)], ["acs"])
                act(eac4[:, j, :], acs[:], AF.Exp, ["acs", "junk"], [("eac4", j)])
                tt("dve", dsd[:], acs[:, 32:64], acs[:, 0:32], ALU.subtract, ["acs"], ["dsd"])
                act(dsd[:], dsd[:], AF.Exp, ["dsd"], ["dsd"])
                tt("dve", dsd4[:, j, :], dsd[:], dtt[:, j, :], ALU.mult, ["dsd", ("dt", j), "junk"], [("dsd4", j)])
                tt("dve", nbias4[:, j, :], lndt[:, j, :], acs[:, 0:32], ALU.subtract, [("lndt", j), "acs", "junk"], [("nbias4", j)])
                cp("dve", nbh4[:, j, :], nbias4[:, j, :], [("nbias4", j)], [("nbh", j)])
                tt("dve", nbl4[:, j, :], nbias4[:, j, :], nbh4[:, j, :], ALU.subtract, [("nbias4", j), ("nbh", j)], [("nbh", j)])
            BY = [0, 1]
            BO = [2, 3]
            BL = [4, 5]
            for j in range(NS):
                zk = [("zs", j, c) for c in range(4)]
                bc_ = 7
                for g4 in range(4):
                    mm(PS[bc_][:, g4 * 128:(g4 + 1) * 128], BT[:, g4, j * 128:(j + 1) * 128], CT[:, g4, j * 128:(j + 1) * 128], True, True,
                       [("BT", g4), ("CT", g4)], ("ps", bc_))
                cp("act", CBs[:].rearrange("p g i -> p (g i)"), PS[bc_][:, :], [("ps", bc_)], ["CBs"])

                def stageA(t, j=j):
                    g4, half = t // 2, t % 2
                    by, bo = BY[g4 % 2], BO[g4 % 2]
                    if half == 0:
                        mm(PS[bo][:, :], CT[:, g4, j * 128:(j + 1) * 128], Sbf[:, g4 * 512:(g4 + 1) * 512], True, True,
                           [("CT", g4), ("Sbf", g4)], ("ps", bo))
                    li = t % 3
                    bl = BL[t % 2]
                    h0 = g4 * 8 + half * 4
                    mm(PS[bl][:, :], ident_b, fap(mneg_b, [[0, 4], [1, 128]]), True, False, ["cstb"], ("ps", bl))
                    mm(PS[bl][:, :], ident_b, fap(nbh4[:, j, h0:h0 + 1], [[1, 4], [0, 128]]), False, False, ["cstb", ("nbh", j)], ("ps", bl))
                    mm(PS[bl][:, :], ident_b, fap(nbl4[:, j, h0:h0 + 1], [[1, 4], [0, 128]]), False, False, ["cstb", ("nbh", j)], ("ps", bl))
                    for hh in range(4):
                        h = h0 + hh
                        mm(PS[bl][:, hh * 128:(hh + 1) * 128], fap(dtab[:, j, h:h + 1], [[0, 128]]), tri_b, False, hh == 3,
                           [("dtab", j), "cstb"], ("ps", bl))
                    act(Lt[li][:].rearrange("p a b -> p (a b)"), PS[bl][:, :], AF.Exp, [("ps", bl)], [("Lt", li)])
                    tt("dve", MT[li][:], Lt[li][:], fap(CBs[:, g4, 0:1], [[0, 4], [1, 128]]), ALU.mult, [("Lt", li), "CBs"], [("MT", li)])

                def S1(t, j=j):
                    g4, half = t // 2, t % 2
                    by = BY[g4 % 2]
                    li = t % 3
                    for hh in range(4):
                        h = g4 * 8 + half * 4 + hh
                        hl = half * 4 + hh
                        mm(PS[by][:, hl * 64:(hl + 1) * 64], MT[li][:, hh, :], xs_v[:, j, h * 64:(h + 1) * 64], True, True,
                           [("MT", li)] + xsk, ("ps", by))

                def S2(g4, j=j, zk=zk):
                    by, bo = BY[g4 % 2], BO[g4 % 2]
                    i = g4 % 2
                    tt("dve", sA[i][:, :].rearrange("p (h q) -> p h q", h=8), PS[bo][:, :].rearrange("p (h q) -> p h q", h=8),
                       fap(eac4[:, j, g4 * 8:g4 * 8 + 1], [[1, 8], [0, 64]]), ALU.mult, [("ps", bo), ("eac4", j)], [("sA", i)])
                    tt("dve", sA[i][:, :], sA[i][:, :], PS[by][:, :], ALU.add, [("sA", i), ("ps", by)], [("sA", i)])
                    tt("pool", sB[i][:, :].rearrange("p (h q) -> p h q", h=8), xs_v[:, j, g4 * 512:(g4 + 1) * 512].rearrange("p (h q) -> p h q", h=8),
                       fap(hp[:, 2, g4 * 8:g4 * 8 + 1], [[1, 8], [0, 64]]), ALU.mult, xsk + ["hp"], [("sB", i)])
                    tt("dve", sA[i][:, :], sA[i][:, :], sB[i][:, :], ALU.add, [("sA", i), ("sB", i)], [("sA", i)])
                    tt("dve", yb[i][:, :], sA[i][:, :], zs_v[:, j, g4 * 512:(g4 + 1) * 512], ALU.mult, [("sA", i)] + zk, [("yb", i)])
                    xg = xddg[i]
                    tt("pool", xg[:].rearrange("p (h q) -> p h q", h=8), xs_v[:, j, g4 * 512:(g4 + 1) * 512].rearrange("p (h q) -> p h q", h=8),
                       fap(dsd4[:, j, g4 * 8:g4 * 8 + 1], [[1, 8], [0, 64]]), ALU.mult, xsk + [("dsd4", j)], [("xddg", i)])
                    tt("pool", Sf[:, g4 * 512:(g4 + 1) * 512].rearrange("p (h q) -> p h q", h=8),
                       Sf[:, g4 * 512:(g4 + 1) * 512].rearrange("p (h q) -> p h q", h=8),
                       fap(eac4[:, j, 32 + g4 * 8:32 + g4 * 8 + 1], [[1, 8], [0, 64]]), ALU.mult, [("Sf", g4), ("eac4", j)], [("Sf", g4)])

                def S3(g4):
                    i = g4 % 2
                    act(sB[i][:, :], yb[i][:, :], AF.Square, [("yb", i)], [("sB", i), ("gss", i)], accum=stat[:, 8 + i:9 + i])
                    rstd_from_ssq(8 + i, 12 + i, 512, 128, ("gss", i))

                def S4(g4, j=j):
                    i = g4 % 2
                    act(ynb[i][:, :], yb[i][:, :], AF.Copy, [("yb", i), ("gss", i)], [("ynb", i)], scale=stat[:, 12 + i:13 + i])
                    mm(PS[7][:, :], Btok[:, j, g4, :], xddg[i][:], True, True, ["Btok", ("xddg", i)], ("ps", 7))
                    pT = PS[6][:].bitcast(BF16)
                    for q in range(4):
                        tr(pT[:, q * 128:(q + 1) * 128], ynb[i][:, q * 128:(q + 1) * 128], ident_b, [("ynb", i), "cstb"], ("ps", 6))

                def S5(g4, j=j):
                    pT = PS[6][:].bitcast(BF16)
                    tt("dve", Sf[:, g4 * 512:(g4 + 1) * 512], Sf[:, g4 * 512:(g4 + 1) * 512], PS[7][:, :], ALU.add,
                       [("Sf", g4), ("ps", 7)], [("Sf", g4)])
                    cp("act", Sbf[:, g4 * 512:(g4 + 1) * 512], Sf[:, g4 * 512:(g4 + 1) * 512], [("Sf", g4)], [("Sbf", g4)])
                    for q in range(4):
                        ch = g4 * 4 + q
                        dstb = G2 if ch < 8 else G3
                        dk = ("G2" if ch < 8 else "G3", ch % 8)
                        if q % 2 == 0:
                            ts("dve", dstb[:, ch % 8, j * 128:(j + 1) * 128], pT[:, q * 128:(q + 1) * 128], nw[:, ch:ch + 1], None,
                               ALU.mult, None, [("ps", 6), "nw"], [dk])
                        else:
                            act(dstb[:, ch % 8, j * 128:(j + 1) * 128], pT[:, q * 128:(q + 1) * 128], AF.Copy, [("ps", 6), "nw"], [dk],
                                scale=nw[:, ch:ch + 1])

                for u in range(14):
                    for g4 in range(4):
                        if u == 2 * g4 + 7:
                            S5(g4)
                    for g4 in range(4):
                        if u == 2 * g4 + 6:
                            S4(g4)
                    for g4 in range(4):
                        if u == 2 * g4 + 5:
                            S3(g4)
                    for g4 in range(4):
                        if u == 2 * g4 + 4:
                            S2(g4)
                    if 2 <= u < 10:
                        S1(u - 2)
                    if u < 8:
                        stageA(u)
            if last:
                for q in range(16):
                    b = nb()
                    tr(PS[b][:, 0:128], Sf[:, q * 128:(q + 1) * 128], ident_f, [("Sf", q // 4), "cstf"], ("ps", b))
                    so = ostg[q % 2]
                    cp(ev_eng(), so[:, 0:128], PS[b][:, 0:128], [("ps", b)], [("ostg", q % 2)])
                    dma_out(pssm[q * 128:(q + 1) * 128, :], so[:, 0:128], [("ostg", q % 2)], slot=("ostg", q % 2))
            ybank = {}

            def ynT(k):
                return (G2 if k < 8 else G3)[:, k % 8, 0:Tn]
            S.add("dve", lambda g: g.memset(stat[:, 20:21], 0.0), writes=["HBphase2", "HBphase1"] + xsk + [("zs", j, c) for j in range(4) for c in range(4)])
            for blk2 in range(4):
                slab, wk = ypre.pop(("y", blk2)) if ("y", blk2) in ypre else wload(w_ssm_out, 0, 16, blk2 * 256, 256)
                for bb in range(2):
                    blk = blk2 * 2 + bb
                    b = nb()
                    for k in range(16):
                        mm(PS[b][:, 0:Tn], slab[:, k, bb * 128:(bb + 1) * 128], ynT(k), k == 0, k == 15,
                           [wk, ("G2", k % 8) if k < 8 else ("G3", k % 8)], ("ps", b))
                    slg, wkg = ypre.pop(("g", blk)) if ("g", blk) in ypre else wload(w_in, 0, 8, C_G + 1024 + blk * 128, 128)
                    bg = nb()
                    for k in range(8):
                        mm(PS[bg][:, 0:Tn], slg[:, k, 0:128], G1[:, k, 0:Tn], k == 0, k == 7, [wkg] + xk, ("ps", bg))
                    i = blk % 2
                    act(sA[i][:, 0:Tn], PS[bg][:, 0:Tn], AF.Sigmoid, [("ps", bg)], [("sA", i)])
                    tt("dve", sB[i][:, 0:Tn], sA[i][:, 0:Tn], PS[b][:, 0:Tn], ALU.mult, [("sA", i), ("ps", b)], [("sB", i)])
                    tt("dve", mT_v[:, blk, 0:Tn], MG[:, blk, 0:Tn], sB[i][:, 0:Tn], ALU.add,
                       [("sB", i), ("MG", blk), "HBphase2"], [("mT", blk)])

        if 'A' in branches:
            branchA()
        if 'C' in branches:
            branchC()
        if 'B' in branches:
            branchB()
        if branches != 'ACB':
            return
        mtk = [("mT", c) for c in range(8)]
        mslabs = [wload(w_merge_o, 0, 8, cbk * 512, 512) for cbk in range(2)]
        xn2 = [(xnb[0], ("xnb", 0)), (ostg[1][:].bitcast(BF16)[:, 0:D], ("ostg", 1))]
        pendn = [None]
        for j in range(NS):
            for cbk in range(2):
                slab, wk = mslabs[cbk]
                b = nb()
                for k in range(8):
                    mm(PS[b][:, :], mT_v[:, k, j * 128:(j + 1) * 128], slab[:, k, :], k == 0, k == 7, [wk] + mtk, ("ps", b))
                tt("dve", X[:, j, cbk * 512:(cbk + 1) * 512], X[:, j, cbk * 512:(cbk + 1) * 512], PS[b][:, :], ALU.add,
                   [("X", j), ("ps", b)], [("X", j)])
            act(junk[:, :], X[:, j, :], AF.Square, [("X", j)], ["junk", ("nst", j)], accum=stat[:, 48 + j:49 + j])
            ts("dve", stat[:, 52 + j:53 + j], stat[:, 48 + j:49 + j], 1.0 / D, EPS, ALU.mult, ALU.add, [("nst", j)], [("nst", j)])
            act(stat[:, 52 + j:53 + j], stat[:, 52 + j:53 + j], AF.Ln, [("nst", j)], [("nst", j)])
            act(stat[:, 52 + j:53 + j], stat[:, 52 + j:53 + j], AF.Exp, [("nst", j)], [("nst", j)], scale=-0.5)
            xn, xkey = xn2[j % 2]
            stt(xn[:, :], X[:, j, :], stat[:, 52 + j:53 + j], nrm[:, :], ALU.mult, ALU.mult, [("X", j), ("nst", j), "nrm"], [xkey])

            def trp(j=j, xn=xn, xkey=xkey):
                b = nb()
                pT = PS[b][:].bitcast(BF16)
                for k in range(8):
                    tr(pT[:, k * 128:(k + 1) * 128], xn[:, k * 128:(k + 1) * 128], ident_b, [xkey, "cstb"], ("ps", b))
                cp(ev_eng(), G1[:, :, j * 128:(j + 1) * 128], pT[:, 0:1024].rearrange("p (k t) -> p k t", k=8), [("ps", b)], [("G1", "all")])
            if pendn[0]:
                pendn[0]()
            pendn[0] = trp
        pendn[0]()
        dma_in(nrm[:], nrm_bc[:, 2, :], "nrm")
        for fb in range(22):
            slg, wkg = wload(w_gate, 0, 8, fb * 128, 128)
            slu, wku = wload(w_up, 0, 8, fb * 128, 128)
            b1 = nb()
            for k in range(8):
                mm(PS[b1][:, 0:Tn], slg[:, k, :], G1[:, k, 0:Tn], k == 0, k == 7, [wkg] + xk, ("ps", b1))
            i = fb % 2
            act(sA[i][:, 0:Tn], PS[b1][:, 0:Tn], AF.Silu, [("ps", b1)], [("sA", i)])
            b2 = nb()
            for k in range(8):
                mm(PS[b2][:, 0:Tn], slu[:, k, :], G1[:, k, 0:Tn], k == 0, k == 7, [wku] + xk, ("ps", b2))
            tt("dve", hT_v[:, fb, 0:Tn], sA[i][:, 0:Tn], PS[b2][:, 0:Tn], ALU.mult, [("sA", i), ("ps", b2), "HBphase2"], [("hT", fb)])
        htk = [("hT", c) for c in range(22)]
        for cbk in range(2):
            banks = [nb() for _ in range(NS)]
            for half in range(2):
                slab, wk = wload(w_down, half * 11 * 128, 11, cbk * 512, 512)
                for j in range(NS):
                    for k in range(11):
                        kk = half * 11 + k
                        mm(PS[banks[j]][:, :], hT_v[:, kk, j * 128:(j + 1) * 128], slab[:, k, :], kk == 0, kk == 21,
                           [wk] + htk, ("ps", banks[j]))
            for j in range(NS):
                tt("dve", X[:, j, cbk * 512:(cbk + 1) * 512], X[:, j, cbk * 512:(cbk + 1) * 512], PS[banks[j]][:, :], ALU.add,
                   [("X", j), ("ps", banks[j])], [("X", j)])
        S.add("dve", lambda g: g.memset(stat[:, 21:22], 0.0), writes=["HBphase1", "HBphase2"] + htk + mtk)
        for j in range(NS):
            act(junk[:, :], X[:, j, :], AF.Square, [("X", j)], ["junk", ("nst", j)], accum=stat[:, 48 + j:49 + j])
        for j in range(NS):
            ts("dve", stat[:, 52 + j:53 + j], stat[:, 48 + j:49 + j], 1.0 / D, EPS, ALU.mult, ALU.add, [("nst", j)], [("nst", j)])
        for j in range(NS):
            act(stat[:, 52 + j:53 + j], stat[:, 52 + j:53 + j], AF.Ln, [("nst", j)], [("nst", j)])
        for j in range(NS):
            act(stat[:, 52 + j:53 + j], stat[:, 52 + j:53 + j], AF.Exp, [("nst", j)], [("nst", j)], scale=-0.5)
        for j in range(NS):
            so = ostg[j % 2]
            stt(so[:, :], X[:, j, :], stat[:, 52 + j:53 + j], nrm[:, :], ALU.mult, ALU.mult, [("X", j), ("nst", j), "nrm"], [("ostg", j % 2)])
            dma_out(y_prompt[t0 + j * 128:t0 + (j + 1) * 128, :], so[:, :], [("ostg", j % 2)], slot=("ostg", j % 2))
            if ti + 1 < ntiles:
                dma_in(X[:, j, :], xin[t0 + T + j * 128:t0 + T + (j + 1) * 128, :], ("X", j))
        if ti + 1 < ntiles:
            xpref["done"] = True
            dma_in(nrm[:], nrm_bc[:, 0, :], "nrm")

    if stop in (3, 4, 5):
        return finish()
    for ti in range(ntiles):
        tile(ti, ti == NT - 1)
    dma_out(pconv[:, :, :], pcv[:], ["pcv"])
    dma_out(pssmc[:, :, :], pscv[:], ["pscv"])
    def sample():
        P = NSB
        Tn = NSB
        allkeys = list(set(S.lastw.keys()) | set(S.readers.keys()))
        S.add("dve", lambda g: g.memset(stat[:, 30:31], 0.0), writes=allkeys + ["SPH"])
        S.auto_reads = ["SPH"]
        xk = [("G1", "all")]
        Dc = dsd[:, 0:16]
        stat_g = eac[:, 0:64]
        dma_in(Dc, Dc_in[:, :], "Dc")
        Xf = X[:].rearrange("p j c -> p (j c)")
        Xs3 = X
        xtok2 = Xf[0:P, 1024:3072]
        bctok = Xf[0:P, 3072:4096]
        MGf = MG[:].rearrange("p k t -> p (k t)")
        t2 = MGf[:, 0:2048]
        prod = MGf[:, 2048:4096]
        HBf = HB[:].bitcast(F32)
        Kb = HBf[:, 0:2048]
        Vs = HBf[:, 2048:4096]
        St = HBf[:, 4096:6144]
        t1 = HBf[:, 6144:8192]
        G3f = G3[:].rearrange("p k t -> p (k t)").bitcast(F32)
        G2b = G2[:].rearrange("p k t -> p (k t)")
        G2f = G2b.bitcast(F32)

        def v3(ap, n):
            return ap.rearrange("p (a b) -> p a b", b=n)
        hsA = v3(G3f[:, 0:256], 16)
        hsB = v3(G3f[:, 256:1408], 16)
        uT = v3(G3f[:, 1408:1536], 16)
        xbcT = v3(G3f[:, 1536:1920], 16)
        MGs = v3(G3f[:, 1920:2048], 16)
        zT = v3(G2f[:, 0:256], 16)
        dtT = v3(G2f[:, 256:512], 16)
        decT = v3(G2f[:, 512:768], 16)
        Ys = v3(G2f[:, 768:1024], 16)
        yg = v3(G2f[:, 1024:1280], 16)
        sq = v3(G2f[:, 1280:1536], 16)
        vTs = v3(G2b[:, 3072:3200], 16)
        qTs = v3(G2b[:, 3200:3328], 16)
        oTs = v3(G2b[:, 3328:3456], 16)
        ynTs = v3(G2b[:, 3456:3712], 16)
        mTs = v3(G2b[:, 3712:3840], 16)
        hTs = v3(sC[0][:, 0:352], 16)
        xraw = v3(sA[1][:, 0:384], 16)
        Vaug = Sf[:].bitcast(BF16)[:, 0:2056].rearrange("p (m h d) -> p m h d", m=2, h=4)
        q_tok = xdd[0:P, 0:1024]
        o_tokb = xdd[0:P, 1024:2048]
        o_raw = Xf[0:P, 1024:2048]
        tA = sA[0][:, 0:Tn]
        tB = sB[0][:, 0:Tn]
        tC = sB[1][:, 0:Tn]
        dts = dtt[0:P, 0, :]
        dtas = dtt[0:P, 1, :]
        decs = dtt[0:P, 2, :]
        ssc = acs[:, 0:8]
        sp8 = sm[:, 0:8]
        p8 = CBs[:, 0, 0:8]
        den1 = stat[0:1, 40:44]
        dent = stat[0:P, 44:48]
        idf16 = ident_f[0:P, 0:P]
        idb16 = ident_b[0:P, 0:P]

        dma_in(X[0:P, 0, :], xs_in[:, :], ("X", 0))
        norm_transpose(X, P, 1, 0, G1, "X", Tn)

        def load_T(src_ap, dst3, row0, tag):
            i = cnt["rr"] % 2
            cnt["rr"] += 1
            so = ostg[i]
            S.add("sp", lambda g: g.dma_start(out=so[0:P, :], in_=src_ap), writes=[("ostg", i)], slot=("ostg", i))
            b = nb()
            for c in range(8):
                tr(PS[b][:, c * 16:(c + 1) * 16], so[0:P, c * 128:(c + 1) * 128], idf16, [("ostg", i), "cstf"], ("ps", b))
            cp(ev_eng(), dst3[:, row0:row0 + 8, :], PS[b][:, 0:128].rearrange("p (c t) -> p c t", c=8), [("ps", b)], [tag])
        for r in range(2):
            load_T(st_conv[:, r, :], hsA, r * 8, "hsA")
        for r in range(3):
            for q3 in range(3):
                load_T(st_sconv[:, r, q3 * 1024:(q3 + 1) * 1024], hsB, r * 24 + q3 * 8, "hsB")
        S.add("sp", lambda g: g.dma_start(out=sconv[:, 0, :], in_=st_conv[:, 1, :]), slot=mslot())
        S.add("sp", lambda g: g.dma_start(out=sssmc[:, 0:2, :], in_=st_sconv[:, 1:3, :]), slot=mslot())

        def store_T(src3, row0, dst_ap, skeys):
            i = cnt["rr"] % 2
            cnt["rr"] += 1
            so = ostg[i]
            for half in range(2):
                b = nb()
                for c in range(4):
                    tr(PS[b][0:P, c * 128:(c + 1) * 128], src3[:, row0 + half * 4 + c, :], ident_f, skeys + ["cstf"], ("ps", b))
                cp(ev_eng(), so[0:P, half * 512:(half + 1) * 512], PS[b][0:P, :], [("ps", b)], [("ostg", i)])
            S.add("sp", lambda g: g.dma_start(out=dst_ap, in_=so[0:P, :]), reads=[("ostg", i)], slot=("ostg", i))

        def s_out_proj(Wo, kch, srcT, src_keys, gate_c0, mode):
            bps = 4 if kch == 8 else 2
            for blk0 in range(0, 8, bps):
                slab, wk = wload(Wo, 0, kch, blk0 * 128, bps * 128)
                if blk0 % 4 == 0:
                    slg, wkg = wload(w_in, 0, 8, gate_c0 + blk0 * 128, 512)
                for bb in range(bps):
                    blk = blk0 + bb
                    b = nb()
                    for k in range(kch):
                        mm(PS[b][:, 0:Tn], slab[:, k, bb * 128:(bb + 1) * 128], srcT[:, k, :], k == 0, k == kch - 1, [wk] + src_keys, ("ps", b))
                    bg = nb()
                    go = (blk % 4) * 128
                    for k in range(8):
                        mm(PS[bg][:, 0:Tn], slg[:, k, go:go + 128], G1[:, k, 0:Tn], k == 0, k == 7, [wkg] + xk, ("ps", bg))
                    act(tA, PS[bg][:, 0:Tn], AF.Sigmoid, [("ps", bg)], ["tA"])
                    if mode == "first":
                        tt("dve", MGs[:, blk, :], tA, PS[b][:, 0:Tn], ALU.mult, ["tA", ("ps", b)], ["MGs"])
                    else:
                        tt("dve", tC, tA, PS[b][:, 0:Tn], ALU.mult, ["tA", ("ps", b)], ["tC"])
                        if mode == "mid":
                            tt("dve", MGs[:, blk, :], MGs[:, blk, :], tC, ALU.add, ["tC", "MGs"], ["MGs"])
                        else:
                            tt("dve", mTs[:, blk, :], MGs[:, blk, :], tC, ALU.add, ["tC", "MGs"], ["mTs"])

        for c in range(8):
            if c % 4 == 0:
                slc4, wkc = wload(w_in, 0, 8, C_SCC + c * 128, 512)
                slx4, wkx = wload(w_in, 0, 8, C_SCX + c * 128, 512)
                slb4, wkb = wload(w_in, 0, 8, C_SCB + c * 128, 512)
            co = (c % 4) * 128
            b1 = nb()
            for k in range(8):
                mm(PS[b1][:, 0:Tn], slc4[:, k, co:co + 128], G1[:, k, 0:Tn], k == 0, k == 7, [wkc] + xk, ("ps", b1))
            act(tA, PS[b1][:, 0:Tn], AF.Copy, [("ps", b1)], ["tA"])
            b2 = nb()
            for k in range(8):
                mm(PS[b2][:, 0:Tn], slx4[:, k, co:co + 128], G1[:, k, 0:Tn], k == 0, k == 7, [wkx] + xk, ("ps", b2))
            tt("dve", uT[:, c, :], tA, PS[b2][:, 0:Tn], ALU.mult, ["tA", ("ps", b2)], ["uT"])
            ts("dve", tB, hsA[:, c, :], scw[:, c, 0:1], None, ALU.mult, None, ["hsA", "scw"], ["tB"])
            stt(tB, hsA[:, 8 + c, :], scw[:, c, 1:2], tB, ALU.mult, ALU.add, ["hsA", "scw", "tB"], ["tB"])
            stt(tB, uT[:, c, :], scw[:, c, 2:3], tB, ALU.mult, ALU.add, ["uT", "scw", "tB"], ["tB"])
            b4 = nb()
            for k in range(8):
                mm(PS[b4][:, 0:Tn], slb4[:, k, co:co + 128], G1[:, k, 0:Tn], k == 0, k == 7, [wkb] + xk, ("ps", b4))
            tt("dve", vTs[:, c, :], tB, PS[b4][:, 0:Tn], ALU.mult, ["tB", ("ps", b4)], ["vTs"])
        store_T(uT, 0, sconv[:, 1, :], ["uT"])
        s_out_proj(w_sc_out, 8, vTs, ["vTs"], C_G, "first")

        def cons_q(blk, b):
            act(qTs[:, blk, :], PS[b][:, 0:Tn], AF.Copy, [("ps", b)], ["qTs"], scale=1.0 / 16.0)
        proj_fm(w_in, 0, 8, C_Q, 8, G1, Tn, xk, cons_q)
        bq = nb()
        pTq = PS[bq][:].bitcast(BF16)
        for blk in range(8):
            tr(pTq[0:P, blk * 128:(blk + 1) * 128], qTs[:, blk, :], ident_b, ["qTs", "cstb"], ("ps", bq))
        cp("act", q_tok, pTq[0:P, 0:1024], [("ps", bq)], ["q_tok"])
        memset("dve", Vaug[:, :, :, 256:257], 1.0, ["Vaug1"])
        HQ = [HBf[:, 0:2048], HBf[:, 2048:4096], HBf[:, 4096:6144], HBf[:, 6144:8192]]
        HK = ["hbA", "hbB", "hbC", "hbD"]
        den1s = [stat[0:1, 40:44], stat[0:1, 60:64]]

        def kv_load(bi):
            i = bi % 2
            S.add("sp", lambda g: g.dma_start(out=HQ[2 * i].rearrange("p (m c) -> p m c", m=2),
                                              in_=ck_in[bi].rearrange("(m p) c -> p m c", p=128)), writes=[HK[2 * i]], slot=("kb", i))
            S.add("sp", lambda g: g.dma_start(out=HQ[2 * i + 1].rearrange("p (m c) -> p m c", m=2),
                                              in_=cv_in[bi].rearrange("(m p) c -> p m c", p=128)), writes=[HK[2 * i + 1]], slot=("vs", i))
        kv_load(0)
        for bi in range(P):
            if bi + 1 < P:
                kv_load(bi + 1)
            i = bi % 2
            Kb_, Vs_, kk, vk = HQ[2 * i], HQ[2 * i + 1], HK[2 * i], HK[2 * i + 1]
            oh = fap(ident_b[0:P, bi:bi + 1], [[0, 128]])
            bq0 = nb()
            mm(PS[bq0][:, :], oh, q_tok[:, 0:512], True, True, ["cstb", "q_tok"], ("ps", bq0))
            bq1 = nb()
            mm(PS[bq1][:, :], oh, q_tok[:, 512:1024], True, True, ["cstb", "q_tok"], ("ps", bq1))
            Kb3 = Kb_.rearrange("p (m c) -> p m c", m=2)
            pr3 = prod.rearrange("p (m c) -> p m c", m=2)
            tt("dve", pr3[:, :, 0:512], Kb3[:, :, 0:512], fap(PS[bq0][:, 0:1], [[0, 2], [1, 512]]), ALU.mult, [kk, ("ps", bq0)], ["prod"])
            tt("dve", pr3[:, :, 512:1024], Kb3[:, :, 512:1024], fap(PS[bq1][:, 0:1], [[0, 2], [1, 512]]), ALU.mult, [kk, ("ps", bq1)], ["prod"])
            S.add("dve", lambda g: g.tensor_reduce(out=ssc, in_=prod.rearrange("p (a d) -> p a d", d=256), op=ALU.add, axis=AX),
                  reads=["prod"], writes=["ssc"])
            ts("dve", ssc, ssc, 80.0, None, ALU.min, None, ["ssc"], ["ssc"])
            act(p8, ssc, AF.Exp, ["ssc"], ["p8"])
            act(Vaug[:, :, :, 0:256], Vs_.rearrange("p (m h d) -> p m h d", m=2, h=4), AF.Copy, [vk], ["Vaug"])
            bos = [nb(), nb(), nb(), nb()]
            for h in range(4):
                bo = bos[h]
                for mc in range(2):
                    mm(PS[bo][0:1, 0:257], p8[:, mc * 4 + h:mc * 4 + h + 1], Vaug[:, mc, h, :], mc == 0, mc == 1,
                       ["p8", "Vaug", "Vaug1"], ("ps", bo))
            orow = ostg[i][0:1, :]
            den1 = den1s[i]
            for h in range(4):
                bo = bos[h]
                cp("act", orow[:, h * 256:(h + 1) * 256], PS[bo][0:1, 0:256], [("ps", bo)], [("ostg", i)])
                cp("act", den1[:, h:h + 1], PS[bo][0:1, 256:257], [("ps", bo)], [("den1", i)])
            S.add("sp", lambda g, bi=bi, orow=orow: g.dma_start(out=o_raw[bi:bi + 1, :], in_=orow), reads=[("ostg", i)], writes=["o_raw"], slot=("ostg", i))
            S.add("sp", lambda g, bi=bi, den1=den1: g.dma_start(out=dent[bi:bi + 1, :], in_=den1), reads=[("den1", i)], writes=["dent"], slot=("den", i))
        S.add("dve", lambda g: g.reciprocal(dent, dent), reads=["dent"], writes=["dent"])
        tt("dve", o_tokb.rearrange("p (h d) -> p h d", h=4), o_raw.rearrange("p (h d) -> p h d", h=4),
           fap(dent[:, 0:1], [[1, 4], [0, 256]]), ALU.mult, ["o_raw", "dent"], ["o_tokb"])
        bo_ = nb()
        pTo = PS[bo_][:].bitcast(BF16)
        for blk in range(8):
            tr(pTo[:, blk * 16:(blk + 1) * 16], o_tokb[:, blk * 128:(blk + 1) * 128], idb16, ["o_tokb", "cstb"], ("ps", bo_))
        cp("act", oTs, pTo[:, 0:128].rearrange("p (k t) -> p k t", k=8), [("ps", bo_)], ["oTs"])
        s_out_proj(w_attn_o, 8, oTs, ["oTs"], C_G + 2048, "mid")

        def cons_z(blk, b):
            act(zT[:, blk, :], PS[b][:, 0:Tn], AF.Silu, [("ps", b)], ["zT"])
        proj_fm(w_in, 0, 8, C_Z, 16, G1, Tn, xk, cons_z)
        slab, wk = wload(w_in, 0, 8, C_DT, 32)
        b = nb()
        for k in range(8):
            mm(PS[b][0:P, 0:32], G1[:, k, 0:Tn], slab[:, k, :], k == 0, k == 7, [wk] + xk, ("ps", b))
        tt("dve", dts, PS[b][0:P, 0:32], hp[0:P, 0, :], ALU.add, [("ps", b), "hp"], ["dts"])
        act(dts, dts, AF.Exp, ["dts"], ["dts"])
        act(dts, dts, AF.Ln, ["dts"], ["dts"], bias=1.0)
        tt("dve", dtas, dts, abc[0:P, :], ALU.mult, ["dts", "abc"], ["dtas"])
        act(decs, dtas, AF.Exp, ["dtas"], ["decs"])

        def cons_xbc(cc, b):
            act(xraw[:, cc, :], PS[b][:, 0:Tn], AF.Copy, [("ps", b)], ["xraw"])
            ts("dve", tB, hsB[:, cc, :], cw[:, cc, 0:1], None, ALU.mult, None, ["hsB", "cw"], ["tB"])
            stt(tB, hsB[:, 24 + cc, :], cw[:, cc, 1:2], tB, ALU.mult, ALU.add, ["hsB", "cw", "tB"], ["tB"])
            stt(tB, hsB[:, 48 + cc, :], cw[:, cc, 2:3], tB, ALU.mult, ALU.add, ["hsB", "cw", "tB"], ["tB"])
            stt(tB, xraw[:, cc, :], cw[:, cc, 3:4], tB, ALU.mult, ALU.add, ["xraw", "cw", "tB"], ["tB"])
            act(xbcT[:, cc, :], tB, AF.Silu, ["tB", "cb"], ["xbcT"], bias=cb[:, cc:cc + 1])
        proj_fm(w_in, 0, 8, C_XBC, 24, G1, Tn, xk, cons_xbc)
        for q3 in range(3):
            store_T(xraw, q3 * 8, sssmc[:, 2, q3 * 1024:(q3 + 1) * 1024], ["xraw"])
        for half in range(2):
            b = nb()
            for c in range(4):
                tr(PS[b][0:P, c * 128:(c + 1) * 128], xbcT[:, 16 + half * 4 + c, :], ident_f, ["xbcT", "cstf"], ("ps", b))
            cp(ev_eng(), bctok[:, half * 512:(half + 1) * 512], PS[b][0:P, :], [("ps", b)], ["bctok"])

        def bcast_T(src16, dstT, key):
            cp("dve", xtok2.rearrange("p (h q) -> p h q", h=32), fap(src16[:, 0:1], [[1, 32], [0, 64]]), [key], ["xtok2"])
            b = nb()
            for hpi in range(16):
                tr(PS[b][:, hpi * 16:(hpi + 1) * 16], xtok2[:, hpi * 128:(hpi + 1) * 128], idf16, ["xtok2", "cstf"], ("ps", b))
            cp(ev_eng(), dstT, PS[b][:, 0:256].rearrange("p (a t) -> p a t", a=16), [("ps", b)], [key + "T"])
        bcast_T(dts, dtT, "dts")
        bcast_T(decs, decT, "decs")
        tt("dve", dtT, dtT, xbcT[:, 0:16, :], ALU.mult, ["dtsT", "xbcT"], ["dtsT"])
        def st_load(bi):
            i = bi % 2
            S.add("sp", lambda g: g.dma_start(out=HQ[2 * i].rearrange("p (a n) -> p a n", a=16),
                                              in_=st_ssm[bi].rearrange("(a q) n -> q a n", q=128)), writes=[HK[2 * i]], slot=("kb", i))
        st_load(0)
        for bi in range(P):
            if bi + 1 < P:
                st_load(bi + 1)
            i = bi % 2
            St_, t1_, sk_, tk_ = HQ[2 * i], HQ[2 * i + 1], HK[2 * i], HK[2 * i + 1]
            oh = fap(ident_f[0:P, bi:bi + 1], [[0, 128]])
            bB = nb()
            mm(PS[bB][:, :], oh, bctok[:, 0:512], True, True, ["cstf", "bctok"], ("ps", bB))
            bC = nb()
            mm(PS[bC][:, :], oh, bctok[:, 512:1024], True, True, ["cstf", "bctok"], ("ps", bC))
            tt("dve", t1_.rearrange("p (a n) -> p a n", a=16), St_.rearrange("p (a n) -> p a n", a=16),
               fap(decT[:, 0, bi:bi + 1], [[16, 16], [0, 128]]), ALU.mult, [sk_, "decsT"], [tk_])
            tt("dve", fap(t2[:, 0:1], [[512, 4], [128, 4], [1, 128]]), fap(PS[bB][:, 0:1], [[128, 4], [0, 4], [1, 128]]),
               fap(dtT[:, 0, bi:bi + 1], [[64, 4], [16, 4], [0, 128]]), ALU.mult, [("ps", bB), "dtsT"], ["t2"])
            tt("pool", t1_, t1_, t2, ALU.add, [tk_, "t2"], [tk_])
            S.add("pool", lambda g, bi=bi, t1_=t1_: g.dma_start(out=sssm[bi].rearrange("(a q) n -> q a n", q=128),
                                                             in_=t1_.rearrange("p (a n) -> p a n", a=16)), reads=[tk_], slot=("sto", i))
            tt("dve", fap(t2[:, 0:1], [[512, 4], [128, 4], [1, 128]]), fap(t1_[:, 0:1], [[512, 4], [128, 4], [1, 128]]),
               fap(PS[bC][:, 0:1], [[128, 4], [0, 4], [1, 128]]), ALU.mult, [tk_, ("ps", bC)], ["t2"])
            S.add("dve", lambda g, bi=bi: g.tensor_reduce(out=fap(Ys[:, 0, bi:bi + 1], [[16, 16]]), in_=t2.rearrange("p (a n) -> p a n", a=16),
                                                          op=ALU.add, axis=AX), reads=["t2"], writes=["Ys"])
        tt("dve", yg, xbcT[:, 0:16, :], fap(Dc[:, 0:1], [[1, 16], [0, 16]]), ALU.mult, ["xbcT", "Dc"], ["yg"])
        tt("dve", yg, yg, Ys, ALU.add, ["yg", "Ys"], ["yg"])
        tt("dve", yg, yg, zT, ALU.mult, ["yg", "zT"], ["yg"])
        tt("dve", sq, yg, yg, ALU.mult, ["yg"], ["sq"])
        bs_ = nb()
        mm(PS[bs_][:, 0:256], cstf[:, 3, :], sq.rearrange("p a t -> p (a t)"), True, True, ["cstf", "sq"], ("ps", bs_))
        gsv = stat_g
        S.add("dve", lambda g: g.tensor_reduce(out=gsv.rearrange("p (g t) -> p g t", g=4), in_=fap(PS[bs_][:, 0:1], [[64, 4], [1, 16], [16, 4]]),
                                               op=ALU.add, axis=AX), reads=[("ps", bs_)], writes=["gsv"])
        ts("dve", gsv, gsv, 1.0 / 512, EPS, ALU.mult, ALU.add, ["gsv"], ["gsv"])
        act(gsv, gsv, AF.Sqrt, ["gsv"], ["gsv"])
        S.add("dve", lambda g: g.reciprocal(gsv, gsv), reads=["gsv"], writes=["gsv"])
        tt("dve", fap(yg[:, 0, 0:1], [[64, 4], [16, 4], [1, 16]]), fap(yg[:, 0, 0:1], [[64, 4], [16, 4], [1, 16]]),
           fap(gsv[:, 0:1], [[16, 4], [0, 4], [1, 16]]), ALU.mult, ["yg", "gsv"], ["yg"])
        tt("dve", ynTs, yg, fap(nw[:, 0:1], [[1, 16], [0, 16]]), ALU.mult, ["yg", "nw"], ["ynTs"])
        s_out_proj(w_ssm_out, 16, ynTs, ["ynTs"], C_G + 1024, "last")
        for cbk in range(2):
            slab, wk = wload(w_merge_o, 0, 8, cbk * 512, 512)
            b = nb()
            for k in range(8):
                mm(PS[b][0:P, :], mTs[:, k, :], slab[:, k, :], k == 0, k == 7, [wk, "mTs"], ("ps", b))
            tt("dve", X[0:P, 0, cbk * 512:(cbk + 1) * 512], X[0:P, 0, cbk * 512:(cbk + 1) * 512], PS[b][0:P, :], ALU.add,
               [("X", 0), ("ps", b)], [("X", 0)])
        norm_transpose(X, P, 1, 1, G1, "X", Tn)
        for fb in range(22):
            if fb % 4 == 0:
                ncb = min(4, 22 - fb) * 128
                slg4, wkg = wload(w_gate, 0, 8, fb * 128, ncb)
                slu4, wku = wload(w_up, 0, 8, fb * 128, ncb)
            fo = (fb % 4) * 128
            b1 = nb()
            for k in range(8):
                mm(PS[b1][:, 0:Tn], slg4[:, k, fo:fo + 128], G1[:, k, 0:Tn], k == 0, k == 7, [wkg] + xk, ("ps", b1))
            act(tA, PS[b1][:, 0:Tn], AF.Silu, [("ps", b1)], ["tA"])
            b2 = nb()
            for k in range(8):
                mm(PS[b2][:, 0:Tn], slu4[:, k, fo:fo + 128], G1[:, k, 0:Tn], k == 0, k == 7, [wku] + xk, ("ps", b2))
            tt("dve", hTs[:, fb, :], tA, PS[b2][:, 0:Tn], ALU.mult, ["tA", ("ps", b2)], ["hTs"])
        for cbk in range(2):
            b = nb()
            for half in range(2):
                slab, wk = wload(w_down, half * 11 * 128, 11, cbk * 512, 512)
                for k in range(11):
                    kk = half * 11 + k
                    mm(PS[b][0:P, :], hTs[:, kk, :], slab[:, k, :], kk == 0, kk == 21, [wk, "hTs"], ("ps", b))
            tt("dve", X[0:P, 0, cbk * 512:(cbk + 1) * 512], X[0:P, 0, cbk * 512:(cbk + 1) * 512], PS[b][0:P, :], ALU.add,
               [("X", 0), ("ps", b)], [("X", 0)])
        dma_in(nrm[:], nrm_bc[:, 2, :], "nrm")
        act(junk[0:P, :], X[0:P, 0, :], AF.Square, [("X", 0)], ["junk", "stat0"], accum=stat[0:P, 0:1])
        rstd_from_ssq(0, 1, D, P, "stat0")
        stt(ostg[0][0:P, :], X[0:P, 0, :], stat[0:P, 1:2], nrm[0:P, :], ALU.mult, ALU.mult, [("X", 0), "stat0", "nrm"], [("ostg", 0)])
        dma_out(y_sample[:, :], ostg[0][0:P, :], [("ostg", 0)], slot=("ostg", 0))

    if with_sample:
        sample()
    S.emit(st)
    st.close()
    return nc


_CACHE = {}


def _consts():
    c = np.zeros((128, 5, 128), np.float32)
    c[:, 0, :] = np.eye(128, dtype=np.float32)
    j = np.arange(128)[:, None]
    i = np.arange(128)[None, :]
    c[:, 1, :] = (j <= i).astype(np.float32)
    c[:, 2, :] = np.where(i < j, -30000.0, 0.0).astype(np.float32)
    c[:, 3, :] = 1.0
    return c


def kernel(**inp):
    f = lambda a: np.ascontiguousarray(np.asarray(a, dtype=np.float32))
    if "nc" not in _CACHE:
        _CACHE["nc"] = build()
    nc = _CACHE["nc"]
    bc = lambda v: np.ascontiguousarray(np.broadcast_to(f(v).reshape(1, -1), (128, f(v).size)))
    nrm_bc = np.stack([bc(inp["norm_mix_w"][0]), bc(inp["norm_ffn_w"][0]), bc(inp["norm_final_w"]), bc(inp["norm_mem_w"][0])], 1)
    scw_c = np.ascontiguousarray(f(inp["sc_conv_w"][0]).reshape(3, 8, 128).transpose(2, 1, 0))
    cw_c = np.ascontiguousarray(f(inp["ssm_conv_w"][0]).reshape(4, 24, 128).transpose(2, 1, 0))
    cb_c = np.ascontiguousarray(f(inp["ssm_conv_b"][0]).reshape(24, 128).T)
    nw_c = np.ascontiguousarray(f(inp["ssm_norm_w"][0]).reshape(16, 128).T)
    hp_bc = np.stack([bc(inp["ssm_dt_bias"][0]), bc(inp["ssm_a_log"][0]), bc(inp["ssm_d"][0])], 1)
    shared = {
        "w_in": f(inp["w_in"][0]), "w_sc_out": f(inp["w_sc_out"][0]), "w_ssm_out": f(inp["w_ssm_out"][0]),
        "w_mem_k": f(inp["w_mem_k"][0]), "w_mem_v": f(inp["w_mem_v"][0]), "w_attn_o": f(inp["w_attn_o"][0]),
        "w_merge_o": f(inp["w_merge_o"][0]), "w_ffn_gate": f(inp["w_ffn_gate"][0]), "w_ffn_up": f(inp["w_ffn_up"][0]),
        "w_ffn_down": f(inp["w_ffn_down"][0]),
        "nrm_bc": np.ascontiguousarray(nrm_bc), "scw_c": scw_c, "cw_c": cw_c, "cb_c": cb_c, "nw_c": nw_c,
        "hp_bc": np.ascontiguousarray(hp_bc), "cst": _consts(),
        "Dc_in": np.ascontiguousarray(np.repeat(f(inp["ssm_d"][0]), 64).reshape(16, 128).T),
    }
    import os
    NCORE = int(os.environ.get('NCORE', '8'))
    in_maps = []
    for c in range(NCORE):
        m = dict(shared)
        m["x_prompt"] = f(inp["x_prompt"][c])
        m["mem_prompt"] = f(inp["mem_prompt"][c])
        sl = slice(c * NSB, (c + 1) * NSB)
        m["x_sample"] = f(inp["x_sample"][sl, 0])
        m["cache_k"] = f(inp["cache_mem_k"][0, sl]).reshape(NSB, 256, D)
        m["cache_v"] = f(inp["cache_mem_v"][0, sl]).reshape(NSB, 256, D)
        m["st_conv"] = f(inp["state_conv"][0, sl])
        m["st_sconv"] = f(inp["state_ssm_conv"][0, sl])
        m["st_ssm"] = f(inp["state_ssm"][0, sl]).reshape(NSB, 2048, 128)
        in_maps.append(m)
    res = run_bass_kernel_spmd(nc, in_maps, core_ids=list(range(NCORE)))
    R = list(res.results) + [res.results[0]] * (8 - NCORE)
    y_prompt = np.stack([R[c]["y_prompt"] for c in range(8)], 0)
    pmk = np.stack([R[c]["pmk"].reshape(256, 4, 256) for c in range(8)], 0)[None]
    pmv = np.stack([R[c]["pmv"].reshape(256, 4, 256) for c in range(8)], 0)[None]
    pconv = np.stack([R[c]["pconv"].transpose(2, 1, 0).reshape(2, 1024) for c in range(8)], 0)[None]
    pssmc = np.stack([R[c]["pssmc"].transpose(2, 1, 0).reshape(3, 3072) for c in range(8)], 0)[None]
    pssm = np.stack([R[c]["pssm"].reshape(32, 64, 128) for c in range(8)], 0)[None]
    y_sample = np.concatenate([R[c]["y_sample"] for c in range(8)], 0)[:, None, :]
    sconv = np.concatenate([R[c]["sconv"] for c in range(8)], 0)[None]
    sssmc = np.concatenate([R[c]["sssmc"] for c in range(8)], 0)[None]
    sssm = np.concatenate([R[c]["sssm"].reshape(NSB, 32, 64, 128) for c in range(8)], 0)[None]
    return (y_prompt, y_sample, pmk, pmv, pconv, pssmc, pssm, sconv, sssmc, sssm)
```
